# Optimizing a Trainium2 kernel written in Bass

```python
import math
import jax
import jax.numpy as jnp
from jax import lax
import numpy as np


D_MODEL = 1024
BATCH = 4
SEQ = 8192
DEPTH = 2

DEEPNORM_ALPHA = (2 * DEPTH) ** 0.25
DEEPNORM_BETA = (8 * DEPTH) ** -0.25
LN_EPS = 1e-5
RMS_EPS = 1e-6
NUM_BUCKETS = 32
REL_MAX_DIST = 2048
REL_HEADS = 4
GDN_HEADS = 6
GDN_DK = 128
GDN_DV = 128
GDN_CONV = 4
GDN_CHUNK = 64
GDN_QK = GDN_HEADS * GDN_DK
GDN_V = GDN_HEADS * GDN_DV
SWA_CONFIGS = ((128, 1), (512, 4), (2048, 16))
SWA_GROUPS = len(SWA_CONFIGS)
SWA_HEADS = 4
SWA_HEAD_DIM = 64
SWA_BLOCK = 128
SWA_QKV = SWA_GROUPS * SWA_HEADS * SWA_HEAD_DIM
SWA_OUT = SWA_HEADS * SWA_HEAD_DIM
DIFF_HEADS = 4
DIFF_HEAD_DIM = 64
DIFF_QK = DIFF_HEADS * 2 * DIFF_HEAD_DIM
DIFF_V = DIFF_HEADS * 2 * DIFF_HEAD_DIM
Q_BLOCK = 128
GLA_HEADS = 4
GLA_DK = 64
GLA_DV = 128
GLA_GATE_RANK = 16
GLA_GATE_TAU = 16.0
GLA_CHUNK = 64
GLA_QK = GLA_HEADS * GLA_DK
GLA_V = GLA_HEADS * GLA_DV
D_FF = 2816
FFN_CONV = 3

EVEN_SIZES = (GDN_QK, GDN_QK, GDN_V, GDN_V, GDN_HEADS, GDN_HEADS, SWA_QKV, SWA_QKV, SWA_QKV)
EVEN_IN = sum(EVEN_SIZES)
EVEN_SPLITS = tuple(int(v) for v in np.cumsum(EVEN_SIZES)[:-1])
EVEN_OUT = GDN_V + SWA_OUT
ODD_SIZES = (DIFF_QK, DIFF_QK, DIFF_V, GLA_QK, GLA_QK, GLA_V, GLA_V, GLA_GATE_RANK)
ODD_IN = sum(ODD_SIZES)
ODD_SPLITS = tuple(int(v) for v in np.cumsum(ODD_SIZES)[:-1])
ODD_OUT = DIFF_V + GLA_V
N_EVEN = (DEPTH + 1) // 2
N_ODD = DEPTH // 2

kernel_name = 'hybrid_gdn_dilated_diff_gla_trunk'


def layer_norm(x, g, b):
    xf = x.astype(jnp.float32)
    mu = jnp.mean(xf, -1, keepdims=True)
    var = jnp.mean(jnp.square(xf - mu), -1, keepdims=True)
    y = (xf - mu) * lax.rsqrt(var + LN_EPS) * g.astype(jnp.float32) + b.astype(jnp.float32)
    return y.astype(x.dtype)


def rms_norm(x, w):
    xf = x.astype(jnp.float32)
    return xf * lax.rsqrt(jnp.mean(xf * xf, -1, keepdims=True) + RMS_EPS) * w.astype(jnp.float32)


def l2_normalize(x):
    return x * lax.rsqrt(jnp.sum(x * x, -1, keepdims=True) + RMS_EPS)


def causal_depthwise_conv(x, w):
    width, seq = w.shape[0], x.shape[1]
    xp = jnp.pad(x, ((0, 0), (width - 1, 0), (0, 0)))
    y = w[0] * xp[:, :seq]
    for j in range(1, width):
        y = y + w[j] * xp[:, j:j + seq]
    return y


def rel_bucket(dist):
    max_exact = NUM_BUCKETS // 2
    d = jnp.maximum(dist, 1).astype(jnp.float32)
    large = max_exact + (jnp.log(d / max_exact) / math.log(REL_MAX_DIST / max_exact)
                         * (NUM_BUCKETS - max_exact)).astype(jnp.int32)
    large = jnp.minimum(large, NUM_BUCKETS - 1)
    return jnp.where(dist < max_exact, dist, large)


def gated_delta_rule(q, k, v, g, beta):
    bsz, seq, nh, dk = q.shape
    dv = v.shape[-1]
    c = GDN_CHUNK
    n = seq // c

    def chunk(t):
        return jnp.moveaxis(t.reshape((bsz, n, c, nh) + t.shape[3:]), 3, 1)

    qc, kc, vc, gc, bc = chunk(q), chunk(k), chunk(v), chunk(g), chunk(beta)
    gam = jnp.cumsum(gc, axis=-1)
    idx = jnp.arange(c)
    incl = idx[:, None] >= idx[None, :]
    strict = idx[:, None] > idx[None, :]
    decay = jnp.exp(jnp.where(incl, gam[..., :, None] - gam[..., None, :], -jnp.inf))
    kk = jnp.einsum('bhnid,bhnjd->bhnij', kc, kc)
    lower = jnp.eye(c, dtype=jnp.float32) + jnp.where(strict, bc[..., :, None] * kk * decay, 0.0)
    rhs = jnp.concatenate([vc * bc[..., None], kc * (bc * jnp.exp(gam))[..., None]], axis=-1)
    sol = lax.linalg.triangular_solve(lower, rhs, left_side=True, lower=True, unit_diagonal=True)
    u, w = sol[..., :dv], sol[..., dv:]
    qk = jnp.einsum('bhnid,bhnjd->bhnij', qc, kc) * decay
    q_dec = qc * jnp.exp(gam)[..., None]
    k_dec = kc * jnp.exp(gam[..., -1:] - gam)[..., None]
    g_last = jnp.exp(gam[..., -1])

    def step(state, inp):
        u_n, w_n, qk_n, qd_n, kd_n, gl_n = inp
        delta = u_n - jnp.einsum('bhck,bhkv->bhcv', w_n, state)
        o_n = jnp.einsum('bhck,bhkv->bhcv', qd_n, state) + jnp.einsum('bhij,bhjv->bhiv', qk_n, delta)
        state = gl_n[..., None, None] * state + jnp.einsum('bhck,bhcv->bhkv', kd_n, delta)
        return state, o_n

    xs = tuple(jnp.moveaxis(t, 2, 0) for t in (u, w, qk, q_dec, k_dec, g_last))
    state0 = jnp.zeros((bsz, nh, dk, dv), jnp.float32)
    _, o = lax.scan(step, state0, xs)
    return o.transpose(1, 0, 3, 2, 4).reshape(bsz, seq, nh, dv)


def dilated_branch(q, k, v, window, dilation, rel_bias):
    bsz, seq, nh, dh = q.shape
    span = window // dilation
    sub = seq // dilation
    nb = -(-sub // SWA_BLOCK)
    padded = nb * SWA_BLOCK

    def to_blocks(t):
        t = t.reshape(bsz, sub, dilation, nh, dh).transpose(0, 2, 1, 3, 4)
        t = jnp.pad(t, ((0, 0), (0, 0), (0, padded - sub), (0, 0), (0, 0)))
        return t.reshape(bsz, dilation, nb, SWA_BLOCK, nh, dh)

    def with_prev(t):
        prev = jnp.pad(t, ((0, 0), (0, 0), (1, 0), (0, 0), (0, 0), (0, 0)))[:, :, :-1]
        return jnp.concatenate([prev, t], axis=3)

    qb = to_blocks(q)
    kb, vb = with_prev(to_blocks(k)), with_prev(to_blocks(v))
    qi = jnp.arange(SWA_BLOCK)[:, None] + SWA_BLOCK
    kj = jnp.arange(2 * SWA_BLOCK)[None, :]
    rel = qi - kj
    key_sub = jnp.arange(nb)[:, None, None] * SWA_BLOCK + kj[None] - SWA_BLOCK
    mask = (rel >= 0) & (rel <= span) & (key_sub >= 0)
    bias = jnp.moveaxis(rel_bias[rel_bucket(jnp.maximum(rel, 0) * dilation)], -1, 0)
    s = jnp.einsum('bgnqhd,bgnkhd->bgnhqk', qb, kb).astype(jnp.float32) * dh ** -0.5
    s = s + bias.astype(jnp.float32)
    s = jnp.where(mask[None, None, :, None], s, -jnp.inf)
    m = jnp.max(s, axis=-1, keepdims=True)
    p = jnp.exp(s - m)
    l = jnp.sum(p, axis=-1, keepdims=True)
    o = jnp.einsum('bgnhqk,bgnkhd->bgnqhd', (p / l).astype(v.dtype), vb)
    lse = (m + jnp.log(l))[..., 0]
    o = o.reshape(bsz, dilation, padded, nh, dh)[:, :, :sub]
    o = o.transpose(0, 2, 1, 3, 4).reshape(bsz, seq, nh, dh)
    lse = lse.transpose(0, 1, 2, 4, 3).reshape(bsz, dilation, padded, nh)[:, :, :sub]
    lse = lse.transpose(0, 2, 1, 3).reshape(bsz, seq, nh)
    return o, lse


def dilated_attention(q, k, v, rel_bias):
    bsz, seq = q.shape[0], q.shape[1]
    outs, lses = [], []
    for g, (window, dilation) in enumerate(SWA_CONFIGS):
        o, lse = dilated_branch(q[:, :, g], k[:, :, g], v[:, :, g], window, dilation, rel_bias)
        outs.append(o)
        lses.append(lse)
    wts = jax.nn.softmax(jnp.stack(lses, 0), axis=0)
    o = jnp.einsum('gbsh,gbshd->bshd', wts, jnp.stack(outs, 0).astype(jnp.float32))
    return o.reshape(bsz, seq, SWA_OUT)


def even_mixer(x, rel_bias, w_in, conv_w, a_log, dt_bias, norm_w, w_out):
    bsz, seq, _ = x.shape
    f32 = jnp.float32
    q_a, k_a, v_a, z_a, b_a, a_a, q_b, k_b, v_b = jnp.split(x @ w_in, EVEN_SPLITS, axis=-1)
    qkv = jax.nn.silu(causal_depthwise_conv(jnp.concatenate([q_a, k_a, v_a], -1), conv_w)).astype(f32)
    q_a, k_a, v_a = jnp.split(qkv, (GDN_QK, 2 * GDN_QK), axis=-1)
    hk = (bsz, seq, GDN_HEADS, GDN_DK)
    q_a = l2_normalize(q_a.reshape(hk)) * GDN_DK ** -0.5
    k_a = l2_normalize(k_a.reshape(hk))
    v_a = v_a.reshape(bsz, seq, GDN_HEADS, GDN_DV)
    beta = jax.nn.sigmoid(b_a.astype(f32))
    g = -jnp.exp(a_log.astype(f32)) * jax.nn.softplus(a_a.astype(f32) + dt_bias.astype(f32))
    o_a = gated_delta_rule(q_a, k_a, v_a, g, beta)
    o_a = rms_norm(o_a, norm_w) * jax.nn.silu(z_a.astype(f32).reshape(bsz, seq, GDN_HEADS, GDN_DV))
    o_a = o_a.reshape(bsz, seq, GDN_V).astype(x.dtype)
    shp = (bsz, seq, SWA_GROUPS, SWA_HEADS, SWA_HEAD_DIM)
    o_b = dilated_attention(q_b.reshape(shp), k_b.reshape(shp), v_b.reshape(shp), rel_bias).astype(x.dtype)
    return jnp.concatenate([o_a, o_b], axis=-1) @ w_out


def diff_attention(q, k, v, lam, rel_bias):
    bsz, seq, nh, _, dh = q.shape
    nq = seq // Q_BLOCK
    qb = q.reshape(bsz, nq, Q_BLOCK, nh, 2, dh).transpose(1, 0, 3, 4, 2, 5)
    kt = k.transpose(0, 2, 3, 1, 4)
    vt = v.transpose(0, 2, 1, 3)
    k_pos = jnp.arange(seq)

    def block(args):
        q_blk, start = args
        dist = (start + jnp.arange(Q_BLOCK))[:, None] - k_pos[None, :]
        bias = jnp.moveaxis(rel_bias[rel_bucket(jnp.maximum(dist, 0))], -1, 0).astype(jnp.float32)
        s = jnp.einsum('bhcqd,bhckd->bhcqk', q_blk, kt).astype(jnp.float32) * dh ** -0.5
        s = jnp.where(dist >= 0, s + bias[None, :, None], -jnp.inf)
        p = jax.nn.softmax(s, axis=-1)
        attn = p[:, :, 0] - lam * p[:, :, 1]
        return jnp.einsum('bhqk,bhkv->bhqv', attn.astype(v.dtype), vt)

    o = lax.map(block, (qb, jnp.arange(nq) * Q_BLOCK))
    return o.transpose(1, 0, 3, 2, 4).reshape(bsz, seq, nh, 2 * dh)


def gla_chunked(q, k, v, log_a):
    bsz, seq, nh, dk = q.shape
    dv = v.shape[-1]
    c = GLA_CHUNK
    n = seq // c

    def chunk(t):
        return t.reshape(bsz, n, c, nh, t.shape[-1]).transpose(1, 0, 3, 2, 4)

    qc, kc, vc = chunk(q), chunk(k), chunk(v)
    bc = jnp.cumsum(chunk(log_a), axis=3)
    idx = jnp.arange(c)
    incl = (idx[:, None] >= idx[None, :])[:, :, None]

    def step(state, inp):
        q_n, k_n, v_n, b_n = inp
        dec = jnp.exp(jnp.where(incl, b_n[..., :, None, :] - b_n[..., None, :, :], -jnp.inf))
        attn = jnp.einsum('bhik,bhjk,bhijk->bhij', q_n, k_n, dec)
        o_n = jnp.einsum('bhik,bhkv->bhiv', q_n * jnp.exp(b_n), state) + jnp.einsum('bhij,bhjv->bhiv', attn, v_n)
        b_last = b_n[..., -1:, :]
        state = jnp.exp(b_last[..., 0, :])[..., None] * state + jnp.einsum(
            'bhck,bhcv->bhkv', k_n * jnp.exp(b_last - b_n), v_n)
        return state, o_n

    state0 = jnp.zeros((bsz, nh, dk, dv), jnp.float32)
    _, o = lax.scan(step, state0, (qc, kc, vc, bc))
    return o.transpose(1, 0, 3, 2, 4).reshape(bsz, seq, nh, dv)


def odd_mixer(x, rel_bias, w_in, lam_params, diff_norm_w, gla_w_gate, gla_b_gate, gla_norm_w, w_out, lam_init):
    bsz, seq, _ = x.shape
    f32 = jnp.float32
    q_c, k_c, v_c, q_d, k_d, v_d, r_d, g_d = jnp.split(x @ w_in, ODD_SPLITS, axis=-1)
    lp = lam_params.astype(f32)
    lam = jnp.exp(jnp.sum(lp[0] * lp[1])) - jnp.exp(jnp.sum(lp[2] * lp[3])) + lam_init
    shp = (bsz, seq, DIFF_HEADS, 2, DIFF_HEAD_DIM)
    o_c = diff_attention(q_c.reshape(shp), k_c.reshape(shp),
                         v_c.reshape(bsz, seq, DIFF_HEADS, 2 * DIFF_HEAD_DIM), lam, rel_bias)
    o_c = (rms_norm(o_c, diff_norm_w) * (1.0 - lam_init)).reshape(bsz, seq, DIFF_V).astype(x.dtype)
    log_a = jax.nn.log_sigmoid((g_d @ gla_w_gate + gla_b_gate).astype(f32)) / GLA_GATE_TAU
    hk = (bsz, seq, GLA_HEADS, GLA_DK)
    o_d = gla_chunked(q_d.astype(f32).reshape(hk) * GLA_DK ** -0.5, k_d.astype(f32).reshape(hk),
                      v_d.astype(f32).reshape(bsz, seq, GLA_HEADS, GLA_DV), log_a.reshape(hk))
    o_d = rms_norm(o_d, gla_norm_w) * jax.nn.silu(r_d.astype(f32).reshape(bsz, seq, GLA_HEADS, GLA_DV))
    o_d = o_d.reshape(bsz, seq, GLA_V).astype(x.dtype)
    return jnp.concatenate([o_c, o_d], axis=-1) @ w_out


def conv_ffn(x, w_up, conv_w, conv_b, w_down):
    h = causal_depthwise_conv(x @ w_up, conv_w) + conv_b
    gate, val = jnp.split(h, 2, axis=-1)
    return (jax.nn.silu(gate) * val) @ w_down


def setup_inputs(seed: int = 0) -> dict:
    key = jax.random.key(seed)
    ks = jax.random.split(key, 24)
    f32 = jnp.float32

    def normal(k, shape, scale):
        return jax.random.normal(k, shape, f32) * scale

    dt = jnp.exp(jax.random.uniform(ks[5], (N_EVEN, GDN_HEADS), f32, math.log(1e-3), math.log(1e-1)))
    return {
        'x': normal(ks[0], (BATCH, SEQ, D_MODEL), 1.0),
        'rel_bias': normal(ks[1], (NUM_BUCKETS, REL_HEADS), 0.1),
        'w_in_even': normal(ks[2], (N_EVEN, D_MODEL, EVEN_IN), D_MODEL ** -0.5),
        'gdn_conv_w': normal(ks[3], (N_EVEN, GDN_CONV, 2 * GDN_QK + GDN_V), GDN_CONV ** -0.5),
        'gdn_a_log': jnp.log(jax.random.uniform(ks[4], (N_EVEN, GDN_HEADS), f32, 1.0, 16.0)),
        'gdn_dt_bias': dt + jnp.log(-jnp.expm1(-dt)),
        'gdn_norm_w': 1.0 + normal(ks[6], (N_EVEN, GDN_DV), 0.05),
        'w_out_even': normal(ks[7], (N_EVEN, EVEN_OUT, D_MODEL), EVEN_OUT ** -0.5 * DEEPNORM_BETA),
        'w_in_odd': normal(ks[8], (N_ODD, D_MODEL, ODD_IN), D_MODEL ** -0.5),
        'diff_lambda': normal(ks[9], (N_ODD, 4, DIFF_HEAD_DIM), 0.1),
        'diff_norm_w': 1.0 + normal(ks[10], (N_ODD, 2 * DIFF_HEAD_DIM), 0.05),
        'gla_w_gate': normal(ks[11], (N_ODD, GLA_GATE_RANK, GLA_QK), GLA_GATE_RANK ** -0.5),
        'gla_b_gate': normal(ks[12], (N_ODD, GLA_QK), 0.1),
        'gla_norm_w': 1.0 + normal(ks[13], (N_ODD, GLA_DV), 0.05),
        'w_out_odd': normal(ks[14], (N_ODD, ODD_OUT, D_MODEL), ODD_OUT ** -0.5 * DEEPNORM_BETA),
        'ffn_w_up': normal(ks[15], (DEPTH, D_MODEL, 2 * D_FF), D_MODEL ** -0.5),
        'ffn_conv_w': normal(ks[16], (DEPTH, FFN_CONV, 2 * D_FF), FFN_CONV ** -0.5),
        'ffn_conv_b': normal(ks[17], (DEPTH, 2 * D_FF), 0.02),
        'ffn_w_down': normal(ks[18], (DEPTH, D_FF, D_MODEL), D_FF ** -0.5 * DEEPNORM_BETA),
        'ln_g': 1.0 + normal(ks[19], (DEPTH, 2, D_MODEL), 0.05),
        'ln_b': normal(ks[20], (DEPTH, 2, D_MODEL), 0.02),
    }


def reference(x, rel_bias, w_in_even, gdn_conv_w, gdn_a_log, gdn_dt_bias, gdn_norm_w, w_out_even,
              w_in_odd, diff_lambda, diff_norm_w, gla_w_gate, gla_b_gate, gla_norm_w, w_out_odd,
              ffn_w_up, ffn_conv_w, ffn_conv_b, ffn_w_down, ln_g, ln_b):
    h = x
    for layer in range(DEPTH):
        i = layer // 2
        if layer % 2 == 0:
            mix = even_mixer(h, rel_bias, w_in_even[i], gdn_conv_w[i], gdn_a_log[i], gdn_dt_bias[i],
                             gdn_norm_w[i], w_out_even[i])
        else:
            lam_init = 0.8 - 0.6 * math.exp(-0.3 * layer)
            mix = odd_mixer(h, rel_bias, w_in_odd[i], diff_lambda[i], diff_norm_w[i], gla_w_gate[i],
                            gla_b_gate[i], gla_norm_w[i], w_out_odd[i], lam_init)
        h = layer_norm(DEEPNORM_ALPHA * h + mix, ln_g[layer, 0], ln_b[layer, 0])
        ffn = conv_ffn(h, ffn_w_up[layer], ffn_conv_w[layer], ffn_conv_b[layer], ffn_w_down[layer])
        h = layer_norm(DEEPNORM_ALPHA * h + ffn, ln_g[layer, 1], ln_b[layer, 1])
    return h
```

```python
import numpy as np
import concourse.bass as bass
import concourse.mybir as mybir
from concourse.bass_utils import run_bass_kernel_spmd

F32 = mybir.dt.float32
BF16 = mybir.dt.bfloat16
ALU = mybir.AluOpType
AF = mybir.ActivationFunctionType
AX = mybir.AxisListType

SEM_CAP = 30000


class Buf:
    __slots__ = ("name", "t", "w", "r", "excl")

    def __init__(self, name, t=None, excl=False):
        self.name = name
        self.t = t
        self.excl = excl
        self.w = None
        self.r = []

    def __getitem__(self, idx):
        return self.t[idx]


class Rot:
    def __init__(self, bufs):
        self.bufs = bufs
        self.i = 0

    def next(self):
        b = self.bufs[self.i]
        self.i = (self.i + 1) % len(self.bufs)
        return b


class Stream:
    __slots__ = ("sem", "cnt", "id")
    _next = 0

    def __init__(self, sem):
        self.sem = sem
        self.cnt = 0
        self.id = Stream._next
        Stream._next += 1


class KB:
    def __init__(self, nc, n_dma_sems=24):
        self.nc = nc
        self.eng = {"pe": nc.tensor, "act": nc.scalar, "dve": nc.vector,
                    "pool": nc.gpsimd, "sp": nc.sync}
        self._ctx = []
        self._sem_ctx = []
        self.stream = {}
        for e in self.eng:
            self.stream[e] = self._new_stream("s_" + e)
        self.seen = {e: {} for e in self.eng}
        self.dma_pools = {q: [self._new_stream("d%s%d" % (q, i)) for i in range(n)]
                          for q, n in (("sp", 12), ("act", 6), ("pool", 8))}
        self.dma_rr = {"sp": 0, "act": 0, "pool": 0}
        self.ninst = {e: 0 for e in self.eng}
        self.nwait = {e: 0 for e in self.eng}
        self.out_dma = []
        self.prog = {e: [] for e in self.eng}

    def _new_stream(self, name):
        cm = self.nc.semaphore(name + "_%d" % Stream._next)
        sem = cm.__enter__()
        self._sem_ctx.append(cm)
        return Stream(sem)

    def enter(self, cm):
        v = cm.__enter__()
        self._ctx.append(cm)
        return v

    def emit(self):
        with self.nc.Block() as block:
            def mk(e):
                def body(eng):
                    for f in self.prog[e]:
                        f(eng)
                return body
            block.tensor(mk("pe"))
            block.scalar(mk("act"))
            block.vector(mk("dve"))
            block.gpsimd(mk("pool"))
            block.sync(mk("sp"))

    def close(self):
        for cm in reversed(self._ctx):
            cm.__exit__(None, None, None)
        self._ctx = []
        for cm in reversed(self._sem_ctx):
            cm.__exit__(None, None, None)
        self._sem_ctx = []

    def mark(self):
        return len(self._ctx)

    def release(self, mark):
        self.barrier()
        while len(self._ctx) > mark:
            self._ctx.pop().__exit__(None, None, None)

    def rot(self, name, shape, dt, n):
        return Rot([self.sb("%s%d" % (name, i), shape, dt) for i in range(n)])

    def sb(self, name, shape, dt=F32):
        self._uid = getattr(self, "_uid", 0) + 1
        name = "%s_%d" % (name, self._uid)
        t = self.enter(self.nc.sbuf_tensor(name, list(shape), dt))
        return Buf(name, t)

    def ps(self, name, shape, dt=F32):
        t = self.enter(self.nc.psum_tensor(name, list(shape), dt))
        return Buf(name, t, excl=True)

    def _wait(self, e, dep):
        st, c = dep
        if self.seen[e].get(st.id, 0) >= c:
            return
        if st is self.stream[e] and (e == "pe" or c > st.cnt):
            return
        sem = st.sem
        self.prog[e].append(lambda eng, sem=sem, c=c: eng.wait_ge(sem, c))
        self.seen[e][st.id] = c
        self.nwait[e] += 1

    def _deps(self, e, reads, writes):
        for b in reads:
            if b.w is not None:
                self._wait(e, b.w)
        for b in writes:
            if b.w is not None:
                self._wait(e, b.w)
            for d in b.r:
                self._wait(e, d)

    def _record(self, tag, reads, writes):
        for b in reads:
            b.r.append(tag)
            if len(b.r) > 12:
                best = {}
                for st, c in b.r:
                    if st.id not in best or best[st.id][1] < c:
                        best[st.id] = (st, c)
                b.r = list(best.values())
        for b in writes:
            b.w = tag
            b.r = []

    def op(self, e, meth, *args, R=(), W=(), inc=True, **kw):
        if any(b.excl for b in R):
            W = list(W) + [b for b in R if b.excl and b not in W]
            R = [b for b in R if not b.excl]
        self._deps(e, R, W)
        st = self.stream[e]
        if st.cnt >= SEM_CAP and inc:
            st = self.stream[e] = self._new_stream("s_" + e)
        self.ninst[e] += 1
        if inc:
            st.cnt += 1
            sem = st.sem
            self.prog[e].append(lambda eng: getattr(eng, meth)(*args, **kw).then_inc(sem, 1))
            tag = (st, st.cnt)
        else:
            self.prog[e].append(lambda eng: getattr(eng, meth)(*args, **kw))
            tag = (st, st.cnt + 1)
        self._record(tag, R, W)

    def dma(self, out, in_, R=(), W=(), q="sp", is_output=False, **kw):
        self._deps(q, R, W)
        pool = self.dma_pools[q]
        idx = self.dma_rr[q]
        st = pool[idx]
        self.dma_rr[q] = (idx + 1) % len(pool)
        if st.cnt > 0:
            self._wait(q, (st, st.cnt))
        if st.cnt >= SEM_CAP:
            st = pool[idx] = self._new_stream("d" + q)
        st.cnt += 16
        sem = st.sem
        self.prog[q].append(lambda eng: eng.dma_start(out=out, in_=in_, **kw).then_inc(sem, 16))
        self.ninst[q] += 1
        tag = (st, st.cnt)
        self._record(tag, R, W)
        if is_output:
            self.out_dma.append(tag)

    def coll(self, kind, op, groups, ins, outs, R=(), W=()):
        q = "pool"
        self._deps(q, R, W)
        pool = self.dma_pools[q]
        idx = self.dma_rr[q]
        st = pool[idx]
        self.dma_rr[q] = (idx + 1) % len(pool)
        if st.cnt > 0:
            self._wait(q, (st, st.cnt))
        st.cnt += 16
        sem = st.sem
        self.prog[q].append(lambda eng: eng.collective_compute(kind, op, groups, ins, outs).then_inc(sem, 16))
        self.ninst[q] += 1
        self._record((st, st.cnt), R, W)

    def all_dma_streams(self):
        return [st for p in self.dma_pools.values() for st in p]

    def finish(self):
        for tag in self.out_dma:
            self._wait("sp", tag)
        for e in self.eng:
            st = self.stream[e]
            if st.cnt > 0:
                self._wait("sp", (st, st.cnt))
        for st in self.all_dma_streams():
            if st.cnt > 0:
                self._wait("sp", (st, st.cnt))

    def mm(self, out, lhsT, rhs, start, stop, R, W, inc=None):
        if inc is None:
            inc = stop
        self.op("pe", "matmul", out, lhsT, rhs, start=start, stop=stop, R=R, W=W, inc=inc)

    def transpose(self, out, in_, ident, R, W, inc=True):
        self.op("pe", "transpose", out, in_, ident, R=R, W=W, inc=inc)

    def act(self, out, in_, func, R, W, **kw):
        self.op("act", "activation", out, in_, func, R=R, W=W, **kw)

    def dve(self, meth, *args, R=(), W=(), **kw):
        self.op("dve", meth, *args, R=R, W=W, **kw)

    def pool(self, meth, *args, R=(), W=(), **kw):
        self.op("pool", meth, *args, R=R, W=W, **kw)

    def barrier(self):
        tags = []
        for e in self.eng:
            st = self.stream[e]
            if st.cnt > 0:
                tags.append((st, st.cnt))
        for st in self.all_dma_streams():
            if st.cnt > 0:
                tags.append((st, st.cnt))
        for e in self.eng:
            for t in tags:
                self._wait(e, t)


D = 1024
DFF = 2816
ALPHA = float(4 ** 0.25)
LN_EPS = 1e-5
RMS_EPS = 1e-6
MUL = ALU.mult
ADD = ALU.add
SUB = ALU.subtract
POW = ALU.pow


class G:
    pass


def build(S, dbg=False, phases=None):
    nc = bass.Bass("TRN2", target_bir_lowering=False)
    k = KB(nc)
    g = G()
    g.nc, g.k, g.S, g.NB = nc, k, S, S // 512
    skind = "ExternalOutput" if dbg else "Internal"

    def din(name, shape, dt=F32):
        return nc.dram_tensor(name, list(shape), dt, kind="ExternalInput").ap()

    def dsc(name, shape, dt=F32):
        return nc.dram_tensor(name, list(shape), dt, kind=skind).ap()

    g.dsc = dsc
    I = {}
    for name, shape in [
        ("x", [S, D]), ("rel_bias", [32, 4]), ("w_in_even", [D, 5388]), ("gdn_conv_w", [4, 2304]),
        ("gdn_a_log", [1, 6]), ("gdn_dt_bias", [1, 6]), ("gdn_norm_w", [1, 128]),
        ("w_out_even", [D, D]), ("w_in_odd", [D, 3088]), ("diff_lambda", [4, 64]),
        ("diff_norm_w", [128, 1]), ("gla_w_gate", [16, 256]), ("gla_b_gate", [1, 256]),
        ("gla_norm_w", [1, 128]), ("w_out_odd", [D, D]), ("ffn_w_up", [2, D, 2 * DFF]),
        ("ffn_conv_w", [2, 3, 2 * DFF]), ("ffn_conv_b", [2, 1, 2 * DFF]),
        ("ffn_w_down", [2, DFF, D]), ("ln_g", [4, D]), ("ln_b", [4, D]),
        ("c_ident", [128, 128]), ("c_ltri", [128, 128]), ("c_lstrict", [128, 128]),
        ("c_neg", [128, 128]), ("c_ohdiff", [33, 2560]), ("c_ohdil", [3, 33, 384]),
    ]:
        I[name] = din(name, shape)
    g.I = I
    g.y = nc.dram_tensor("y", [S, D], F32, kind="ExternalOutput").ap()
    g.hT = dsc("hT", [D, S], BF16)
    g.hA = dsc("hA", [S, D])
    g.hB = dsc("hB", [S, D])
    g.oT = dsc("oT", [D, S], BF16)
    g.ftd_h = nc.dram_tensor("ftd", [4, 128, 2560], BF16, kind=skind)
    g.ftl_h = nc.dram_tensor("ftl", [12, 128, 384], BF16, kind=skind)
    g.ftd, g.ftl = g.ftd_h.ap(), g.ftl_h.ap()

    g.PS = Rot([k.ps("psf%d" % i, [128, 512], F32) for i in range(6)])
    g.PB = Rot([k.ps("psb%d" % i, [128, 1024], BF16) for i in range(2)])
    g.ident = k.sb("ident", [128, 128])
    g.identb = k.sb("identb", [128, 128], BF16)
    g.ltri = k.sb("ltri", [128, 128])
    g.lstrict = k.sb("lstrict", [128, 128])
    g.neg = k.sb("neg", [128, 128])
    g.ones = k.sb("ones", [128, 128])
    g.nones = k.sb("nones", [128, 128])
    g.onesb = k.sb("onesb", [128, 128], BF16)
    k.dma(g.ident[:], I["c_ident"][:, :], W=[g.ident])
    k.dma(g.ltri[:], I["c_ltri"][:, :], W=[g.ltri])
    k.dma(g.lstrict[:], I["c_lstrict"][:, :], W=[g.lstrict])
    k.dma(g.neg[:], I["c_neg"][:, :], W=[g.neg])
    k.dve("tensor_copy", g.identb[:], g.ident[:], R=[g.ident], W=[g.identb])
    k.dve("memset", g.ones[:], 1.0, W=[g.ones])
    k.dve("memset", g.nones[:], -1.0, W=[g.nones])
    k.dve("memset", g.onesb[:], 1.0, W=[g.onesb])

    ph = phases
    def on(p):
        return ph is None or p in ph

    if on("tab"):
        phase_tables(g)
    if on("x0"):
        phase_x0(g)
    if on("ine"):
        phase_in_even(g)
    if on("gdn"):
        phase_gdn(g)
    if on("dil"):
        phase_dil(g)
    if on("oute"):
        phase_out(g, I["w_out_even"], I["x"], 0)
    if on("ffn0"):
        phase_ffn(g, 0, g.hB)
    if on("ino"):
        phase_in_odd(g)
    if on("diff"):
        phase_diff(g)
    if on("gla"):
        phase_gla(g)
    if on("outo"):
        phase_out(g, I["w_out_odd"], g.hB, 2)
    if on("ffn1"):
        phase_ffn(g, 1, g.y)
    k.finish()
    k.emit()
    k.close()
    return nc, k


def load_w_bf16(g, src2d, dst, nchunk, ncols, cs=None):
    k = g.k
    cs = cs or ncols
    stg = k.rot("wstg", [128, cs], F32, 2)
    i = 0
    for c in range(nchunk):
        for c0 in range(0, ncols, cs):
            w = min(cs, ncols - c0)
            st = stg.next()
            k.dma(st[:, 0:w], src2d[c * 128:(c + 1) * 128, c0:c0 + w], W=[st], q="act" if i % 2 else "sp")
            e = ("dve", "pool", "act")[i % 3]
            if e == "act":
                k.op("act", "copy", dst[:, c, c0:c0 + w], st[:, 0:w], R=[st], W=[dst])
            else:
                k.op(e, "tensor_copy", dst[:, c, c0:c0 + w], st[:, 0:w], R=[st], W=[dst])
            i += 1


def rows_to_cols(g, src_rows, nrow, ncol, dst):
    k = g.k
    m = k.mark()
    st = k.sb("r2c", [8, ncol])
    k.dma(st[0:nrow, :], src_rows, W=[st])
    for c in range(ncol // 128):
        ps = g.PS.next()
        k.transpose(ps[:, 0:nrow], st[0:nrow, c * 128:(c + 1) * 128], g.ident[0:nrow, 0:nrow], [st, g.ident], [ps])
        k.dve("tensor_copy", dst[:, c, 0:nrow], ps[:, 0:nrow], R=[ps], W=[dst])
    k.release(m)


def ln_block(g, pa_pb_fn, hin_src, lnrow, t0, hout_dst, xTn, tt, lnw, final_bufs):
    k = g.k
    gb, bb, hin_r, z_r, y_r, yb_r, sm_r = lnw
    hin = hin_r.next()
    k.dma(hin[:], hin_src[t0:t0 + 128, :], W=[hin])
    pa, pb = pa_pb_fn()
    z = z_r.next()
    k.dve("scalar_tensor_tensor", z[:, 0:512], hin[:, 0:512], ALPHA, pa[:, :], MUL, ADD, R=[hin, pa], W=[z])
    k.dve("scalar_tensor_tensor", z[:, 512:1024], hin[:, 512:1024], ALPHA, pb[:, :], MUL, ADD, R=[hin, pb], W=[z])
    sm = sm_r.next()
    k.dve("bn_stats", sm[:, 0:6], z[:, 0:512], R=[z], W=[sm])
    k.dve("bn_stats", sm[:, 6:12], z[:, 512:1024], R=[z], W=[sm])
    k.dve("bn_aggr", sm[:, 12:14], sm[:, 0:12], R=[sm], W=[sm])
    k.act(sm[:, 14:15], sm[:, 13:14], AF.Ln, [sm], [sm], bias=LN_EPS)
    k.act(sm[:, 14:15], sm[:, 14:15], AF.Exp, [sm], [sm], scale=-0.5)
    k.dve("scalar_tensor_tensor", sm[:, 15:16], sm[:, 12:13], -1.0, sm[:, 14:15], MUL, MUL, R=[sm], W=[sm])
    k.act(z[:], z[:], AF.Identity, [z, sm], [z], scale=sm[:, 14:15], bias=sm[:, 15:16])
    k.dve("tensor_tensor", z[:], z[:], gb[:], MUL, R=[z, gb], W=[z])
    yt = z
    k.pool("tensor_tensor", yt[:], z[:], bb[:], ADD, R=[z, bb], W=[yt])
    k.dma(hout_dst[t0:t0 + 128, :], yt[:], R=[yt], q="pool", is_output=final_bufs)
    if xTn is not None:
        yb = yb_r.next()
        k.op("act", "copy", yb[:], yt[:], R=[yt], W=[yb])
        pbk = g.PB.next()
        for c in range(8):
            k.transpose(pbk[:, c * 128:(c + 1) * 128], yb[:, c * 128:(c + 1) * 128], g.identb[:], [yb, g.identb], [pbk],
                        inc=(c == 7))
        k.dve("tensor_copy", xTn[:, :, tt * 128:(tt + 1) * 128], pbk[:, :].rearrange("p (c t) -> p c t", c=8),
              R=[pbk], W=[xTn])


def ln_stages(g, pa_pb_fn, hin_src, t0, hout_dst, xTn, tt, lnw, final_bufs):
    k = g.k
    gb, bb, hin_r, z_r, y_r, yb_r, sm_r = lnw
    st = {}

    def s0():
        hin = hin_r.next()
        k.dma(hin[:], hin_src[t0:t0 + 128, :], W=[hin])
        pa, pb = pa_pb_fn()
        z = st["z"] = z_r.next()
        sm = st["sm"] = sm_r.next()
        k.dve("scalar_tensor_tensor", z[:, 0:512], hin[:, 0:512], ALPHA, pa[:, :], MUL, ADD, R=[hin, pa], W=[z])
        k.dve("scalar_tensor_tensor", z[:, 512:1024], hin[:, 512:1024], ALPHA, pb[:, :], MUL, ADD, R=[hin, pb], W=[z])
        k.dve("bn_stats", sm[:, 0:6], z[:, 0:512], R=[z], W=[sm])
        k.dve("bn_stats", sm[:, 6:12], z[:, 512:1024], R=[z], W=[sm])
        k.dve("bn_aggr", sm[:, 12:14], sm[:, 0:12], R=[sm], W=[sm])

    def s1():
        sm = st["sm"]
        k.act(sm[:, 14:15], sm[:, 13:14], AF.Ln, [sm], [sm], bias=LN_EPS)
        k.act(sm[:, 14:15], sm[:, 14:15], AF.Exp, [sm], [sm], scale=-0.5)
        k.dve("scalar_tensor_tensor", sm[:, 15:16], sm[:, 12:13], -1.0, sm[:, 14:15], MUL, MUL, R=[sm], W=[sm])

    def s2():
        z, sm = st["z"], st["sm"]
        k.act(z[:], z[:], AF.Identity, [z, sm], [z], scale=sm[:, 14:15], bias=sm[:, 15:16])

    def s3():
        z = st["z"]
        k.dve("tensor_tensor", z[:], z[:], gb[:], MUL, R=[z, gb], W=[z])
        k.pool("tensor_tensor", z[:], z[:], bb[:], ADD, R=[z, bb], W=[z])
        k.dma(hout_dst[t0:t0 + 128, :], z[:], R=[z], q="pool", is_output=final_bufs)

    def s4():
        if xTn is None:
            return
        z = st["z"]
        yb = yb_r.next()
        k.op("act", "copy", yb[:], z[:], R=[z], W=[yb])
        pbk = g.PB.next()
        for c in range(8):
            k.transpose(pbk[:, c * 128:(c + 1) * 128], yb[:, c * 128:(c + 1) * 128], g.identb[:], [yb, g.identb], [pbk],
                        inc=(c == 7))
        k.dve("tensor_copy", xTn[:, :, tt * 128:(tt + 1) * 128], pbk[:, :].rearrange("p (c t) -> p c t", c=8),
              R=[pbk], W=[xTn])

    return [s0, s1, s2, s3, s4]


def ln_alloc(g, lnrow, depth=2):
    k = g.k
    gb = k.sb("lngb", [128, D])
    bb = k.sb("lnbb", [128, D])
    k.dma(gb[:], g.I["ln_g"][lnrow, :].partition_broadcast(128), W=[gb])
    k.dma(bb[:], g.I["ln_b"][lnrow, :].partition_broadcast(128), W=[bb])
    return (gb, bb, k.rot("lnhin", [128, D], F32, 2), k.rot("lnz", [128, D], F32, depth),
            None, k.rot("lnyb", [128, D], BF16, 1 if depth == 2 else 2), k.rot("lnsm", [128, 16], F32, depth))


def phase_x0(g):
    k, S = g.k, g.S
    m = k.mark()
    xin = k.rot("x0in", [128, D], F32, 2)
    xo = k.rot("x0o", [128, 8, 512], BF16, 2)
    for b in range(g.NB):
        xt = xo.next()
        for t in range(4):
            t0 = b * 512 + t * 128
            xi = xin.next()
            k.dma(xi[:], g.I["x"][t0:t0 + 128, :], W=[xi])
            for hf in range(2):
                ps = g.PS.next()
                for c in range(4):
                    cc = hf * 4 + c
                    k.transpose(ps[:, c * 128:(c + 1) * 128], xi[:, cc * 128:(cc + 1) * 128], g.ident[:],
                                [xi, g.ident], [ps], inc=(c == 3))
                k.op("act" if hf else "dve", "copy" if hf else "tensor_copy",
                     xt[:, hf * 4:(hf + 1) * 4, t * 128:(t + 1) * 128],
                     ps[:, :].rearrange("p (c t) -> p c t", c=4), R=[ps], W=[xt])
        k.dma(g.hT.rearrange("(c p) s -> p c s", p=128)[:, :, b * 512:(b + 1) * 512], xt[:], R=[xt], q="pool")
    k.release(m)


def phase_out(g, wout, hin_src, lnrow):
    k, S = g.k, g.S
    m = k.mark()
    W = k.sb("wout", [128, 8, D], BF16)
    m2 = k.mark()
    load_w_bf16(g, wout, W, 8, D)
    k.release(m2)
    lnw = ln_alloc(g, lnrow, depth=6)
    oin = k.rot("oin", [128, 8, 512], BF16, 2)
    xo = k.rot("xTn", [128, 8, 512], BF16, 3)
    oTv = g.oT.rearrange("(c p) s -> p c s", p=128)
    hTv = g.hT.rearrange("(c p) s -> p c s", p=128)
    NT = S // 128
    tiles = {}
    blk = {}
    NST = 5
    for i in range(NT + NST - 1):
        if i < NT:
            b, t = divmod(i, 4)
            if t == 0:
                ot = oin.next()
                k.dma(ot[:], oTv[:, :, b * 512:(b + 1) * 512], W=[ot])
                blk[b] = (ot, xo.next())
            ot, xTn = blk[b]

            def mk(t=t, ot=ot):
                pa, pb = g.PS.next(), g.PS.next()
                for hf, p in enumerate((pa, pb)):
                    for c in range(8):
                        k.mm(p[:, :], ot[:, c, t * 128:(t + 1) * 128], W[:, c, hf * 512:(hf + 1) * 512],
                             c == 0, c == 7, [ot, W], [p])
                return pa, pb
            tiles[i] = ln_stages(g, mk, hin_src, i * 128, g.hA, xTn, t, lnw, False)
        for s in range(NST):
            j = i - s
            if 0 <= j < NT:
                tiles[j][s]()
                if s == NST - 1:
                    del tiles[j]
                    bj, tj = divmod(j, 4)
                    if tj == 3:
                        k.dma(hTv[:, :, bj * 512:(bj + 1) * 512], blk[bj][1][:], R=[blk[bj][1]], q="pool")
                        del blk[bj]
    k.release(m)


def phase_ffn(g, layer, hout):
    k, S = g.k, g.S
    m = k.mark()
    Wu = k.sb("wup", [128, 8, 2 * DFF], BF16)
    Wd = k.sb("wdn", [128, 22, D], BF16)
    cw = k.sb("fcw", [128, 44, 4], F32)
    m2 = k.mark()
    load_w_bf16(g, g.I["ffn_w_up"][layer], Wu, 8, 2 * DFF, cs=2816)
    load_w_bf16(g, g.I["ffn_w_down"][layer], Wd, 22, D)
    k.release(m2)
    m2 = k.mark()
    st = k.sb("fcst", [8, 2 * DFF])
    k.dma(st[0:3, :], g.I["ffn_conv_w"][layer], W=[st])
    k.dma(st[3:4, :], g.I["ffn_conv_b"][layer], W=[st])
    for c in range(44):
        ps = g.PS.next()
        k.transpose(ps[:, 0:4], st[0:4, c * 128:(c + 1) * 128], g.ident[0:4, 0:4], [st, g.ident], [ps])
        k.dve("tensor_copy", cw[:, c, :], ps[:, 0:4], R=[ps], W=[cw])
    k.release(m2)
    final = hout is g.y
    lnw = ln_alloc(g, 2 * layer + 1)
    halo = k.sb("fhalo", [128, 44, 2], F32)
    k.dve("memset", halo[:], 0.0, W=[halo])
    FB = 256
    xin = k.rot("fxT", [128, 8, FB], BF16, 2)
    gT = k.sb("fgT", [128, 22, FB], BF16)
    ur = k.rot("fu", [128, FB + 2], F32, 7)
    yr = k.rot("fy", [128, FB], F32, 10)
    hTv = g.hT.rearrange("(c p) s -> p c s", p=128)
    xT_next = None
    for b in range(S // FB):
        if b == 0:
            xT = xin.next()
            k.dma(xT[:], hTv[:, :, 0:FB], W=[xT])
        else:
            xT = xT_next
        if b + 1 < S // FB:
            xT_next = xin.next()
            k.dma(xT_next[:], hTv[:, :, (b + 1) * FB:(b + 2) * FB], W=[xT_next])
        items = [(j, half) for j in range(22) for half in range(2)]
        us, yvs = {}, {}
        LAG = 2
        for i in range(len(items) + 6):
            if i < len(items):
                j, half = items[i]
                cc = j + 22 * half
                ps = g.PS.next()
                for c in range(8):
                    k.mm(ps[:, 0:FB], Wu[:, c, cc * 128:(cc + 1) * 128], xT[:, c, :], c == 0, c == 7, [Wu, xT], [ps])
                u = ur.next()
                k.pool("tensor_copy", u[:, 0:2], halo[:, cc, :], R=[halo], W=[u])
                k.op("act", "copy", u[:, 2:FB + 2], ps[:, 0:FB], R=[ps], W=[u])
                k.pool("tensor_copy", halo[:, cc, :], u[:, FB:FB + 2], R=[u], W=[halo])
                us[i] = u
            if i >= 2 and i - 2 < len(items):
                j, half = items[i - 2]
                cc = j + 22 * half
                u = us[i - 2]
                yv = yvs[(j, half)] = yr.next()
                if half == 0:
                    k.act(yv[:], u[:, 2:FB + 2], AF.Identity, [u, cw], [yv], scale=cw[:, cc, 2:3], bias=cw[:, cc, 3:4])
                else:
                    k.dve("tensor_scalar", yv[:], u[:, 2:FB + 2], cw[:, cc, 2:3], cw[:, cc, 3:4], MUL, ADD, R=[u, cw], W=[yv])
            if i >= 3 and i - 3 < len(items):
                j, half = items[i - 3]
                cc = j + 22 * half
                u, yv = us[i - 3], yvs[(j, half)]
                k.dve("scalar_tensor_tensor", yv[:], u[:, 1:FB + 1], cw[:, cc, 1:2], yv[:], MUL, ADD, R=[u, cw, yv], W=[yv])
            if i >= 4 and i - 4 < len(items):
                j, half = items[i - 4]
                cc = j + 22 * half
                u, yv = us.pop(i - 4), yvs[(j, half)]
                k.dve("scalar_tensor_tensor", yv[:], u[:, 0:FB], cw[:, cc, 0:1], yv[:], MUL, ADD, R=[u, cw, yv], W=[yv])
            if i >= 5 and i - 5 < len(items) and items[i - 5][1] == 1:
                j = items[i - 5][0]
                k.act(yvs[(j, 0)][:], yvs[(j, 0)][:], AF.Silu, [yvs[(j, 0)]], [yvs[(j, 0)]])
            if i >= 6 and i - 6 < len(items) and items[i - 6][1] == 1:
                j = items[i - 6][0]
                y0, y1 = yvs.pop((j, 0)), yvs.pop((j, 1))
                k.dve("tensor_tensor", gT[:, j, :], y0[:], y1[:], MUL, R=[y0, y1], W=[gT])
        xTn = xT if not final else None
        for t in range(FB // 128):
            def mk(t=t):
                pa, pb = g.PS.next(), g.PS.next()
                for hf, p in enumerate((pa, pb)):
                    for c in range(22):
                        k.mm(p[:, :], gT[:, c, t * 128:(t + 1) * 128], Wd[:, c, hf * 512:(hf + 1) * 512],
                             c == 0, c == 21, [gT, Wd], [p])
                return pa, pb
            ln_block(g, mk, g.hA, 2 * layer + 1, b * FB + t * 128, hout, xTn, t, lnw, final)
        if xTn is not None:
            k.dma(hTv[:, :, b * FB:(b + 1) * FB], xTn[:], R=[xTn], q="pool")
    k.release(m)


def phase_tables(g):
    k, nc = g.k, g.nc
    m = k.mark()
    rb = k.sb("rb", [33, 4])
    k.dve("memset", rb[:], 1.0, W=[rb])
    k.dma(rb[0:32, :], g.I["rel_bias"][:, :], W=[rb])
    ohd = k.sb("ohd", [33, 2560])
    k.dma(ohd[:], g.I["c_ohdiff"][:, :], W=[ohd])
    ohl = k.sb("ohl", [33, 3, 384])
    k.dma(ohl[:], g.I["c_ohdil"].rearrange("g b x -> b g x"), W=[ohl])
    lhs_r = k.rot("tlhs", [33, 128], F32, 2)
    tb_r = k.rot("ttb", [128, 2560], BF16, 2)
    tl_r = k.rot("ttl", [128, 384], BF16, 2)
    for h in range(4):
        lhs = lhs_r.next()
        k.dve("tensor_scalar_mul", lhs[:], g.ones[0:33, :], rb[:, h:h + 1], R=[g.ones, rb], W=[lhs])
        tb = tb_r.next()
        for x0 in range(0, 2560, 512):
            ps = g.PS.next()
            k.mm(ps[:, :], lhs[:], ohd[:, x0:x0 + 512], True, True, [lhs, ohd], [ps])
            k.act(tb[:, x0:x0 + 512], ps[:, :], AF.Exp, [ps], [tb])
        k.dma(g.ftd[h], tb[:], R=[tb], q="pool")
        for gi in range(3):
            tl = tl_r.next()
            ps = g.PS.next()
            k.mm(ps[:, 0:384], lhs[:], ohl[:, gi, :], True, True, [lhs, ohl], [ps])
            k.act(tl[:], ps[:, 0:384], AF.Exp, [ps], [tl])
            k.dma(g.ftl[gi * 4 + h], tl[:], R=[tl], q="pool")
    k.release(m)


def phase_in_even(g):
    k, S, nc = g.k, g.S, g.nc
    dsc = g.dsc
    g.gqT = dsc("gqT", [768, S], BF16)
    g.gkT = dsc("gkT", [768, S], BF16)
    g.gk = dsc("gk", [S, 768], BF16)
    g.gv = dsc("gv", [S, 768], BF16)
    g.gz = dsc("gz", [S, 768], F32)
    g.gg = dsc("gg", [S, 6], F32)
    g.gbeta = dsc("gbeta", [S, 6], F32)
    g.dq = dsc("dq", [768, S], BF16)
    g.dk = dsc("dk", [768, S], BF16)
    g.dvv = dsc("dvv", [S, 768], BF16)
    m = k.mark()
    W = k.sb("wine", [128, 8, 5388], BF16)
    m2 = k.mark()
    load_w_bf16(g, g.I["w_in_even"], W, 8, 5388, cs=2694)
    k.release(m2)
    cw = k.sb("gcw", [128, 18, 4], F32)
    rows_to_cols(g, g.I["gdn_conv_w"], 4, 2304, cw)
    dtb = k.sb("dtb", [128, 6])
    nea = k.sb("nea", [128, 6])
    k.dma(dtb[:], g.I["gdn_dt_bias"][0, :].partition_broadcast(128), W=[dtb])
    k.dma(nea[:], g.I["gdn_a_log"][0, :].partition_broadcast(128), W=[nea])
    k.act(nea[:], nea[:], AF.Exp, [nea], [nea])
    k.dve("tensor_scalar_mul", nea[:], nea[:], -1.0, R=[nea], W=[nea])
    halo = k.sb("ghalo", [128, 18, 3], F32)
    k.dve("memset", halo[:], 0.0, W=[halo])
    xin = k.rot("exT", [128, 8, 512], BF16, 2)
    ur = k.rot("eu", [128, 515], F32, 5)
    yr = k.rot("ey", [128, 512], F32, 4)
    rsr = k.rot("ers", [128, 512], F32, 2)
    sqr = k.rot("esq", [128, 512], BF16, 2)
    qnr = k.rot("eqn", [128, 512], BF16, 3)
    tkr = k.rot("etk", [128, 4, 128], BF16, 2)
    ztr = k.rot("ezt", [128, 768], F32, 2)
    dvr = k.rot("edv", [128, 768], BF16, 2)
    smr = k.rot("esm", [128, 12], F32, 2)
    ggr = k.rot("egg", [128, 4, 6], F32, 2)
    ber = k.rot("ebe", [128, 4, 6], F32, 2)
    ybig = k.sb("eybig", [128, 12, 512], F32)
    hTv = g.hT.rearrange("(c p) s -> p c s", p=128)

    def to_tok(qn, dst, hh):
        pbk = g.PB.next()
        for t in range(4):
            k.transpose(pbk[:, t * 128:(t + 1) * 128], qn[:, t * 128:(t + 1) * 128], g.identb[:],
                        [qn, g.identb], [pbk], inc=(t == 3))
        tk = tkr.next()
        k.dve("tensor_copy", tk[:], pbk[:, 0:512].rearrange("p (t d) -> p t d", t=4), R=[pbk], W=[tk])
        k.dma(dst[to_tok.s0:to_tok.s0 + 512, hh * 128:(hh + 1) * 128].rearrange("(t p) d -> p t d", p=128), tk[:],
              R=[tk], q="pool")

    xT_next = None
    for b in range(g.NB):
        s0 = b * 512
        to_tok.s0 = s0
        if b == 0:
            xT = xin.next()
            k.dma(xT[:], hTv[:, :, 0:512], W=[xT])
        else:
            xT = xT_next
        if b + 1 < g.NB:
            xT_next = xin.next()
            k.dma(xT_next[:], hTv[:, :, s0 + 512:s0 + 1024], W=[xT_next])
        us = {}
        LAG = 2
        for i in range(18 + LAG):
            if i < 18:
                cc = i
                ps = g.PS.next()
                for c in range(8):
                    k.mm(ps[:, :], W[:, c, cc * 128:(cc + 1) * 128], xT[:, c, :], c == 0, c == 7, [W, xT], [ps])
                u = ur.next()
                k.pool("tensor_copy", u[:, 0:3], halo[:, cc, :], R=[halo], W=[u])
                k.op("act", "copy", u[:, 3:515], ps[:, :], R=[ps], W=[u])
                k.pool("tensor_copy", halo[:, cc, :], u[:, 512:515], R=[u], W=[halo])
                us[i] = u
            if i >= LAG:
                cc = i - LAG
                u = us.pop(cc)
                y = yr.next()
                if cc % 2:
                    k.act(y[:], u[:, 3:515], AF.Identity, [u, cw], [y], scale=cw[:, cc, 3:4])
                else:
                    k.dve("tensor_scalar_mul", y[:], u[:, 3:515], cw[:, cc, 3:4], R=[u, cw], W=[y])
                k.dve("scalar_tensor_tensor", y[:], u[:, 2:514], cw[:, cc, 2:3], y[:], MUL, ADD, R=[u, cw, y], W=[y])
                k.dve("scalar_tensor_tensor", y[:], u[:, 1:513], cw[:, cc, 1:2], y[:], MUL, ADD, R=[u, cw, y], W=[y])
                k.dve("scalar_tensor_tensor", y[:], u[:, 0:512], cw[:, cc, 0:1], y[:], MUL, ADD, R=[u, cw, y], W=[y])
                if cc < 12:
                    k.act(ybig[:, cc, :], y[:], AF.Silu, [y], [ybig])
                else:
                    hh = cc - 12
                    qn = qnr.next()
                    k.act(qn[:], y[:], AF.Silu, [y], [qn])
                    to_tok(qn, g.gv, hh)
        for cc in range(12):
            sq = sqr.next()
            k.dve("tensor_tensor", sq[:], ybig[:, cc, :], ybig[:, cc, :], MUL, R=[ybig], W=[sq])
            ps2 = g.PS.next()
            k.mm(ps2[:, :], g.onesb[:], sq[:], True, True, [g.onesb, sq], [ps2])
            rs = rsr.next()
            k.act(rs[:], ps2[:, :], AF.Ln, [ps2], [rs], bias=RMS_EPS)
            k.act(rs[:], rs[:], AF.Exp, [rs], [rs], scale=-0.5)
            sc = float(128 ** -0.5) if cc < 6 else 1.0
            qn = qnr.next()
            k.dve("scalar_tensor_tensor", qn[:], ybig[:, cc, :], sc, rs[:], MUL, MUL, R=[ybig, rs], W=[qn])
            dst = g.gqT if cc < 6 else g.gkT
            hh = cc % 6
            k.dma(dst[hh * 128:(hh + 1) * 128, s0:s0 + 512], qn[:], R=[qn], q="pool")
            if cc >= 6:
                to_tok(qn, g.gk, hh)
        gg = ggr.next()
        be = ber.next()
        for t in range(4):
            t0 = s0 + t * 128
            zt = ztr.next()
            dv = dvr.next()
            for hf in range(2):
                ps = g.PS.next()
                for c in range(8):
                    k.mm(ps[:, 0:384], xT[:, c, t * 128:(t + 1) * 128], W[:, c, 2304 + hf * 384:2304 + (hf + 1) * 384],
                         c == 0, c == 7, [W, xT], [ps])
                k.act(zt[:, hf * 384:(hf + 1) * 384], ps[:, 0:384], AF.Silu, [ps], [zt])
                ps = g.PS.next()
                for c in range(8):
                    k.mm(ps[:, 0:384], xT[:, c, t * 128:(t + 1) * 128], W[:, c, 4620 + hf * 384:4620 + (hf + 1) * 384],
                         c == 0, c == 7, [W, xT], [ps])
                k.dve("tensor_copy", dv[:, hf * 384:(hf + 1) * 384], ps[:, 0:384], R=[ps], W=[dv])
            k.dma(g.gz[t0:t0 + 128, :], zt[:], R=[zt], q="pool")
            k.dma(g.dvv[t0:t0 + 128, :], dv[:], R=[dv], q="pool")
            ps = g.PS.next()
            for c in range(8):
                k.mm(ps[:, 0:12], xT[:, c, t * 128:(t + 1) * 128], W[:, c, 3072:3084], c == 0, c == 7, [W, xT], [ps])
            sm = smr.next()
            k.act(sm[:, 0:6], ps[:, 0:6], AF.Exp, [ps], [sm], scale=-1.0)
            k.dve("tensor_scalar_add", sm[:, 0:6], sm[:, 0:6], 1.0, R=[sm], W=[sm])
            k.dve("reciprocal", be[:, t, :], sm[:, 0:6], R=[sm], W=[be])
            k.dve("tensor_tensor", sm[:, 6:12], ps[:, 6:12], dtb[:], ADD, R=[ps, dtb], W=[sm])
            k.act(sm[:, 6:12], sm[:, 6:12], AF.Exp, [sm], [sm])
            k.act(sm[:, 6:12], sm[:, 6:12], AF.Ln, [sm], [sm], bias=1.0)
            k.dve("tensor_tensor", gg[:, t, :], sm[:, 6:12], nea[:], MUL, R=[sm, nea], W=[gg])
        k.dma(g.gg[s0:s0 + 512, :].rearrange("(t p) h -> p t h", p=128), gg[:], R=[gg], q="pool")
        k.dma(g.gbeta[s0:s0 + 512, :].rearrange("(t p) h -> p t h", p=128), be[:], R=[be], q="pool")
        for cc in range(12):
            c0 = 3084 + cc * 128
            ps = g.PS.next()
            for c in range(8):
                k.mm(ps[:, :], W[:, c, c0:c0 + 128], xT[:, c, :], c == 0, c == 7, [W, xT], [ps])
            qn = qnr.next()
            if cc % 2:
                k.op("act", "copy", qn[:], ps[:, :], R=[ps], W=[qn])
            else:
                k.dve("tensor_copy", qn[:], ps[:, :], R=[ps], W=[qn])
            dst = g.dq if cc < 6 else g.dk
            hh = cc % 6
            k.dma(dst[hh * 128:(hh + 1) * 128, s0:s0 + 512], qn[:], R=[qn], q="pool")
    k.release(m)


def phase_gdn(g):
    k, S, nc = g.k, g.S, g.nc
    m = k.mark()
    H = 6
    normw = k.sb("gnw", [128, 128])
    k.dma(normw[:], g.I["gdn_norm_w"][0, :].partition_broadcast(128), W=[normw])
    St = k.sb("gS", [128, H, 128], F32)
    Sb = k.sb("gSb", [128, H, 128], BF16)
    k.dve("memset", St[:], 0.0, W=[St])
    k.dve("memset", Sb[:], 0.0, W=[Sb])
    qTr = k.rot("gqT", [128, H, 128], BF16, 2)
    kTr = k.rot("gkT", [128, H, 128], BF16, 2)
    ktr = k.rot("gkt", [128, 768], BF16, 2)
    vtr = k.rot("gvt", [128, 768], BF16, 2)
    zsr = k.rot("gzs", [128, 768], F32, 2)
    ggr = k.rot("ggg", [128, 6], F32, 2)
    ber = k.rot("gbe", [128, 6], F32, 2)
    smr = k.rot("gsm", [128, 32], F32, 2)
    oabr = k.rot("goab", [128, 768], BF16, 2)
    oTr = k.rot("goT", [128, H, 128], BF16, 2)

    def per_head(name, dt=F32, n=1):
        return [[k.sb("%s%d_%d" % (name, i, h), [128, 128], dt) for h in range(H)] for i in range(n)]

    P = per_head("gP", F32, 2)
    PT = per_head("gPT", F32, 2)
    Tt = per_head("gTt", F32, 2)
    Lg = per_head("gLg")[0]
    GT = per_head("gGT")[0]
    GTs = per_head("gGTs")[0]
    Ttb = per_head("gTtb", BF16)[0]
    Ut = per_head("gUt")[0]
    Ke = per_head("gKe", BF16)[0]
    WtT = per_head("gWtT", BF16)[0]
    tmp = per_head("gtmp")[0]
    dlb = per_head("gdlb", BF16)[0]
    asb = per_head("gasb")[0]
    QKG = per_head("gQKG", BF16)[0]
    ob = per_head("gob")[0]
    Kd = per_head("gKd", BF16)[0]
    st8 = [k.sb("gst8_%d" % h, [128, 12]) for h in range(H)]
    gqTv = g.gqT.rearrange("(h d) s -> d h s", d=128)
    gkTv = g.gkT.rearrange("(h d) s -> d h s", d=128)
    oTv = g.oT[0:768, :].rearrange("(h d) s -> d h s", d=128)
    HS = lambda h: slice(h * 128, (h + 1) * 128)
    for n in range(S // 128):
        t0 = n * 128
        qT, kT, kt, vt, zs, gg, be = qTr.next(), kTr.next(), ktr.next(), vtr.next(), zsr.next(), ggr.next(), ber.next()
        k.dma(qT[:], gqTv[:, :, t0:t0 + 128], W=[qT])
        k.dma(kT[:], gkTv[:, :, t0:t0 + 128], W=[kT])
        k.dma(kt[:], g.gk[t0:t0 + 128, :], W=[kt])
        k.dma(vt[:], g.gv[t0:t0 + 128, :], W=[vt])
        k.dma(zs[:], g.gz[t0:t0 + 128, :], W=[zs])
        k.dma(gg[:], g.gg[t0:t0 + 128, :], W=[gg])
        k.dma(be[:], g.gbeta[t0:t0 + 128, :], W=[be])
        sm = smr.next()
        ps = g.PS.next()
        k.mm(ps[:, 0:6], g.ltri[:], gg[:], True, True, [g.ltri, gg], [ps])
        k.mm(ps[:, 8:14], g.ones[:], gg[:], True, True, [g.ones, gg], [ps])
        k.act(sm[:, 0:6], ps[:, 0:6], AF.Exp, [ps], [sm])
        k.act(sm[:, 6:12], ps[:, 8:14], AF.Exp, [ps], [sm])
        k.dve("tensor_copy", sm[:, 18:24], ps[:, 0:6], R=[ps], W=[sm])
        k.dve("tensor_tensor", sm[:, 12:18], ps[:, 8:14], sm[:, 18:24], SUB, R=[ps, sm], W=[sm])
        k.act(sm[:, 12:18], sm[:, 12:18], AF.Exp, [sm], [sm])
        k.dve("tensor_scalar_mul", sm[:, 24:30], be[:], -1.0, R=[be], W=[sm])
        HR = range(H)
        psDs, psKs, psTs = {}, {}, {}
        for h in HR:
            k.pool("tensor_scalar_mul", Lg[h][:], g.ltri[:], gg[:, h:h + 1], R=[g.ltri, gg], W=[Lg[h]])
        for h in HR:
            psD = psDs[h] = g.PS.next()
            k.mm(psD[:, 0:128], g.ones[:], Lg[h][:], True, False, [g.ones, Lg[h]], [psD])
            k.mm(psD[:, 0:128], Lg[h][:], g.nones[:], False, True, [g.nones, Lg[h]], [psD])
            if h >= 2:
                hh = h - 2
                k.dve("tensor_tensor", GT[hh][:], psDs[hh][:, 0:128], g.neg[:], ADD, R=[psDs[hh], g.neg], W=[GT[hh]])
        for hh in (H - 2, H - 1):
            k.dve("tensor_tensor", GT[hh][:], psDs[hh][:, 0:128], g.neg[:], ADD, R=[psDs[hh], g.neg], W=[GT[hh]])
        for h in HR:
            k.act(GT[h][:], GT[h][:], AF.Exp, [GT[h]], [GT[h]])
        for h in HR:
            k.pool("tensor_tensor", GTs[h][:], GT[h][:], g.lstrict[:], MUL, R=[GT[h], g.lstrict], W=[GTs[h]])
        for h in HR:
            psK = g.PS.next()
            k.mm(psK[:, 0:128], kT[:, h, :], kT[:, h, :], True, True, [kT], [psK])
            k.dve("scalar_tensor_tensor", PT[0][h][:], psK[:, 0:128], sm[:, 24 + h:25 + h], GTs[h][:], MUL, MUL,
                  R=[psK, sm, GTs[h]], W=[PT[0][h]])
        for h in HR:
            k.pool("tensor_tensor", Tt[0][h][:], PT[0][h][:], g.ident[:], ADD, R=[PT[0][h], g.ident], W=[Tt[0][h]])
        for h in HR:
            psT = g.PS.next()
            k.transpose(psT[:, 0:128], PT[0][h][:], g.ident[:], [PT[0][h], g.ident], [psT])
            k.op("act", "copy", P[0][h][:], psT[:, 0:128], R=[psT], W=[P[0][h]])
        for s in range(1, 8):
            a, b = (s - 1) % 2, s % 2
            for h in HR:
                if s <= 6:
                    psP = g.PS.next()
                    k.mm(psP[:, 0:128], PT[a][h][:], P[a][h][:], True, True, [PT[a][h], P[a][h]], [psP])
                    k.op("act", "copy", P[b][h][:], psP[:, 0:128], R=[psP], W=[P[b][h]])
                if s < 6:
                    psPT = g.PS.next()
                    k.mm(psPT[:, 0:128], P[a][h][:], PT[a][h][:], True, True, [PT[a][h], P[a][h]], [psPT])
                    k.op("act", "copy", PT[b][h][:], psPT[:, 0:128], R=[psPT], W=[PT[b][h]])
                if s >= 2:
                    psTt = g.PS.next()
                    k.mm(psTt[:, 0:128], P[a][h][:], Tt[b][h][:], True, True, [P[a][h], Tt[b][h]], [psTt])
                    k.dve("tensor_tensor", Tt[a][h][:], Tt[b][h][:], psTt[:, 0:128], ADD, R=[Tt[b][h], psTt], W=[Tt[a][h]])
        TF = Tt[0]
        oab = oabr.next()
        for h in range(H):
            k.op("act", "copy", Ttb[h][:], TF[h][:], R=[TF[h]], W=[Ttb[h]])
            k.dve("tensor_scalar_mul", Ke[h][:], kt[:, HS(h)], sm[:, h:h + 1], R=[kt, sm], W=[Ke[h]])
            k.pool("tensor_scalar_mul", Kd[h][:], kt[:, HS(h)], sm[:, 12 + h:13 + h], R=[kt, sm], W=[Kd[h]])
        for h in range(H):
            psU = g.PS.next()
            k.mm(psU[:, 0:128], Ttb[h][:], vt[:, HS(h)], True, True, [Ttb[h], vt], [psU])
            k.op("act", "copy", Ut[h][:], psU[:, 0:128], R=[psU], W=[Ut[h]])
            psW = g.PS.next()
            k.mm(psW[:, 0:128], Ke[h][:], Ttb[h][:], True, True, [Ke[h], Ttb[h]], [psW])
            k.dve("tensor_copy", WtT[h][:], psW[:, 0:128], R=[psW], W=[WtT[h]])
        for h in range(H):
            psWS = g.PS.next()
            k.mm(psWS[:, 0:128], WtT[h][:], Sb[:, h, :], True, True, [WtT[h], Sb], [psWS])
            k.dve("tensor_tensor", tmp[h][:], Ut[h][:], psWS[:, 0:128], SUB, R=[Ut[h], psWS], W=[tmp[h]])
            k.act(dlb[h][:], tmp[h][:], AF.Identity, [tmp[h], be], [dlb[h]], scale=be[:, h:h + 1])
            psQS = g.PS.next()
            k.mm(psQS[:, 0:128], qT[:, h, :], Sb[:, h, :], True, True, [qT, Sb], [psQS])
            k.act(asb[h][:], psQS[:, 0:128], AF.Identity, [psQS, sm], [asb[h]], scale=sm[:, h:h + 1])
            psQK = g.PS.next()
            k.mm(psQK[:, 0:128], kT[:, h, :], qT[:, h, :], True, True, [kT, qT], [psQK])
            k.dve("tensor_tensor", QKG[h][:], psQK[:, 0:128], GT[h][:], MUL, R=[psQK, GT[h]], W=[QKG[h]])
        for h in range(H):
            psIN = g.PS.next()
            k.mm(psIN[:, 0:128], QKG[h][:], dlb[h][:], True, True, [QKG[h], dlb[h]], [psIN])
            k.dve("tensor_tensor", ob[h][:], psIN[:, 0:128], asb[h][:], ADD, R=[psIN, asb[h]], W=[ob[h]])
            psSU = g.PS.next()
            k.mm(psSU[:, 0:128], Kd[h][:], dlb[h][:], True, True, [Kd[h], dlb[h]], [psSU])
            k.dve("scalar_tensor_tensor", St[:, h, :], St[:, h, :], sm[:, 6 + h:7 + h], psSU[:, 0:128], MUL, ADD,
                  R=[St, sm, psSU, Sb], W=[St])
            k.op("act", "copy", Sb[:, h, :], St[:, h, :], R=[St], W=[Sb])
        for h in range(H):
            s8 = st8[h]
            k.dve("bn_stats", s8[:, 0:6], ob[h][:], R=[ob[h]], W=[s8])
            k.dve("bn_aggr", s8[:, 6:8], s8[:, 0:6], R=[s8], W=[s8])
            k.dve("scalar_tensor_tensor", s8[:, 8:9], s8[:, 6:7], s8[:, 6:7], s8[:, 7:8], MUL, ADD, R=[s8], W=[s8])
            k.act(s8[:, 9:10], s8[:, 8:9], AF.Ln, [s8], [s8], bias=RMS_EPS)
            k.act(s8[:, 9:10], s8[:, 9:10], AF.Exp, [s8], [s8], scale=-0.5)
            k.dve("scalar_tensor_tensor", tmp[h][:], ob[h][:], s8[:, 9:10], normw[:], MUL, MUL,
                  R=[ob[h], s8, normw], W=[tmp[h]])
            k.pool("tensor_tensor", oab[:, HS(h)], tmp[h][:], zs[:, HS(h)], MUL, R=[tmp[h], zs], W=[oab])
        pbk = g.PB.next()
        for h in range(H):
            k.transpose(pbk[:, HS(h)], oab[:, HS(h)], g.identb[:], [oab, g.identb], [pbk], inc=(h == H - 1))
        oT = oTr.next()
        k.dve("tensor_copy", oT[:], pbk[:, 0:768].rearrange("p (h t) -> p h t", h=H), R=[pbk], W=[oT])
        k.dma(oTv[:, :, t0:t0 + 128], oT[:], R=[oT], q="pool")
    k.release(m)


def phase_dil(g):
    k, S, nc = g.k, g.S, g.nc
    m = k.mark()
    U = min(2048, S)
    DIL = (1, 4, 16)
    ft = k.sb("dft", [128, 12, 256], BF16)
    for i in range(12):
        src = bass.AP(tensor=g.ftl_h, offset=i * 128 * 384 + 127, ap=[[383, 128], [1, 256]])
        k.dma(ft[:, i, :], src, W=[ft])
    Oacc = k.sb("dOacc", [128, 2, U], F32)
    Lacc = k.sb("dLacc", [128, 2, U], F32)
    ob = k.sb("dob", [128, 2, U], BF16)
    kbuf = [k.rot("dkT%d" % gi, [128, 2, 128 * d], BF16, 2) for gi, d in enumerate(DIL)]
    vbuf = [k.rot("dvt%d" % gi, [128, d, 256], BF16, 2) for gi, d in enumerate(DIL)]
    qbuf = [k.rot("dqT%d" % gi, [128, 4, 128 * d], BF16, 1) for gi, d in enumerate(DIL)]
    for qb_ in qbuf:
        k.dve("memset", qb_.bufs[0][:], 0.0, W=[qb_.bufs[0]])
    ptr = k.rot("dpt", [128, 128], BF16, 6)
    scb = Rot(g.PS.bufs[4:6])
    accb = Rot(g.PS.bufs[0:4])
    oTv = g.oT[768:1024, :].rearrange("(pr p) s -> p pr s", p=128)
    for u in range(S // U):
        u0 = u * U
        for gi, d in enumerate(DIL):
            SBT = 128 * d
            dqg = g.dq[gi * 256:(gi + 1) * 256, :]
            dkv = g.dk[gi * 256:(gi + 1) * 256, :].rearrange("(pr p) s -> p pr s", p=128)
            kprev = vprev = None
            if u0 > 0:
                kprev, vprev = kbuf[gi].next(), vbuf[gi].next()
                p0 = u0 - SBT
                k.dma(kprev[:], dkv[:, :, p0:p0 + SBT], W=[kprev])
                k.dma(vprev[:], g.dvv[p0:p0 + SBT, gi * 256:(gi + 1) * 256].rearrange("(m r) c -> m r c", r=d), W=[vprev])
            for sbi in range(U // SBT):
                s0 = u0 + sbi * SBT
                kc, vc, qc = kbuf[gi].next(), vbuf[gi].next(), qbuf[gi].next()
                k.dma(kc[:], dkv[:, :, s0:s0 + SBT], W=[kc])
                for h in range(4):
                    k.dma(qc[(h % 2) * 64:(h % 2 + 1) * 64, h, :], dqg[h * 64:(h + 1) * 64, s0:s0 + SBT], W=[qc])
                k.dma(vc[:], g.dvv[s0:s0 + SBT, gi * 256:(gi + 1) * 256].rearrange("(m r) c -> m r c", r=d), W=[vc])
                blocks = ([(kprev, vprev, 128)] if kprev is not None else []) + [(kc, vc, 0)]
                items = [(r, h, bi) for r in range(d) for h in range(4) for bi in range(len(blocks))]
                LA = 2
                pts = {}
                for i in range(len(items) + LA):
                    if i < len(items):
                        r, h, bi = items[i]
                        kk, vv, off = blocks[bi]
                        psS = scb.next()
                        k.mm(psS[:, 0:128], kk[:, h // 2, r:r + 127 * d + 1:d], qc[:, h, r:r + 127 * d + 1:d], True, True, [kk, qc], [psS])
                        pt = ptr.next()
                        k.act(pt[:], psS[:, 0:128], AF.Exp, [psS], [pt], scale=0.125)
                        k.dve("tensor_tensor", pt[:], pt[:], ft[:, gi * 4 + h, off:off + 128], MUL, R=[pt, ft], W=[pt])
                        pts[i] = pt
                    if i >= LA:
                        r, h, bi = items[i - LA]
                        kk, vv, off = blocks[bi]
                        pt = pts.pop(i - LA)
                        if bi == 0:
                            psO, psL = accb.next(), accb.next()
                        first, last = bi == 0, bi == len(blocks) - 1
                        pr, hp = h // 2, slice((h % 2) * 64, (h % 2 + 1) * 64)
                        k.mm(psO[:, 0:128], vv[:, r, pr * 128:(pr + 1) * 128], pt[:], first, last, [vv, pt], [psO])
                        k.mm(psL[:, 0:128], g.onesb[:], pt[:], first, last, [g.onesb, pt], [psL])
                        if last:
                            lo = sbi * SBT + r
                            osl = Oacc[hp, pr, lo:lo + 127 * d + 1:d]
                            lsl = Lacc[hp, pr, lo:lo + 127 * d + 1:d]
                            if gi == 0:
                                k.op("act", "copy", osl, psO[hp, 0:128], R=[psO], W=[Oacc])
                                k.dve("tensor_copy", lsl, psL[hp, 0:128], R=[psL], W=[Lacc])
                            else:
                                k.dve("tensor_tensor", osl, osl, psO[hp, 0:128], ADD, R=[psO, Oacc], W=[Oacc])
                                k.dve("tensor_tensor", lsl, lsl, psL[hp, 0:128], ADD, R=[psL, Lacc], W=[Lacc])
                kprev, vprev = kc, vc
        for pr in range(2):
            k.dve("reciprocal", Lacc[:, pr, :], Lacc[:, pr, :], R=[Lacc], W=[Lacc])
            k.dve("tensor_tensor", ob[:, pr, :], Oacc[:, pr, :], Lacc[:, pr, :], MUL, R=[Oacc, Lacc], W=[ob])
        k.dma(oTv[:, :, u0:u0 + U], ob[:], R=[ob], q="pool")
    k.release(m)


def phase_in_odd(g):
    k, S, nc = g.k, g.S, g.nc
    dsc = g.dsc
    g.cq = dsc("cq", [512, S], BF16)
    g.ck = dsc("ck", [512, S], BF16)
    g.cv = dsc("cv", [S, 512], BF16)
    g.lqT = dsc("lqT", [256, S], F32)
    g.lkT = dsc("lkT", [256, S], F32)
    g.lk = dsc("lk", [S, 256], F32)
    g.lv = dsc("lv", [S, 512], BF16)
    g.lr = dsc("lr", [S, 512], F32)
    g.la = dsc("la", [S, 256], F32)
    m = k.mark()
    W = k.sb("wino", [128, 8, 3088], BF16)
    m2 = k.mark()
    load_w_bf16(g, g.I["w_in_odd"], W, 8, 3088)
    k.release(m2)
    wg32 = k.sb("wg32", [128, 256])
    wg = k.sb("wg", [128, 256], BF16)
    k.dve("memset", wg32[:], 0.0, W=[wg32])
    k.dma(wg32[112:128, :], g.I["gla_w_gate"][:, :], W=[wg32])
    k.dve("tensor_copy", wg[:], wg32[:], R=[wg32], W=[wg])
    bgb = k.sb("bgb", [128, 256])
    k.dma(bgb[:], g.I["gla_b_gate"][0, :].partition_broadcast(128), W=[bgb])
    xin = k.rot("oxT", [128, 8, 512], BF16, 2)
    fbr = k.rot("ofb", [128, 512], BF16, 3)
    ffr = k.rot("off", [128, 512], F32, 3)
    tbr = k.rot("otb", [128, 512], BF16, 3)
    tfr = k.rot("otf", [128, 512], F32, 3)
    gdr = k.rot("ogd", [128, 512], BF16, 2)
    hTv = g.hT.rearrange("(c p) s -> p c s", p=128)
    xT_next = None
    for b in range(g.NB):
        s0 = b * 512
        if b == 0:
            xT = xin.next()
            k.dma(xT[:], hTv[:, :, 0:512], W=[xT])
        else:
            xT = xT_next
        if b + 1 < g.NB:
            xT_next = xin.next()
            k.dma(xT_next[:], hTv[:, :, s0 + 512:s0 + 1024], W=[xT_next])

        def fm(c0, ncol):
            ps = g.PS.next()
            for c in range(8):
                k.mm(ps[0:ncol, :], W[:, c, c0:c0 + ncol], xT[:, c, :], c == 0, c == 7, [W, xT], [ps])
            return ps

        def tm(t, c0, ncol):
            ps = g.PS.next()
            for c in range(8):
                k.mm(ps[:, 0:ncol], xT[:, c, t * 128:(t + 1) * 128], W[:, c, c0:c0 + ncol], c == 0, c == 7, [W, xT], [ps])
            return ps

        for cc in range(8):
            ps = fm(cc * 128, 128)
            fb = fbr.next()
            if cc % 2:
                k.op("act", "copy", fb[:], ps[:, :], R=[ps], W=[fb])
            else:
                k.dve("tensor_copy", fb[:], ps[:, :], R=[ps], W=[fb])
            dst = g.cq if cc < 4 else g.ck
            k.dma(dst[(cc % 4) * 128:(cc % 4 + 1) * 128, s0:s0 + 512], fb[:], R=[fb], q="pool")
        for cc in range(4):
            ps = fm(1536 + cc * 128, 128)
            ff = ffr.next()
            k.op("act", "copy", ff[:], ps[:, :], R=[ps], W=[ff])
            dst = g.lqT if cc < 2 else g.lkT
            k.dma(dst[(cc % 2) * 128:(cc % 2 + 1) * 128, s0:s0 + 512], ff[:], R=[ff], q="pool")
        ps = fm(2960, 128)
        gd = gdr.next()
        k.dve("tensor_copy", gd[:], ps[:, :], R=[ps], W=[gd])
        for t in range(4):
            t0 = s0 + t * 128
            ps = tm(t, 1024, 512)
            tb = tbr.next()
            k.dve("tensor_copy", tb[:], ps[:, :], R=[ps], W=[tb])
            k.dma(g.cv[t0:t0 + 128, :], tb[:], R=[tb], q="pool")
            ps = tm(t, 2048, 512)
            tb = tbr.next()
            k.op("act", "copy", tb[:], ps[:, :], R=[ps], W=[tb])
            k.dma(g.lv[t0:t0 + 128, :], tb[:], R=[tb], q="pool")
            ps = tm(t, 2560, 512)
            tf = tfr.next()
            k.act(tf[:], ps[:, :], AF.Silu, [ps], [tf])
            k.dma(g.lr[t0:t0 + 128, :], tf[:], R=[tf], q="pool")
            ps = tm(t, 1792, 256)
            tf = tfr.next()
            k.dve("tensor_copy", tf[:, 0:256], ps[:, 0:256], R=[ps], W=[tf])
            k.dma(g.lk[t0:t0 + 128, :], tf[:, 0:256], R=[tf], q="pool")
            ps = g.PS.next()
            k.mm(ps[:, 0:256], gd[:, t * 128:(t + 1) * 128], wg[:], True, True, [gd, wg], [ps])
            tf = tfr.next()
            k.dve("tensor_tensor", tf[:, 0:256], ps[:, 0:256], bgb[:], ADD, R=[ps, bgb], W=[tf])
            k.act(tf[:, 0:256], tf[:, 0:256], AF.Exp, [tf], [tf], scale=-1.0)
            k.act(tf[:, 0:256], tf[:, 0:256], AF.Ln, [tf], [tf], bias=1.0)
            k.dve("tensor_scalar_mul", tf[:, 0:256], tf[:, 0:256], -1.0 / 16.0, R=[tf], W=[tf])
            k.dma(g.la[t0:t0 + 128, :], tf[:, 0:256], R=[tf], q="pool")
    k.release(m)


def phase_diff(g):
    import math
    k, S, nc = g.k, g.S, g.nc
    m = k.mark()
    lam_init = 0.8 - 0.6 * math.exp(-0.3 * 1)
    lt = k.sb("clt", [128, 256])
    k.dma(lt[:], g.I["diff_lambda"].rearrange("a b -> (a b)").partition_broadcast(128), W=[lt])
    ls = k.sb("cls", [128, 8])
    pr = k.sb("cpr", [128, 128])
    k.dve("tensor_tensor", pr[:, 0:64], lt[:, 0:64], lt[:, 64:128], MUL, R=[lt], W=[pr])
    k.dve("tensor_tensor", pr[:, 64:128], lt[:, 128:192], lt[:, 192:256], MUL, R=[lt], W=[pr])
    k.dve("reduce_sum", ls[:, 0:1], pr[:, 0:64], AX.X, R=[pr], W=[ls])
    k.dve("reduce_sum", ls[:, 1:2], pr[:, 64:128], AX.X, R=[pr], W=[ls])
    k.act(ls[:, 0:2], ls[:, 0:2], AF.Exp, [ls], [ls])
    k.dve("tensor_tensor", ls[:, 2:3], ls[:, 1:2], ls[:, 0:1], SUB, R=[ls], W=[ls])
    k.dve("tensor_scalar_add", ls[:, 2:3], ls[:, 2:3], -lam_init, R=[ls], W=[ls])
    cs = k.sb("ccs", [128, 1])
    k.dma(cs[:], g.I["diff_norm_w"][:, :], W=[cs])
    k.dve("tensor_scalar_mul", cs[:], cs[:], 1.0 - lam_init, R=[cs], W=[cs])
    kT = k.sb("ckT", [128, S], BF16)
    V = k.sb("cV", [128, S // 128, 128], BF16)
    F = k.sb("cF", [128, 2432], BF16)
    qr = k.rot("cqT", [128, 2, 512], BF16, 2)
    for qb_ in qr.bufs:
        k.dve("memset", qb_[:], 0.0, W=[qb_])
    ptr = k.rot("cpt", [128, 512], BF16, 4)
    wr = k.rot("cw", [128, 512], F32, 4)
    sqr = k.rot("csq", [128, 512], BF16, 2)
    obr = k.rot("cob", [128, 512], BF16, 2)
    acc = g.PS.bufs[0:4]
    scb = Rot(g.PS.bufs[4:6])
    paccr = [k.rot("cpacc%d" % c, [128, 512], F32, 2) for c in range(2)]
    for h in range(4):
        k.dma(kT[:], g.ck[h * 128:(h + 1) * 128, :], W=[kT])
        k.dma(V[:], g.cv[:, h * 128:(h + 1) * 128].rearrange("(kb p) d -> p kb d", p=128), W=[V])
        k.dma(F[:], bass.AP(tensor=g.ftd_h, offset=h * 128 * 2560 + 127, ap=[[2559, 128], [1, 2432]]), W=[F])
        for Q in range(S // 512):
            qT = qr.next()
            for c in range(2):
                k.dma(qT[c * 64:(c + 1) * 64, c, :], g.cq[h * 128 + c * 64:h * 128 + (c + 1) * 64, Q * 512:(Q + 1) * 512], W=[qT])
            nkb = 4 * Q + 4
            pacc = [paccr[0].next(), paccr[1].next()]
            items = [(kb, c) for kb in range(nkb) for c in range(2)]
            LA = 2
            pts = {}
            for i in range(len(items) + LA):
                if i < len(items):
                    kb, c = items[i]
                    dl = Q * 512 - kb * 128
                    psS = scb.next()
                    k.mm(psS[:, :], kT[:, kb * 128:(kb + 1) * 128], qT[:, c, :], True, True, [kT, qT], [psS])
                    pt = ptr.next()
                    k.act(pt[:], psS[:, :], AF.Exp, [psS], [pt], scale=0.125)
                    if dl <= 1536:
                        k.dve("tensor_tensor", pt[:], pt[:], F[:, dl + 384:dl + 384 + 512], MUL, R=[pt, F], W=[pt])
                    pts[i] = pt
                if i >= LA:
                    kb, c = items[i - LA]
                    pt = pts.pop(i - LA)
                    k.mm(acc[c][:, :], V[:, kb, :], pt[:], kb == 0, kb == nkb - 1, [V, pt], [acc[c]])
                    eng = "dve" if c == 0 else "pool"
                    if kb == 0:
                        k.op(eng, "tensor_copy", pacc[c][:], pt[:], R=[pt], W=[pacc[c]])
                    else:
                        k.op(eng, "tensor_tensor", pacc[c][:], pacc[c][:], pt[:], ADD, R=[pt, pacc[c]], W=[pacc[c]])
            r0, r1, o0, o1 = wr.next(), wr.next(), wr.next(), wr.next()
            for c in range(2):
                k.mm(acc[2 + c][:, :], g.ones[:], pacc[c][:], True, True, [g.ones, pacc[c]], [acc[2 + c]])
            k.dve("reciprocal", r0[:], acc[2][:, :], R=[acc[2]], W=[r0])
            k.dve("reciprocal", r1[:], acc[3][:, :], R=[acc[3]], W=[r1])
            k.dve("tensor_tensor", o0[:], acc[0][:, :], r0[:], MUL, R=[acc[0], r0], W=[o0])
            k.dve("tensor_tensor", o1[:], acc[1][:, :], r1[:], MUL, R=[acc[1], r1], W=[o1])
            k.dve("scalar_tensor_tensor", o0[:], o1[:], ls[:, 2:3], o0[:], MUL, ADD, R=[o1, ls, o0], W=[o0])
            sq = sqr.next()
            k.pool("tensor_tensor", sq[:], o0[:], o0[:], MUL, R=[o0], W=[sq])
            psq = scb.next()
            k.mm(psq[:, :], g.onesb[:], sq[:], True, True, [g.onesb, sq], [psq])
            k.act(r0[:], psq[:, :], AF.Ln, [psq], [r0], scale=1.0 / 128.0, bias=RMS_EPS)
            k.act(r0[:], r0[:], AF.Exp, [r0], [r0], scale=-0.5)
            ob = obr.next()
            k.dve("scalar_tensor_tensor", ob[:], o0[:], cs[:, 0:1], r0[:], MUL, MUL, R=[o0, cs, r0], W=[ob])
            k.dma(g.oT[h * 128:(h + 1) * 128, Q * 512:(Q + 1) * 512], ob[:], R=[ob], q="pool")
    k.release(m)


def phase_gla(g):
    k, S, nc = g.k, g.S, g.nc
    m = k.mark()
    H, NP = 4, 2
    normw = k.sb("lnw", [128, 128])
    k.dma(normw[:], g.I["gla_norm_w"][0, :].partition_broadcast(128), W=[normw])
    St = [k.sb("lS%d" % p, [128, 2, 128], F32) for p in range(NP)]
    Sb = [k.sb("lSb%d" % p, [128, 128], BF16) for p in range(NP)]
    for p in range(NP):
        k.dve("memset", St[p][:], 0.0, W=[St[p]])
        k.dve("memset", Sb[p][:], 0.0, W=[Sb[p]])
    qTr = k.rot("lqT", [128, NP, 128], F32, 2)
    kTr = k.rot("lkT", [128, NP, 128], F32, 2)
    ktr = k.rot("lkt", [128, 256], F32, 2)
    lar = k.rot("lla", [128, 256], F32, 2)
    vtr = k.rot("lvt", [128, 512], BF16, 2)
    rsr = k.rot("lrs", [128, 512], F32, 2)
    oabr = k.rot("loab", [128, 512], BF16, 2)
    oTr = k.rot("loT", [128, H, 128], BF16, 2)

    def pp(name, shape, dt=F32, n=NP):
        return [k.sb("%s_%d" % (name, i), shape, dt) for i in range(n)]

    ebT = pp("lebT", [128, 128])
    enbT = pp("lenbT", [128, 128])
    kpT = pp("lkpT", [128, 128], BF16)
    enb = pp("lenb", [128, 128])
    kp = pp("lkp", [128, 128], BF16)
    ebl = pp("lebl", [128, 1])
    qz = pp("lqz", [128, 128], BF16, H)
    for h in range(H):
        k.dve("memset", qz[h][:], 0.0, W=[qz[h]])
    att = pp("latt", [128, 128], BF16, H)
    ob = pp("lob", [128, 128], F32, H)
    tmp = pp("ltmp", [128, 128], F32, H)
    st8 = pp("lst8", [128, 12], F32, H)
    lqv = g.lqT.rearrange("(pr p) s -> p pr s", p=128)
    lkv = g.lkT.rearrange("(pr p) s -> p pr s", p=128)
    oTv = g.oT[512:1024, :].rearrange("(h d) s -> d h s", d=128)
    PK = lambda p: slice(p * 128, (p + 1) * 128)
    HV = lambda h: slice(h * 128, (h + 1) * 128)
    for n in range(S // 128):
        t0 = n * 128
        qT, kT, kt, la, vt, rs = qTr.next(), kTr.next(), ktr.next(), lar.next(), vtr.next(), rsr.next()
        k.dma(qT[:], lqv[:, :, t0:t0 + 128], W=[qT])
        k.dma(kT[:], lkv[:, :, t0:t0 + 128], W=[kT])
        k.dma(kt[:], g.lk[t0:t0 + 128, :], W=[kt])
        k.dma(la[:], g.la[t0:t0 + 128, :], W=[la])
        k.dma(vt[:], g.lv[t0:t0 + 128, :], W=[vt])
        k.dma(rs[:], g.lr[t0:t0 + 128, :], W=[rs])
        oab = oabr.next()
        for p in range(NP):
            psB = g.PS.next()
            k.mm(psB[:, 0:128], la[:, PK(p)], g.ltri[:], True, True, [la, g.ltri], [psB])
            k.act(ebT[p][:], psB[:, 0:128], AF.Exp, [psB], [ebT[p]])
            k.act(enbT[p][:], psB[:, 0:128], AF.Exp, [psB], [enbT[p]], scale=-1.0)
            k.dve("tensor_copy", ebl[p][:], ebT[p][:, 127:128], R=[ebT[p]], W=[ebl[p]])
            psb = g.PS.next()
            k.mm(psb[:, 0:128], g.ltri[:], la[:, PK(p)], True, True, [la, g.ltri], [psb])
            k.act(enb[p][:], psb[:, 0:128], AF.Exp, [psb], [enb[p]], scale=-1.0)
            k.pool("tensor_tensor", kpT[p][:], kT[:, p, :], enbT[p][:], MUL, R=[kT, enbT[p]], W=[kpT[p]])
            k.pool("tensor_tensor", kp[p][:], kt[:, PK(p)], enb[p][:], MUL, R=[kt, enb[p]], W=[kp[p]])
            for j in range(2):
                h, hp = 2 * p + j, slice(j * 64, (j + 1) * 64)
                k.dve("scalar_tensor_tensor", qz[h][hp, :], qT[hp, p, :], 0.125, ebT[p][hp, :], MUL, MUL,
                      R=[qT, ebT[p]], W=[qz[h]])
        for h in range(H):
            p, j = h // 2, h % 2
            hp = slice(j * 64, (j + 1) * 64)
            psA = g.PS.next()
            k.mm(psA[:, 0:128], kpT[p][:], qz[h][:], True, True, [kpT[p], qz[h]], [psA])
            k.dve("tensor_tensor", att[h][:], psA[:, 0:128], g.ltri[:], MUL, R=[psA, g.ltri], W=[att[h]])
            psO = g.PS.next()
            k.mm(psO[:, 0:128], qz[h][:], Sb[p][:], True, False, [qz[h], Sb[p]], [psO])
            k.mm(psO[:, 0:128], att[h][:], vt[:, HV(h)], False, True, [att[h], vt], [psO])
            k.op("act", "copy", ob[h][:], psO[:, 0:128], R=[psO], W=[ob[h]])
            psS = g.PS.next()
            k.mm(psS[:, 0:128], kp[p][:], vt[:, HV(h)], True, True, [kp[p], vt], [psS])
            k.dve("tensor_tensor", St[p][hp, j, :], St[p][hp, j, :], psS[hp, 0:128], ADD, R=[St[p], psS, Sb[p]], W=[St[p]])
            k.dve("tensor_scalar_mul", St[p][hp, j, :], St[p][hp, j, :], ebl[p][hp, 0:1], R=[St[p], ebl[p]], W=[St[p]])
            k.op("act", "copy", Sb[p][hp, :], St[p][hp, j, :], R=[St[p]], W=[Sb[p]])
        for h in range(H):
            s8 = st8[h]
            k.dve("bn_stats", s8[:, 0:6], ob[h][:], R=[ob[h]], W=[s8])
            k.dve("bn_aggr", s8[:, 6:8], s8[:, 0:6], R=[s8], W=[s8])
            k.dve("scalar_tensor_tensor", s8[:, 8:9], s8[:, 6:7], s8[:, 6:7], s8[:, 7:8], MUL, ADD, R=[s8], W=[s8])
            k.act(s8[:, 9:10], s8[:, 8:9], AF.Ln, [s8], [s8], bias=RMS_EPS)
            k.act(s8[:, 9:10], s8[:, 9:10], AF.Exp, [s8], [s8], scale=-0.5)
            k.dve("scalar_tensor_tensor", tmp[h][:], ob[h][:], s8[:, 9:10], normw[:], MUL, MUL,
                  R=[ob[h], s8, normw], W=[tmp[h]])
            k.pool("tensor_tensor", oab[:, HV(h)], tmp[h][:], rs[:, HV(h)], MUL, R=[tmp[h], rs], W=[oab])
        pbk = g.PB.next()
        for h in range(H):
            k.transpose(pbk[:, HV(h)], oab[:, HV(h)], g.identb[:], [oab, g.identb], [pbk], inc=(h == H - 1))
        oT = oTr.next()
        k.dve("tensor_copy", oT[:], pbk[:, 0:512].rearrange("p (h t) -> p h t", h=H), R=[pbk], W=[oT])
        k.dma(oTv[:, :, t0:t0 + 128], oT[:], R=[oT], q="pool")
    k.release(m)


def _rel_bucket(dist):
    import math
    dist = np.asarray(dist)
    d = np.maximum(dist, 1).astype(np.float32)
    large = 16 + (np.log(d / np.float32(16)) / np.float32(math.log(128.0)) * np.float32(16)).astype(np.int32)
    large = np.minimum(large, 31)
    return np.where(dist < 16, dist, large)


def _consts():
    idx = np.arange(128)
    c = {}
    c["c_ident"] = np.eye(128, dtype=np.float32)
    c["c_ltri"] = (idx[:, None] <= idx[None, :]).astype(np.float32)
    c["c_lstrict"] = (idx[:, None] < idx[None, :]).astype(np.float32)
    c["c_neg"] = np.where(idx[:, None] > idx[None, :], -30000.0, 0.0).astype(np.float32)
    oh = np.zeros((33, 2560), np.float32)
    d = np.arange(2560) - 511
    bk = _rel_bucket(np.maximum(d, 0))
    oh[bk, np.arange(2560)] = 1.0
    oh[31, :] -= 1.0
    oh[:32, d < 0] = 0.0
    oh[32, d < 0] = -30000.0
    c["c_ohdiff"] = oh
    ohl = np.zeros((3, 33, 384), np.float32)
    for gi, dil in enumerate((1, 4, 16)):
        rel = np.arange(384) - 127
        ok = (rel >= 0) & (rel <= 128)
        bk = _rel_bucket(np.maximum(rel, 0) * dil)
        ohl[gi, bk, np.arange(384)] = 1.0
        ohl[gi, :32, ~ok] = 0.0
        ohl[gi, 32, ~ok] = -30000.0
    c["c_ohdil"] = ohl
    return c


_CACHE = {}


def kernel(**inputs):
    S = 8192
    B = 4
    if "nc" not in _CACHE:
        _CACHE["nc"] = build(S)[0]
    nc = _CACHE["nc"]
    consts = _consts()
    f = lambda a: np.ascontiguousarray(np.asarray(a, dtype=np.float32))
    shared = {
        "rel_bias": f(inputs["rel_bias"]),
        "w_in_even": f(inputs["w_in_even"][0]), "gdn_conv_w": f(inputs["gdn_conv_w"][0]),
        "gdn_a_log": f(inputs["gdn_a_log"]).reshape(1, 6), "gdn_dt_bias": f(inputs["gdn_dt_bias"]).reshape(1, 6),
        "gdn_norm_w": f(inputs["gdn_norm_w"]).reshape(1, 128), "w_out_even": f(inputs["w_out_even"][0]),
        "w_in_odd": f(inputs["w_in_odd"][0]), "diff_lambda": f(inputs["diff_lambda"][0]),
        "diff_norm_w": f(inputs["diff_norm_w"]).reshape(128, 1), "gla_w_gate": f(inputs["gla_w_gate"][0]),
        "gla_b_gate": f(inputs["gla_b_gate"]).reshape(1, 256), "gla_norm_w": f(inputs["gla_norm_w"]).reshape(1, 128),
        "w_out_odd": f(inputs["w_out_odd"][0]), "ffn_w_up": f(inputs["ffn_w_up"]),
        "ffn_conv_w": f(inputs["ffn_conv_w"]), "ffn_conv_b": f(inputs["ffn_conv_b"]).reshape(2, 1, 2 * DFF),
        "ffn_w_down": f(inputs["ffn_w_down"]), "ln_g": f(inputs["ln_g"]).reshape(4, D),
        "ln_b": f(inputs["ln_b"]).reshape(4, D),
    }
    shared.update(consts)
    x = f(inputs["x"])
    in_maps = []
    for c in range(8):
        mp = dict(shared)
        mp["x"] = x[c % B]
        in_maps.append(mp)
    res = run_bass_kernel_spmd(nc, in_maps, core_ids=list(range(8)))
    return np.stack([res.results[b]["y"] for b in range(B)], 0).astype(np.float32)
```

```python
import numpy as np
import concourse.bass as bass
import concourse.mybir as mybir
from concourse.bass_utils import run_bass_kernel_spmd

F32 = mybir.dt.float32
BF16 = mybir.dt.bfloat16
ALU = mybir.AluOpType
AF = mybir.ActivationFunctionType
AX = mybir.AxisListType

SEM_CAP = 30000


class Buf:
    __slots__ = ("name", "t", "w", "r", "excl")

    def __init__(self, name, t=None, excl=False):
        self.name = name
        self.t = t
        self.excl = excl
        self.w = None
        self.r = []

    def __getitem__(self, idx):
        return self.t[idx]


class Rot:
    def __init__(self, bufs):
        self.bufs = bufs
        self.i = 0

    def next(self):
        b = self.bufs[self.i]
        self.i = (self.i + 1) % len(self.bufs)
        return b


class Stream:
    __slots__ = ("sem", "cnt", "id")
    _next = 0

    def __init__(self, sem):
        self.sem = sem
        self.cnt = 0
        self.id = Stream._next
        Stream._next += 1


class KB:
    def __init__(self, nc, n_dma_sems=24):
        self.nc = nc
        self.eng = {"pe": nc.tensor, "act": nc.scalar, "dve": nc.vector,
                    "pool": nc.gpsimd, "sp": nc.sync}
        self._ctx = []
        self._sem_ctx = []
        self.stream = {}
        for e in self.eng:
            self.stream[e] = self._new_stream("s_" + e)
        self.seen = {e: {} for e in self.eng}
        self.dma_pools = {q: [self._new_stream("d%s%d" % (q, i)) for i in range(n)]
                          for q, n in (("sp", 12), ("act", 6), ("pool", 8))}
        self.dma_rr = {"sp": 0, "act": 0, "pool": 0}
        self.ninst = {e: 0 for e in self.eng}
        self.nwait = {e: 0 for e in self.eng}
        self.out_dma = []
        self.prog = {e: [] for e in self.eng}

    def _new_stream(self, name):
        cm = self.nc.semaphore(name + "_%d" % Stream._next)
        sem = cm.__enter__()
        self._sem_ctx.append(cm)
        return Stream(sem)

    def enter(self, cm):
        v = cm.__enter__()
        self._ctx.append(cm)
        return v

    def emit(self):
        with self.nc.Block() as block:
            def mk(e):
                def body(eng):
                    for f in self.prog[e]:
                        f(eng)
                return body
            block.tensor(mk("pe"))
            block.scalar(mk("act"))
            block.vector(mk("dve"))
            block.gpsimd(mk("pool"))
            block.sync(mk("sp"))

    def close(self):
        for cm in reversed(self._ctx):
            cm.__exit__(None, None, None)
        self._ctx = []
        for cm in reversed(self._sem_ctx):
            cm.__exit__(None, None, None)
        self._sem_ctx = []

    def mark(self):
        return len(self._ctx)

    def release(self, mark):
        self.barrier()
        while len(self._ctx) > mark:
            self._ctx.pop().__exit__(None, None, None)

    def rot(self, name, shape, dt, n):
        return Rot([self.sb("%s%d" % (name, i), shape, dt) for i in range(n)])

    def sb(self, name, shape, dt=F32):
        self._uid = getattr(self, "_uid", 0) + 1
        name = "%s_%d" % (name, self._uid)
        t = self.enter(self.nc.sbuf_tensor(name, list(shape), dt))
        return Buf(name, t)

    def ps(self, name, shape, dt=F32):
        t = self.enter(self.nc.psum_tensor(name, list(shape), dt))
        return Buf(name, t, excl=True)

    def _wait(self, e, dep):
        st, c = dep
        if self.seen[e].get(st.id, 0) >= c:
            return
        if st is self.stream[e] and (e == "pe" or c > st.cnt):
            return
        sem = st.sem
        self.prog[e].append(lambda eng, sem=sem, c=c: eng.wait_ge(sem, c))
        self.seen[e][st.id] = c
        self.nwait[e] += 1

    def _deps(self, e, reads, writes):
        for b in reads:
            if b.w is not None:
                self._wait(e, b.w)
        for b in writes:
            if b.w is not None:
                self._wait(e, b.w)
            for d in b.r:
                self._wait(e, d)

    def _record(self, tag, reads, writes):
        for b in reads:
            b.r.append(tag)
            if len(b.r) > 12:
                best = {}
                for st, c in b.r:
                    if st.id not in best or best[st.id][1] < c:
                        best[st.id] = (st, c)
                b.r = list(best.values())
        for b in writes:
            b.w = tag
            b.r = []

    def op(self, e, meth, *args, R=(), W=(), inc=True, **kw):
        if any(b.excl for b in R):
            W = list(W) + [b for b in R if b.excl and b not in W]
            R = [b for b in R if not b.excl]
        self._deps(e, R, W)
        st = self.stream[e]
        if st.cnt >= SEM_CAP and inc:
            st = self.stream[e] = self._new_stream("s_" + e)
        self.ninst[e] += 1
        if inc:
            st.cnt += 1
            sem = st.sem
            self.prog[e].append(lambda eng: getattr(eng, meth)(*args, **kw).then_inc(sem, 1))
            tag = (st, st.cnt)
        else:
            self.prog[e].append(lambda eng: getattr(eng, meth)(*args, **kw))
            tag = (st, st.cnt + 1)
        self._record(tag, R, W)

    def dma(self, out, in_, R=(), W=(), q="sp", is_output=False, **kw):
        self._deps(q, R, W)
        pool = self.dma_pools[q]
        idx = self.dma_rr[q]
        st = pool[idx]
        self.dma_rr[q] = (idx + 1) % len(pool)
        if st.cnt > 0:
            self._wait(q, (st, st.cnt))
        if st.cnt >= SEM_CAP:
            st = pool[idx] = self._new_stream("d" + q)
        st.cnt += 16
        sem = st.sem
        self.prog[q].append(lambda eng: eng.dma_start(out=out, in_=in_, **kw).then_inc(sem, 16))
        self.ninst[q] += 1
        tag = (st, st.cnt)
        self._record(tag, R, W)
        if is_output:
            self.out_dma.append(tag)

    def coll(self, kind, op, groups, ins, outs, R=(), W=()):
        q = "pool"
        self._deps(q, R, W)
        pool = self.dma_pools[q]
        idx = self.dma_rr[q]
        st = pool[idx]
        self.dma_rr[q] = (idx + 1) % len(pool)
        if st.cnt > 0:
            self._wait(q, (st, st.cnt))
        st.cnt += 16
        sem = st.sem
        self.prog[q].append(lambda eng: eng.collective_compute(kind, op, groups, ins, outs).then_inc(sem, 16))
        self.ninst[q] += 1
        self._record((st, st.cnt), R, W)

    def all_dma_streams(self):
        return [st for p in self.dma_pools.values() for st in p]

    def finish(self):
        for tag in self.out_dma:
            self._wait("sp", tag)
        for e in self.eng:
            st = self.stream[e]
            if st.cnt > 0:
                self._wait("sp", (st, st.cnt))
        for st in self.all_dma_streams():
            if st.cnt > 0:
                self._wait("sp", (st, st.cnt))

    def mm(self, out, lhsT, rhs, start, stop, R, W, inc=None):
        if inc is None:
            inc = stop
        self.op("pe", "matmul", out, lhsT, rhs, start=start, stop=stop, R=R, W=W, inc=inc)

    def transpose(self, out, in_, ident, R, W, inc=True):
        self.op("pe", "transpose", out, in_, ident, R=R, W=W, inc=inc)

    def act(self, out, in_, func, R, W, **kw):
        self.op("act", "activation", out, in_, func, R=R, W=W, **kw)

    def dve(self, meth, *args, R=(), W=(), **kw):
        self.op("dve", meth, *args, R=R, W=W, **kw)

    def pool(self, meth, *args, R=(), W=(), **kw):
        self.op("pool", meth, *args, R=R, W=W, **kw)

    def barrier(self):
        tags = []
        for e in self.eng:
            st = self.stream[e]
            if st.cnt > 0:
                tags.append((st, st.cnt))
        for st in self.all_dma_streams():
            if st.cnt > 0:
                tags.append((st, st.cnt))
        for e in self.eng:
            for t in tags:
                self._wait(e, t)


D = 1024
DFF = 2816
ALPHA = float(4 ** 0.25)
LN_EPS = 1e-5
RMS_EPS = 1e-6
MUL = ALU.mult
ADD = ALU.add
SUB = ALU.subtract
POW = ALU.pow


class G:
    pass


def build(S, dbg=False, phases=None):
    nc = bass.Bass("TRN2", target_bir_lowering=False)
    k = KB(nc)
    g = G()
    g.nc, g.k, g.S, g.NB = nc, k, S, S // 512
    skind = "ExternalOutput" if dbg else "Internal"

    def din(name, shape, dt=F32):
        return nc.dram_tensor(name, list(shape), dt, kind="ExternalInput").ap()

    def dsc(name, shape, dt=F32):
        return nc.dram_tensor(name, list(shape), dt, kind=skind).ap()

    g.dsc = dsc
    I = {}
    for name, shape in [
        ("x", [S, D]), ("rel_bias", [32, 4]), ("w_in_even", [D, 5388]), ("gdn_conv_w", [4, 2304]),
        ("gdn_a_log", [1, 6]), ("gdn_dt_bias", [1, 6]), ("gdn_norm_w", [1, 128]),
        ("w_out_even", [D, D]), ("w_in_odd", [D, 3088]), ("diff_lambda", [4, 64]),
        ("diff_norm_w", [128, 1]), ("gla_w_gate", [16, 256]), ("gla_b_gate", [1, 256]),
        ("gla_norm_w", [1, 128]), ("w_out_odd", [D, D]), ("ffn_w_up", [2, D, 2 * DFF]),
        ("ffn_conv_w", [2, 3, 2 * DFF]), ("ffn_conv_b", [2, 1, 2 * DFF]),
        ("ffn_w_down", [2, DFF, D]), ("ln_g", [4, D]), ("ln_b", [4, D]),
        ("c_ident", [128, 128]), ("c_ltri", [128, 128]), ("c_lstrict", [128, 128]),
        ("c_neg", [128, 128]), ("c_ohdiff", [33, 2560]), ("c_ohdil", [3, 33, 384]),
    ]:
        I[name] = din(name, shape)
    g.I = I
    g.y = nc.dram_tensor("y", [S, D], F32, kind="ExternalOutput").ap()
    g.hT = dsc("hT", [D, S], BF16)
    g.hA = dsc("hA", [S, D])
    g.hB = dsc("hB", [S, D])
    g.oT = dsc("oT", [D, S], BF16)
    g.ftd_h = nc.dram_tensor("ftd", [4, 128, 2560], BF16, kind=skind)
    g.ftl_h = nc.dram_tensor("ftl", [12, 128, 384], BF16, kind=skind)
    g.ftd, g.ftl = g.ftd_h.ap(), g.ftl_h.ap()

    g.PS = Rot([k.ps("psf%d" % i, [128, 512], F32) for i in range(6)])
    g.PB = Rot([k.ps("psb%d" % i, [128, 1024], BF16) for i in range(2)])
    g.ident = k.sb("ident", [128, 128])
    g.identb = k.sb("identb", [128, 128], BF16)
    g.ltri = k.sb("ltri", [128, 128])
    g.lstrict = k.sb("lstrict", [128, 128])
    g.neg = k.sb("neg", [128, 128])
    g.ones = k.sb("ones", [128, 128])
    g.nones = k.sb("nones", [128, 128])
    g.onesb = k.sb("onesb", [128, 128], BF16)
    k.dma(g.ident[:], I["c_ident"][:, :], W=[g.ident])
    k.dma(g.ltri[:], I["c_ltri"][:, :], W=[g.ltri])
    k.dma(g.lstrict[:], I["c_lstrict"][:, :], W=[g.lstrict])
    k.dma(g.neg[:], I["c_neg"][:, :], W=[g.neg])
    k.dve("tensor_copy", g.identb[:], g.ident[:], R=[g.ident], W=[g.identb])
    k.dve("memset", g.ones[:], 1.0, W=[g.ones])
    k.dve("memset", g.nones[:], -1.0, W=[g.nones])
    k.dve("memset", g.onesb[:], 1.0, W=[g.onesb])

    ph = phases
    def on(p):
        return ph is None or p in ph

    if on("tab"):
        phase_tables(g)
    if on("x0"):
        phase_x0(g)
    if on("ine"):
        phase_in_even(g)
    if on("gdn"):
        phase_gdn(g)
    if on("dil"):
        phase_dil(g)
    if on("oute"):
        phase_out(g, I["w_out_even"], I["x"], 0)
    if on("ffn0"):
        phase_ffn(g, 0, g.hB)
    if on("ino"):
        phase_in_odd(g)
    if on("diff"):
        phase_diff(g)
    if on("gla"):
        phase_gla(g)
    if on("outo"):
        phase_out(g, I["w_out_odd"], g.hB, 2)
    if on("ffn1"):
        phase_ffn(g, 1, g.y)
    k.finish()
    k.emit()
    k.close()
    return nc, k


def load_w_bf16(g, src2d, dst, nchunk, ncols, cs=None):
    k = g.k
    cs = cs or ncols
    stg = k.rot("wstg", [128, cs], F32, 2)
    i = 0
    for c in range(nchunk):
        for c0 in range(0, ncols, cs):
            w = min(cs, ncols - c0)
            st = stg.next()
            k.dma(st[:, 0:w], src2d[c * 128:(c + 1) * 128, c0:c0 + w], W=[st], q="act" if i % 2 else "sp")
            e = ("dve", "pool", "act")[i % 3]
            if e == "act":
                k.op("act", "copy", dst[:, c, c0:c0 + w], st[:, 0:w], R=[st], W=[dst])
            else:
                k.op(e, "tensor_copy", dst[:, c, c0:c0 + w], st[:, 0:w], R=[st], W=[dst])
            i += 1


def rows_to_cols(g, src_rows, nrow, ncol, dst):
    k = g.k
    m = k.mark()
    st = k.sb("r2c", [8, ncol])
    k.dma(st[0:nrow, :], src_rows, W=[st])
    for c in range(ncol // 128):
        ps = g.PS.next()
        k.transpose(ps[:, 0:nrow], st[0:nrow, c * 128:(c + 1) * 128], g.ident[0:nrow, 0:nrow], [st, g.ident], [ps])
        k.dve("tensor_copy", dst[:, c, 0:nrow], ps[:, 0:nrow], R=[ps], W=[dst])
    k.release(m)


def ln_block(g, pa_pb_fn, hin_src, lnrow, t0, hout_dst, xTn, tt, lnw, final_bufs):
    k = g.k
    gb, bb, hin_r, z_r, y_r, yb_r, sm_r = lnw
    hin = hin_r.next()
    k.dma(hin[:], hin_src[t0:t0 + 128, :], W=[hin])
    pa, pb = pa_pb_fn()
    z = z_r.next()
    k.dve("scalar_tensor_tensor", z[:, 0:512], hin[:, 0:512], ALPHA, pa[:, :], MUL, ADD, R=[hin, pa], W=[z])
    k.dve("scalar_tensor_tensor", z[:, 512:1024], hin[:, 512:1024], ALPHA, pb[:, :], MUL, ADD, R=[hin, pb], W=[z])
    sm = sm_r.next()
    k.dve("bn_stats", sm[:, 0:6], z[:, 0:512], R=[z], W=[sm])
    k.dve("bn_stats", sm[:, 6:12], z[:, 512:1024], R=[z], W=[sm])
    k.dve("bn_aggr", sm[:, 12:14], sm[:, 0:12], R=[sm], W=[sm])
    k.act(sm[:, 14:15], sm[:, 13:14], AF.Ln, [sm], [sm], bias=LN_EPS)
    k.act(sm[:, 14:15], sm[:, 14:15], AF.Exp, [sm], [sm], scale=-0.5)
    k.dve("scalar_tensor_tensor", sm[:, 15:16], sm[:, 12:13], -1.0, sm[:, 14:15], MUL, MUL, R=[sm], W=[sm])
    k.act(z[:], z[:], AF.Identity, [z, sm], [z], scale=sm[:, 14:15], bias=sm[:, 15:16])
    k.dve("tensor_tensor", z[:], z[:], gb[:], MUL, R=[z, gb], W=[z])
    yt = z
    k.pool("tensor_tensor", yt[:], z[:], bb[:], ADD, R=[z, bb], W=[yt])
    k.dma(hout_dst[t0:t0 + 128, :], yt[:], R=[yt], q="pool", is_output=final_bufs)
    if xTn is not None:
        yb = yb_r.next()
        k.op("act", "copy", yb[:], yt[:], R=[yt], W=[yb])
        pbk = g.PB.next()
        for c in range(8):
            k.transpose(pbk[:, c * 128:(c + 1) * 128], yb[:, c * 128:(c + 1) * 128], g.identb[:], [yb, g.identb], [pbk],
                        inc=(c == 7))
        k.dve("tensor_copy", xTn[:, :, tt * 128:(tt + 1) * 128], pbk[:, :].rearrange("p (c t) -> p c t", c=8),
              R=[pbk], W=[xTn])


def ln_stages(g, pa_pb_fn, hin_src, t0, hout_dst, xTn, tt, lnw, final_bufs):
    k = g.k
    gb, bb, hin_r, z_r, y_r, yb_r, sm_r = lnw
    st = {}

    def s0():
        hin = hin_r.next()
        k.dma(hin[:], hin_src[t0:t0 + 128, :], W=[hin])
        pa, pb = pa_pb_fn()
        z = st["z"] = z_r.next()
        sm = st["sm"] = sm_r.next()
        k.dve("scalar_tensor_tensor", z[:, 0:512], hin[:, 0:512], ALPHA, pa[:, :], MUL, ADD, R=[hin, pa], W=[z])
        k.dve("scalar_tensor_tensor", z[:, 512:1024], hin[:, 512:1024], ALPHA, pb[:, :], MUL, ADD, R=[hin, pb], W=[z])
        k.dve("bn_stats", sm[:, 0:6], z[:, 0:512], R=[z], W=[sm])
        k.dve("bn_stats", sm[:, 6:12], z[:, 512:1024], R=[z], W=[sm])
        k.dve("bn_aggr", sm[:, 12:14], sm[:, 0:12], R=[sm], W=[sm])

    def s1():
        sm = st["sm"]
        k.act(sm[:, 14:15], sm[:, 13:14], AF.Ln, [sm], [sm], bias=LN_EPS)
        k.act(sm[:, 14:15], sm[:, 14:15], AF.Exp, [sm], [sm], scale=-0.5)
        k.dve("scalar_tensor_tensor", sm[:, 15:16], sm[:, 12:13], -1.0, sm[:, 14:15], MUL, MUL, R=[sm], W=[sm])

    def s2():
        z, sm = st["z"], st["sm"]
        k.act(z[:], z[:], AF.Identity, [z, sm], [z], scale=sm[:, 14:15], bias=sm[:, 15:16])

    def s3():
        z = st["z"]
        k.dve("tensor_tensor", z[:], z[:], gb[:], MUL, R=[z, gb], W=[z])
        k.pool("tensor_tensor", z[:], z[:], bb[:], ADD, R=[z, bb], W=[z])
        k.dma(hout_dst[t0:t0 + 128, :], z[:], R=[z], q="pool", is_output=final_bufs)

    def s4():
        if xTn is None:
            return
        z = st["z"]
        yb = yb_r.next()
        k.op("act", "copy", yb[:], z[:], R=[z], W=[yb])
        pbk = g.PB.next()
        for c in range(8):
            k.transpose(pbk[:, c * 128:(c + 1) * 128], yb[:, c * 128:(c + 1) * 128], g.identb[:], [yb, g.identb], [pbk],
                        inc=(c == 7))
        k.dve("tensor_copy", xTn[:, :, tt * 128:(tt + 1) * 128], pbk[:, :].rearrange("p (c t) -> p c t", c=8),
              R=[pbk], W=[xTn])

    return [s0, s1, s2, s3, s4]


def ln_alloc(g, lnrow, depth=2):
    k = g.k
    gb = k.sb("lngb", [128, D])
    bb = k.sb("lnbb", [128, D])
    k.dma(gb[:], g.I["ln_g"][lnrow, :].partition_broadcast(128), W=[gb])
    k.dma(bb[:], g.I["ln_b"][lnrow, :].partition_broadcast(128), W=[bb])
    return (gb, bb, k.rot("lnhin", [128, D], F32, 2), k.rot("lnz", [128, D], F32, depth),
            None, k.rot("lnyb", [128, D], BF16, 1 if depth == 2 else 2), k.rot("lnsm", [128, 16], F32, depth))


def phase_x0(g):
    k, S = g.k, g.S
    m = k.mark()
    xin = k.rot("x0in", [128, D], F32, 2)
    xo = k.rot("x0o", [128, 8, 512], BF16, 2)
    for b in range(g.NB):
        xt = xo.next()
        for t in range(4):
            t0 = b * 512 + t * 128
            xi = xin.next()
            k.dma(xi[:], g.I["x"][t0:t0 + 128, :], W=[xi])
            for hf in range(2):
                ps = g.PS.next()
                for c in range(4):
                    cc = hf * 4 + c
                    k.transpose(ps[:, c * 128:(c + 1) * 128], xi[:, cc * 128:(cc + 1) * 128], g.ident[:],
                                [xi, g.ident], [ps], inc=(c == 3))
                k.op("act" if hf else "dve", "copy" if hf else "tensor_copy",
                     xt[:, hf * 4:(hf + 1) * 4, t * 128:(t + 1) * 128],
                     ps[:, :].rearrange("p (c t) -> p c t", c=4), R=[ps], W=[xt])
        k.dma(g.hT.rearrange("(c p) s -> p c s", p=128)[:, :, b * 512:(b + 1) * 512], xt[:], R=[xt], q="pool")
    k.release(m)


def phase_out(g, wout, hin_src, lnrow):
    k, S = g.k, g.S
    m = k.mark()
    W = k.sb("wout", [128, 8, D], BF16)
    m2 = k.mark()
    load_w_bf16(g, wout, W, 8, D)
    k.release(m2)
    lnw = ln_alloc(g, lnrow, depth=6)
    oin = k.rot("oin", [128, 8, 512], BF16, 2)
    xo = k.rot("xTn", [128, 8, 512], BF16, 3)
    oTv = g.oT.rearrange("(c p) s -> p c s", p=128)
    hTv = g.hT.rearrange("(c p) s -> p c s", p=128)
    NT = S // 128
    tiles = {}
    blk = {}
    NST = 5
    for i in range(NT + NST - 1):
        if i < NT:
            b, t = divmod(i, 4)
            if t == 0:
                ot = oin.next()
                k.dma(ot[:], oTv[:, :, b * 512:(b + 1) * 512], W=[ot])
                blk[b] = (ot, xo.next())
            ot, xTn = blk[b]

            def mk(t=t, ot=ot):
                pa, pb = g.PS.next(), g.PS.next()
                for hf, p in enumerate((pa, pb)):
                    for c in range(8):
                        k.mm(p[:, :], ot[:, c, t * 128:(t + 1) * 128], W[:, c, hf * 512:(hf + 1) * 512],
                             c == 0, c == 7, [ot, W], [p])
                return pa, pb
            tiles[i] = ln_stages(g, mk, hin_src, i * 128, g.hA, xTn, t, lnw, False)
        for s in range(NST):
            j = i - s
            if 0 <= j < NT:
                tiles[j][s]()
                if s == NST - 1:
                    del tiles[j]
                    bj, tj = divmod(j, 4)
                    if tj == 3:
                        k.dma(hTv[:, :, bj * 512:(bj + 1) * 512], blk[bj][1][:], R=[blk[bj][1]], q="pool")
                        del blk[bj]
    k.release(m)


def phase_ffn(g, layer, hout):
    k, S = g.k, g.S
    m = k.mark()
    Wu = k.sb("wup", [128, 8, 2 * DFF], BF16)
    Wd = k.sb("wdn", [128, 22, D], BF16)
    cw = k.sb("fcw", [128, 44, 4], F32)
    m2 = k.mark()
    load_w_bf16(g, g.I["ffn_w_up"][layer], Wu, 8, 2 * DFF, cs=2816)
    load_w_bf16(g, g.I["ffn_w_down"][layer], Wd, 22, D)
    k.release(m2)
    m2 = k.mark()
    st = k.sb("fcst", [8, 2 * DFF])
    k.dma(st[0:3, :], g.I["ffn_conv_w"][layer], W=[st])
    k.dma(st[3:4, :], g.I["ffn_conv_b"][layer], W=[st])
    for c in range(44):
        ps = g.PS.next()
        k.transpose(ps[:, 0:4], st[0:4, c * 128:(c + 1) * 128], g.ident[0:4, 0:4], [st, g.ident], [ps])
        k.dve("tensor_copy", cw[:, c, :], ps[:, 0:4], R=[ps], W=[cw])
    k.release(m2)
    final = hout is g.y
    lnw = ln_alloc(g, 2 * layer + 1)
    halo = k.sb("fhalo", [128, 44, 2], F32)
    k.dve("memset", halo[:], 0.0, W=[halo])
    FB = 256
    xin = k.rot("fxT", [128, 8, FB], BF16, 2)
    gT = k.sb("fgT", [128, 22, FB], BF16)
    ur = k.rot("fu", [128, FB + 2], F32, 7)
    yr = k.rot("fy", [128, FB], F32, 10)
    hTv = g.hT.rearrange("(c p) s -> p c s", p=128)
    xT_next = None
    for b in range(S // FB):
        if b == 0:
            xT = xin.next()
            k.dma(xT[:], hTv[:, :, 0:FB], W=[xT])
        else:
            xT = xT_next
        if b + 1 < S // FB:
            xT_next = xin.next()
            k.dma(xT_next[:], hTv[:, :, (b + 1) * FB:(b + 2) * FB], W=[xT_next])
        items = [(j, half) for j in range(22) for half in range(2)]
        us, yvs = {}, {}
        LAG = 2
        for i in range(len(items) + 6):
            if i < len(items):
                j, half = items[i]
                cc = j + 22 * half
                ps = g.PS.next()
                for c in range(8):
                    k.mm(ps[:, 0:FB], Wu[:, c, cc * 128:(cc + 1) * 128], xT[:, c, :], c == 0, c == 7, [Wu, xT], [ps])
                u = ur.next()
                k.pool("tensor_copy", u[:, 0:2], halo[:, cc, :], R=[halo], W=[u])
                k.op("act", "copy", u[:, 2:FB + 2], ps[:, 0:FB], R=[ps], W=[u])
                k.pool("tensor_copy", halo[:, cc, :], u[:, FB:FB + 2], R=[u], W=[halo])
                us[i] = u
            if i >= 2 and i - 2 < len(items):
                j, half = items[i - 2]
                cc = j + 22 * half
                u = us[i - 2]
                yv = yvs[(j, half)] = yr.next()
                if half == 0:
                    k.act(yv[:], u[:, 2:FB + 2], AF.Identity, [u, cw], [yv], scale=cw[:, cc, 2:3], bias=cw[:, cc, 3:4])
                else:
                    k.dve("tensor_scalar", yv[:], u[:, 2:FB + 2], cw[:, cc, 2:3], cw[:, cc, 3:4], MUL, ADD, R=[u, cw], W=[yv])
            if i >= 3 and i - 3 < len(items):
                j, half = items[i - 3]
                cc = j + 22 * half
                u, yv = us[i - 3], yvs[(j, half)]
                k.dve("scalar_tensor_tensor", yv[:], u[:, 1:FB + 1], cw[:, cc, 1:2], yv[:], MUL, ADD, R=[u, cw, yv], W=[yv])
            if i >= 4 and i - 4 < len(items):
                j, half = items[i - 4]
                cc = j + 22 * half
                u, yv = us.pop(i - 4), yvs[(j, half)]
                k.dve("scalar_tensor_tensor", yv[:], u[:, 0:FB], cw[:, cc, 0:1], yv[:], MUL, ADD, R=[u, cw, yv], W=[yv])
            if i >= 5 and i - 5 < len(items) and items[i - 5][1] == 1:
                j = items[i - 5][0]
                k.act(yvs[(j, 0)][:], yvs[(j, 0)][:], AF.Silu, [yvs[(j, 0)]], [yvs[(j, 0)]])
            if i >= 6 and i - 6 < len(items) and items[i - 6][1] == 1:
                j = items[i - 6][0]
                y0, y1 = yvs.pop((j, 0)), yvs.pop((j, 1))
                k.dve("tensor_tensor", gT[:, j, :], y0[:], y1[:], MUL, R=[y0, y1], W=[gT])
        xTn = xT if not final else None
        for t in range(FB // 128):
            def mk(t=t):
                pa, pb = g.PS.next(), g.PS.next()
                for hf, p in enumerate((pa, pb)):
                    for c in range(22):
                        k.mm(p[:, :], gT[:, c, t * 128:(t + 1) * 128], Wd[:, c, hf * 512:(hf + 1) * 512],
                             c == 0, c == 21, [gT, Wd], [p])
                return pa, pb
            ln_block(g, mk, g.hA, 2 * layer + 1, b * FB + t * 128, hout, xTn, t, lnw, final)
        if xTn is not None:
            k.dma(hTv[:, :, b * FB:(b + 1) * FB], xTn[:], R=[xTn], q="pool")
    k.release(m)


def phase_tables(g):
    k, nc = g.k, g.nc
    m = k.mark()
    rb = k.sb("rb", [33, 4])
    k.dve("memset", rb[:], 1.0, W=[rb])
    k.dma(rb[0:32, :], g.I["rel_bias"][:, :], W=[rb])
    ohd = k.sb("ohd", [33, 2560])
    k.dma(ohd[:], g.I["c_ohdiff"][:, :], W=[ohd])
    ohl = k.sb("ohl", [33, 3, 384])
    k.dma(ohl[:], g.I["c_ohdil"].rearrange("g b x -> b g x"), W=[ohl])
    lhs_r = k.rot("tlhs", [33, 128], F32, 2)
    tb_r = k.rot("ttb", [128, 2560], BF16, 2)
    tl_r = k.rot("ttl", [128, 384], BF16, 2)
    for h in range(4):
        lhs = lhs_r.next()
        k.dve("tensor_scalar_mul", lhs[:], g.ones[0:33, :], rb[:, h:h + 1], R=[g.ones, rb], W=[lhs])
        tb = tb_r.next()
        for x0 in range(0, 2560, 512):
            ps = g.PS.next()
            k.mm(ps[:, :], lhs[:], ohd[:, x0:x0 + 512], True, True, [lhs, ohd], [ps])
            k.act(tb[:, x0:x0 + 512], ps[:, :], AF.Exp, [ps], [tb])
        k.dma(g.ftd[h], tb[:], R=[tb], q="pool")
        for gi in range(3):
            tl = tl_r.next()
            ps = g.PS.next()
            k.mm(ps[:, 0:384], lhs[:], ohl[:, gi, :], True, True, [lhs, ohl], [ps])
            k.act(tl[:], ps[:, 0:384], AF.Exp, [ps], [tl])
            k.dma(g.ftl[gi * 4 + h], tl[:], R=[tl], q="pool")
    k.release(m)


def phase_in_even(g):
    k, S, nc = g.k, g.S, g.nc
    dsc = g.dsc
    g.gqT = dsc("gqT", [768, S], BF16)
    g.gkT = dsc("gkT", [768, S], BF16)
    g.gk = dsc("gk", [S, 768], BF16)
    g.gv = dsc("gv", [S, 768], BF16)
    g.gz = dsc("gz", [S, 768], F32)
    g.gg = dsc("gg", [S, 6], F32)
    g.gbeta = dsc("gbeta", [S, 6], F32)
    g.dq = dsc("dq", [768, S], BF16)
    g.dk = dsc("dk", [768, S], BF16)
    g.dvv = dsc("dvv", [S, 768], BF16)
    m = k.mark()
    W = k.sb("wine", [128, 8, 5388], BF16)
    m2 = k.mark()
    load_w_bf16(g, g.I["w_in_even"], W, 8, 5388, cs=2694)
    k.release(m2)
    cw = k.sb("gcw", [128, 18, 4], F32)
    rows_to_cols(g, g.I["gdn_conv_w"], 4, 2304, cw)
    dtb = k.sb("dtb", [128, 6])
    nea = k.sb("nea", [128, 6])
    k.dma(dtb[:], g.I["gdn_dt_bias"][0, :].partition_broadcast(128), W=[dtb])
    k.dma(nea[:], g.I["gdn_a_log"][0, :].partition_broadcast(128), W=[nea])
    k.act(nea[:], nea[:], AF.Exp, [nea], [nea])
    k.dve("tensor_scalar_mul", nea[:], nea[:], -1.0, R=[nea], W=[nea])
    halo = k.sb("ghalo", [128, 18, 3], F32)
    k.dve("memset", halo[:], 0.0, W=[halo])
    xin = k.rot("exT", [128, 8, 512], BF16, 2)
    ur = k.rot("eu", [128, 515], F32, 5)
    yr = k.rot("ey", [128, 512], F32, 4)
    rsr = k.rot("ers", [128, 512], F32, 2)
    sqr = k.rot("esq", [128, 512], BF16, 2)
    qnr = k.rot("eqn", [128, 512], BF16, 3)
    tkr = k.rot("etk", [128, 4, 128], BF16, 2)
    ztr = k.rot("ezt", [128, 768], F32, 2)
    dvr = k.rot("edv", [128, 768], BF16, 2)
    smr = k.rot("esm", [128, 12], F32, 2)
    ggr = k.rot("egg", [128, 4, 6], F32, 2)
    ber = k.rot("ebe", [128, 4, 6], F32, 2)
    ybig = k.sb("eybig", [128, 12, 512], F32)
    hTv = g.hT.rearrange("(c p) s -> p c s", p=128)

    def to_tok(qn, dst, hh):
        pbk = g.PB.next()
        for t in range(4):
            k.transpose(pbk[:, t * 128:(t + 1) * 128], qn[:, t * 128:(t + 1) * 128], g.identb[:],
                        [qn, g.identb], [pbk], inc=(t == 3))
        tk = tkr.next()
        k.dve("tensor_copy", tk[:], pbk[:, 0:512].rearrange("p (t d) -> p t d", t=4), R=[pbk], W=[tk])
        k.dma(dst[to_tok.s0:to_tok.s0 + 512, hh * 128:(hh + 1) * 128].rearrange("(t p) d -> p t d", p=128), tk[:],
              R=[tk], q="pool")

    xT_next = None
    for b in range(g.NB):
        s0 = b * 512
        to_tok.s0 = s0
        if b == 0:
            xT = xin.next()
            k.dma(xT[:], hTv[:, :, 0:512], W=[xT])
        else:
            xT = xT_next
        if b + 1 < g.NB:
            xT_next = xin.next()
            k.dma(xT_next[:], hTv[:, :, s0 + 512:s0 + 1024], W=[xT_next])
        us = {}
        LAG = 2
        for i in range(18 + LAG):
            if i < 18:
                cc = i
                ps = g.PS.next()
                for c in range(8):
                    k.mm(ps[:, :], W[:, c, cc * 128:(cc + 1) * 128], xT[:, c, :], c == 0, c == 7, [W, xT], [ps])
                u = ur.next()
                k.pool("tensor_copy", u[:, 0:3], halo[:, cc, :], R=[halo], W=[u])
                k.op("act", "copy", u[:, 3:515], ps[:, :], R=[ps], W=[u])
                k.pool("tensor_copy", halo[:, cc, :], u[:, 512:515], R=[u], W=[halo])
                us[i] = u
            if i >= LAG:
                cc = i - LAG
                u = us.pop(cc)
                y = yr.next()
                if cc % 2:
                    k.act(y[:], u[:, 3:515], AF.Identity, [u, cw], [y], scale=cw[:, cc, 3:4])
                else:
                    k.dve("tensor_scalar_mul", y[:], u[:, 3:515], cw[:, cc, 3:4], R=[u, cw], W=[y])
                k.dve("scalar_tensor_tensor", y[:], u[:, 2:514], cw[:, cc, 2:3], y[:], MUL, ADD, R=[u, cw, y], W=[y])
                k.dve("scalar_tensor_tensor", y[:], u[:, 1:513], cw[:, cc, 1:2], y[:], MUL, ADD, R=[u, cw, y], W=[y])
                k.dve("scalar_tensor_tensor", y[:], u[:, 0:512], cw[:, cc, 0:1], y[:], MUL, ADD, R=[u, cw, y], W=[y])
                if cc < 12:
                    k.act(ybig[:, cc, :], y[:], AF.Silu, [y], [ybig])
                else:
                    hh = cc - 12
                    qn = qnr.next()
                    k.act(qn[:], y[:], AF.Silu, [y], [qn])
                    to_tok(qn, g.gv, hh)
        for cc in range(12):
            sq = sqr.next()
            k.dve("tensor_tensor", sq[:], ybig[:, cc, :], ybig[:, cc, :], MUL, R=[ybig], W=[sq])
            ps2 = g.PS.next()
            k.mm(ps2[:, :], g.onesb[:], sq[:], True, True, [g.onesb, sq], [ps2])
            rs = rsr.next()
            k.act(rs[:], ps2[:, :], AF.Ln, [ps2], [rs], bias=RMS_EPS)
            k.act(rs[:], rs[:], AF.Exp, [rs], [rs], scale=-0.5)
            sc = float(128 ** -0.5) if cc < 6 else 1.0
            qn = qnr.next()
            k.dve("scalar_tensor_tensor", qn[:], ybig[:, cc, :], sc, rs[:], MUL, MUL, R=[ybig, rs], W=[qn])
            dst = g.gqT if cc < 6 else g.gkT
            hh = cc % 6
            k.dma(dst[hh * 128:(hh + 1) * 128, s0:s0 + 512], qn[:], R=[qn], q="pool")
            if cc >= 6:
                to_tok(qn, g.gk, hh)
        gg = ggr.next()
        be = ber.next()
        for t in range(4):
            t0 = s0 + t * 128
            zt = ztr.next()
            dv = dvr.next()
            for hf in range(2):
                ps = g.PS.next()
                for c in range(8):
                    k.mm(ps[:, 0:384], xT[:, c, t * 128:(t + 1) * 128], W[:, c, 2304 + hf * 384:2304 + (hf + 1) * 384],
                         c == 0, c == 7, [W, xT], [ps])
                k.act(zt[:, hf * 384:(hf + 1) * 384], ps[:, 0:384], AF.Silu, [ps], [zt])
                ps = g.PS.next()
                for c in range(8):
                    k.mm(ps[:, 0:384], xT[:, c, t * 128:(t + 1) * 128], W[:, c, 4620 + hf * 384:4620 + (hf + 1) * 384],
                         c == 0, c == 7, [W, xT], [ps])
                k.dve("tensor_copy", dv[:, hf * 384:(hf + 1) * 384], ps[:, 0:384], R=[ps], W=[dv])
            k.dma(g.gz[t0:t0 + 128, :], zt[:], R=[zt], q="pool")
            k.dma(g.dvv[t0:t0 + 128, :], dv[:], R=[dv], q="pool")
            ps = g.PS.next()
            for c in range(8):
                k.mm(ps[:, 0:12], xT[:, c, t * 128:(t + 1) * 128], W[:, c, 3072:3084], c == 0, c == 7, [W, xT], [ps])
            sm = smr.next()
            k.act(sm[:, 0:6], ps[:, 0:6], AF.Exp, [ps], [sm], scale=-1.0)
            k.dve("tensor_scalar_add", sm[:, 0:6], sm[:, 0:6], 1.0, R=[sm], W=[sm])
            k.dve("reciprocal", be[:, t, :], sm[:, 0:6], R=[sm], W=[be])
            k.dve("tensor_tensor", sm[:, 6:12], ps[:, 6:12], dtb[:], ADD, R=[ps, dtb], W=[sm])
            k.act(sm[:, 6:12], sm[:, 6:12], AF.Exp, [sm], [sm])
            k.act(sm[:, 6:12], sm[:, 6:12], AF.Ln, [sm], [sm], bias=1.0)
            k.dve("tensor_tensor", gg[:, t, :], sm[:, 6:12], nea[:], MUL, R=[sm, nea], W=[gg])
        k.dma(g.gg[s0:s0 + 512, :].rearrange("(t p) h -> p t h", p=128), gg[:], R=[gg], q="pool")
        k.dma(g.gbeta[s0:s0 + 512, :].rearrange("(t p) h -> p t h", p=128), be[:], R=[be], q="pool")
        for cc in range(12):
            c0 = 3084 + cc * 128
            ps = g.PS.next()
            for c in range(8):
                k.mm(ps[:, :], W[:, c, c0:c0 + 128], xT[:, c, :], c == 0, c == 7, [W, xT], [ps])
            qn = qnr.next()
            if cc % 2:
                k.op("act", "copy", qn[:], ps[:, :], R=[ps], W=[qn])
            else:
                k.dve("tensor_copy", qn[:], ps[:, :], R=[ps], W=[qn])
            dst = g.dq if cc < 6 else g.dk
            hh = cc % 6
            k.dma(dst[hh * 128:(hh + 1) * 128, s0:s0 + 512], qn[:], R=[qn], q="pool")
    k.release(m)


def phase_gdn(g):
    k, S, nc = g.k, g.S, g.nc
    m = k.mark()
    H = 6
    normw = k.sb("gnw", [128, 128])
    k.dma(normw[:], g.I["gdn_norm_w"][0, :].partition_broadcast(128), W=[normw])
    St = k.sb("gS", [128, H, 128], F32)
    Sb = k.sb("gSb", [128, H, 128], BF16)
    k.dve("memset", St[:], 0.0, W=[St])
    k.dve("memset", Sb[:], 0.0, W=[Sb])
    qTr = k.rot("gqT", [128, H, 128], BF16, 2)
    kTr = k.rot("gkT", [128, H, 128], BF16, 2)
    ktr = k.rot("gkt", [128, 768], BF16, 2)
    vtr = k.rot("gvt", [128, 768], BF16, 2)
    zsr = k.rot("gzs", [128, 768], F32, 2)
    ggr = k.rot("ggg", [128, 6], F32, 2)
    ber = k.rot("gbe", [128, 6], F32, 2)
    smr = k.rot("gsm", [128, 32], F32, 2)
    oabr = k.rot("goab", [128, 768], BF16, 2)
    oTr = k.rot("goT", [128, H, 128], BF16, 2)

    def per_head(name, dt=F32, n=1):
        return [[k.sb("%s%d_%d" % (name, i, h), [128, 128], dt) for h in range(H)] for i in range(n)]

    P = per_head("gP", F32, 2)
    PT = per_head("gPT", F32, 2)
    Tt = per_head("gTt", F32, 2)
    Lg = per_head("gLg")[0]
    GT = per_head("gGT")[0]
    GTs = per_head("gGTs")[0]
    Ttb = per_head("gTtb", BF16)[0]
    Ut = per_head("gUt")[0]
    Ke = per_head("gKe", BF16)[0]
    WtT = per_head("gWtT", BF16)[0]
    tmp = per_head("gtmp")[0]
    dlb = per_head("gdlb", BF16)[0]
    asb = per_head("gasb")[0]
    QKG = per_head("gQKG", BF16)[0]
    ob = per_head("gob")[0]
    Kd = per_head("gKd", BF16)[0]
    st8 = [k.sb("gst8_%d" % h, [128, 12]) for h in range(H)]
    gqTv = g.gqT.rearrange("(h d) s -> d h s", d=128)
    gkTv = g.gkT.rearrange("(h d) s -> d h s", d=128)
    oTv = g.oT[0:768, :].rearrange("(h d) s -> d h s", d=128)
    HS = lambda h: slice(h * 128, (h + 1) * 128)
    for n in range(S // 128):
        t0 = n * 128
        qT, kT, kt, vt, zs, gg, be = qTr.next(), kTr.next(), ktr.next(), vtr.next(), zsr.next(), ggr.next(), ber.next()
        k.dma(qT[:], gqTv[:, :, t0:t0 + 128], W=[qT])
        k.dma(kT[:], gkTv[:, :, t0:t0 + 128], W=[kT])
        k.dma(kt[:], g.gk[t0:t0 + 128, :], W=[kt])
        k.dma(vt[:], g.gv[t0:t0 + 128, :], W=[vt])
        k.dma(zs[:], g.gz[t0:t0 + 128, :], W=[zs])
        k.dma(gg[:], g.gg[t0:t0 + 128, :], W=[gg])
        k.dma(be[:], g.gbeta[t0:t0 + 128, :], W=[be])
        sm = smr.next()
        ps = g.PS.next()
        k.mm(ps[:, 0:6], g.ltri[:], gg[:], True, True, [g.ltri, gg], [ps])
        k.mm(ps[:, 8:14], g.ones[:], gg[:], True, True, [g.ones, gg], [ps])
        k.act(sm[:, 0:6], ps[:, 0:6], AF.Exp, [ps], [sm])
        k.act(sm[:, 6:12], ps[:, 8:14], AF.Exp, [ps], [sm])
        k.dve("tensor_copy", sm[:, 18:24], ps[:, 0:6], R=[ps], W=[sm])
        k.dve("tensor_tensor", sm[:, 12:18], ps[:, 8:14], sm[:, 18:24], SUB, R=[ps, sm], W=[sm])
        k.act(sm[:, 12:18], sm[:, 12:18], AF.Exp, [sm], [sm])
        k.dve("tensor_scalar_mul", sm[:, 24:30], be[:], -1.0, R=[be], W=[sm])
        HR = range(H)
        psDs, psKs, psTs = {}, {}, {}
        for h in HR:
            k.pool("tensor_scalar_mul", Lg[h][:], g.ltri[:], gg[:, h:h + 1], R=[g.ltri, gg], W=[Lg[h]])
        for h in HR:
            psD = psDs[h] = g.PS.next()
            k.mm(psD[:, 0:128], g.ones[:], Lg[h][:], True, False, [g.ones, Lg[h]], [psD])
            k.mm(psD[:, 0:128], Lg[h][:], g.nones[:], False, True, [g.nones, Lg[h]], [psD])
            if h >= 2:
                hh = h - 2
                k.dve("tensor_tensor", GT[hh][:], psDs[hh][:, 0:128], g.neg[:], ADD, R=[psDs[hh], g.neg], W=[GT[hh]])
        for hh in (H - 2, H - 1):
            k.dve("tensor_tensor", GT[hh][:], psDs[hh][:, 0:128], g.neg[:], ADD, R=[psDs[hh], g.neg], W=[GT[hh]])
        for h in HR:
            k.act(GT[h][:], GT[h][:], AF.Exp, [GT[h]], [GT[h]])
        for h in HR:
            k.pool("tensor_tensor", GTs[h][:], GT[h][:], g.lstrict[:], MUL, R=[GT[h], g.lstrict], W=[GTs[h]])
        for h in HR:
            psK = g.PS.next()
            k.mm(psK[:, 0:128], kT[:, h, :], kT[:, h, :], True, True, [kT], [psK])
            k.dve("scalar_tensor_tensor", PT[0][h][:], psK[:, 0:128], sm[:, 24 + h:25 + h], GTs[h][:], MUL, MUL,
                  R=[psK, sm, GTs[h]], W=[PT[0][h]])
        for h in HR:
            k.pool("tensor_tensor", Tt[0][h][:], PT[0][h][:], g.ident[:], ADD, R=[PT[0][h], g.ident], W=[Tt[0][h]])
        for h in HR:
            psT = g.PS.next()
            k.transpose(psT[:, 0:128], PT[0][h][:], g.ident[:], [PT[0][h], g.ident], [psT])
            k.op("act", "copy", P[0][h][:], psT[:, 0:128], R=[psT], W=[P[0][h]])
        for s in range(1, 8):
            a, b = (s - 1) % 2, s % 2
            for h in HR:
                if s <= 6:
                    psP = g.PS.next()
                    k.mm(psP[:, 0:128], PT[a][h][:], P[a][h][:], True, True, [PT[a][h], P[a][h]], [psP])
                    k.op("act", "copy", P[b][h][:], psP[:, 0:128], R=[psP], W=[P[b][h]])
                if s < 6:
                    psPT = g.PS.next()
                    k.mm(psPT[:, 0:128], P[a][h][:], PT[a][h][:], True, True, [PT[a][h], P[a][h]], [psPT])
                    k.op("act", "copy", PT[b][h][:], psPT[:, 0:128], R=[psPT], W=[PT[b][h]])
                if s >= 2:
                    psTt = g.PS.next()
                    k.mm(psTt[:, 0:128], P[a][h][:], Tt[b][h][:], True, True, [P[a][h], Tt[b][h]], [psTt])
                    k.dve("tensor_tensor", Tt[a][h][:], Tt[b][h][:], psTt[:, 0:128], ADD, R=[Tt[b][h], psTt], W=[Tt[a][h]])
        TF = Tt[0]
        oab = oabr.next()
        for h in range(H):
            k.op("act", "copy", Ttb[h][:], TF[h][:], R=[TF[h]], W=[Ttb[h]])
            k.dve("tensor_scalar_mul", Ke[h][:], kt[:, HS(h)], sm[:, h:h + 1], R=[kt, sm], W=[Ke[h]])
            k.pool("tensor_scalar_mul", Kd[h][:], kt[:, HS(h)], sm[:, 12 + h:13 + h], R=[kt, sm], W=[Kd[h]])
        for h in range(H):
            psU = g.PS.next()
            k.mm(psU[:, 0:128], Ttb[h][:], vt[:, HS(h)], True, True, [Ttb[h], vt], [psU])
            k.op("act", "copy", Ut[h][:], psU[:, 0:128], R=[psU], W=[Ut[h]])
            psW = g.PS.next()
            k.mm(psW[:, 0:128], Ke[h][:], Ttb[h][:], True, True, [Ke[h], Ttb[h]], [psW])
            k.dve("tensor_copy", WtT[h][:], psW[:, 0:128], R=[psW], W=[WtT[h]])
        for h in range(H):
            psWS = g.PS.next()
            k.mm(psWS[:, 0:128], WtT[h][:], Sb[:, h, :], True, True, [WtT[h], Sb], [psWS])
            k.dve("tensor_tensor", tmp[h][:], Ut[h][:], psWS[:, 0:128], SUB, R=[Ut[h], psWS], W=[tmp[h]])
            k.act(dlb[h][:], tmp[h][:], AF.Identity, [tmp[h], be], [dlb[h]], scale=be[:, h:h + 1])
            psQS = g.PS.next()
            k.mm(psQS[:, 0:128], qT[:, h, :], Sb[:, h, :], True, True, [qT, Sb], [psQS])
            k.act(asb[h][:], psQS[:, 0:128], AF.Identity, [psQS, sm], [asb[h]], scale=sm[:, h:h + 1])
            psQK = g.PS.next()
            k.mm(psQK[:, 0:128], kT[:, h, :], qT[:, h, :], True, True, [kT, qT], [psQK])
            k.dve("tensor_tensor", QKG[h][:], psQK[:, 0:128], GT[h][:], MUL, R=[psQK, GT[h]], W=[QKG[h]])
        for h in range(H):
            psIN = g.PS.next()
            k.mm(psIN[:, 0:128], QKG[h][:], dlb[h][:], True, True, [QKG[h], dlb[h]], [psIN])
            k.dve("tensor_tensor", ob[h][:], psIN[:, 0:128], asb[h][:], ADD, R=[psIN, asb[h]], W=[ob[h]])
            psSU = g.PS.next()
            k.mm(psSU[:, 0:128], Kd[h][:], dlb[h][:], True, True, [Kd[h], dlb[h]], [psSU])
            k.dve("scalar_tensor_tensor", St[:, h, :], St[:, h, :], sm[:, 6 + h:7 + h], psSU[:, 0:128], MUL, ADD,
                  R=[St, sm, psSU, Sb], W=[St])
            k.op("act", "copy", Sb[:, h, :], St[:, h, :], R=[St], W=[Sb])
        for h in range(H):
            s8 = st8[h]
            k.dve("bn_stats", s8[:, 0:6], ob[h][:], R=[ob[h]], W=[s8])
            k.dve("bn_aggr", s8[:, 6:8], s8[:, 0:6], R=[s8], W=[s8])
            k.dve("scalar_tensor_tensor", s8[:, 8:9], s8[:, 6:7], s8[:, 6:7], s8[:, 7:8], MUL, ADD, R=[s8], W=[s8])
            k.act(s8[:, 9:10], s8[:, 8:9], AF.Ln, [s8], [s8], bias=RMS_EPS)
            k.act(s8[:, 9:10], s8[:, 9:10], AF.Exp, [s8], [s8], scale=-0.5)
            k.dve("scalar_tensor_tensor", tmp[h][:], ob[h][:], s8[:, 9:10], normw[:], MUL, MUL,
                  R=[ob[h], s8, normw], W=[tmp[h]])
            k.pool("tensor_tensor", oab[:, HS(h)], tmp[h][:], zs[:, HS(h)], MUL, R=[tmp[h], zs], W=[oab])
        pbk = g.PB.next()
        for h in range(H):
            k.transpose(pbk[:, HS(h)], oab[:, HS(h)], g.identb[:], [oab, g.identb], [pbk], inc=(h == H - 1))
        oT = oTr.next()
        k.dve("tensor_copy", oT[:], pbk[:, 0:768].rearrange("p (h t) -> p h t", h=H), R=[pbk], W=[oT])
        k.dma(oTv[:, :, t0:t0 + 128], oT[:], R=[oT], q="pool")
    k.release(m)


def phase_dil(g):
    k, S, nc = g.k, g.S, g.nc
    m = k.mark()
    U = min(2048, S)
    DIL = (1, 4, 16)
    ft = k.sb("dft", [128, 12, 256], BF16)
    for i in range(12):
        src = bass.AP(tensor=g.ftl_h, offset=i * 128 * 384 + 127, ap=[[383, 128], [1, 256]])
        k.dma(ft[:, i, :], src, W=[ft])
    Oacc = k.sb("dOacc", [128, 2, U], F32)
    Lacc = k.sb("dLacc", [128, 2, U], F32)
    ob = k.sb("dob", [128, 2, U], BF16)
    kbuf = [k.rot("dkT%d" % gi, [128, 2, 128 * d], BF16, 2) for gi, d in enumerate(DIL)]
    vbuf = [k.rot("dvt%d" % gi, [128, d, 256], BF16, 2) for gi, d in enumerate(DIL)]
    qbuf = [k.rot("dqT%d" % gi, [128, 4, 128 * d], BF16, 1) for gi, d in enumerate(DIL)]
    for qb_ in qbuf:
        k.dve("memset", qb_.bufs[0][:], 0.0, W=[qb_.bufs[0]])
    ptr = k.rot("dpt", [128, 512], BF16, 4)
    scb = Rot(g.PS.bufs[4:6])
    accb = Rot(g.PS.bufs[0:4])
    oTv = g.oT[768:1024, :].rearrange("(pr p) s -> p pr s", p=128)
    for u in range(S // U):
        u0 = u * U
        for gi, d in enumerate(DIL):
            SBT = 128 * d
            dqg = g.dq[gi * 256:(gi + 1) * 256, :]
            dkv = g.dk[gi * 256:(gi + 1) * 256, :].rearrange("(pr p) s -> p pr s", p=128)
            kprev = vprev = None
            if u0 > 0:
                kprev, vprev = kbuf[gi].next(), vbuf[gi].next()
                p0 = u0 - SBT
                k.dma(kprev[:], dkv[:, :, p0:p0 + SBT], W=[kprev])
                k.dma(vprev[:], g.dvv[p0:p0 + SBT, gi * 256:(gi + 1) * 256].rearrange("(m r) c -> m r c", r=d), W=[vprev])
            for sbi in range(U // SBT):
                s0 = u0 + sbi * SBT
                kc, vc, qc = kbuf[gi].next(), vbuf[gi].next(), qbuf[gi].next()
                k.dma(kc[:], dkv[:, :, s0:s0 + SBT], W=[kc])
                for h in range(4):
                    k.dma(qc[(h % 2) * 64:(h % 2 + 1) * 64, h, :], dqg[h * 64:(h + 1) * 64, s0:s0 + SBT], W=[qc])
                k.dma(vc[:], g.dvv[s0:s0 + SBT, gi * 256:(gi + 1) * 256].rearrange("(m r) c -> m r c", r=d), W=[vc])
                blocks = ([(kprev, vprev, 128)] if kprev is not None else []) + [(kc, vc, 0)]
                nb = len(blocks)
                pts = {}
                for i in range(d + 1):
                    if i < d:
                        r = i
                        cur = []
                        for bi, (kk, vv, off) in enumerate(blocks):
                            psS = scb.next()
                            for h in range(4):
                                k.mm(psS[:, h * 128:(h + 1) * 128], kk[:, h // 2, r:r + 127 * d + 1:d],
                                     qc[:, h, r:r + 127 * d + 1:d], True, True, [kk, qc], [psS], inc=(h == 3))
                            pt = ptr.next()
                            k.act(pt[:], psS[:, :], AF.Exp, [psS], [pt], scale=0.125)
                            ptv = pt[:].rearrange("p (h q) -> p h q", h=4)
                            k.dve("tensor_tensor", ptv, ptv, ft[:, gi * 4:(gi + 1) * 4, off:off + 128], MUL, R=[pt, ft], W=[pt])
                            cur.append(pt)
                        pts[i] = cur
                    if i >= 1:
                        r = i - 1
                        cur = pts.pop(r)
                        psO, psL = accb.next(), accb.next()
                        for h in range(4):
                            pr = h // 2
                            for bi, (kk, vv, off) in enumerate(blocks):
                                last_i = (h == 3 and bi == nb - 1)
                                k.mm(psO[:, h * 128:(h + 1) * 128], vv[:, r, pr * 128:(pr + 1) * 128],
                                     cur[bi][:, h * 128:(h + 1) * 128], bi == 0, bi == nb - 1, [vv, cur[bi]], [psO], inc=last_i)
                        for h in range(4):
                            for bi, (kk, vv, off) in enumerate(blocks):
                                last_i = (h == 3 and bi == nb - 1)
                                k.mm(psL[:, h * 128:(h + 1) * 128], g.onesb[:], cur[bi][:, h * 128:(h + 1) * 128],
                                     bi == 0, bi == nb - 1, [g.onesb, cur[bi]], [psL], inc=last_i)
                        lo = sbi * SBT + r
                        for par in range(2):
                            hp = slice(par * 64, (par + 1) * 64)
                            osl = Oacc[hp, :, lo:lo + 127 * d + 1:d]
                            lsl = Lacc[hp, :, lo:lo + 127 * d + 1:d]
                            pso = psO[hp, :].rearrange("p (h q) -> p h q", h=4)[:, par::2, :]
                            psl = psL[hp, :].rearrange("p (h q) -> p h q", h=4)[:, par::2, :]
                            if gi == 0:
                                k.op("act", "copy", osl, pso, R=[psO], W=[Oacc])
                                k.dve("tensor_copy", lsl, psl, R=[psL], W=[Lacc])
                            else:
                                k.dve("tensor_tensor", osl, osl, pso, ADD, R=[psO, Oacc], W=[Oacc])
                                k.dve("tensor_tensor", lsl, lsl, psl, ADD, R=[psL, Lacc], W=[Lacc])
                kprev, vprev = kc, vc
        for pr in range(2):
            k.dve("reciprocal", Lacc[:, pr, :], Lacc[:, pr, :], R=[Lacc], W=[Lacc])
            k.dve("tensor_tensor", ob[:, pr, :], Oacc[:, pr, :], Lacc[:, pr, :], MUL, R=[Oacc, Lacc], W=[ob])
        k.dma(oTv[:, :, u0:u0 + U], ob[:], R=[ob], q="pool")
    k.release(m)


def phase_in_odd(g):
    k, S, nc = g.k, g.S, g.nc
    dsc = g.dsc
    g.cq = dsc("cq", [512, S], BF16)
    g.ck = dsc("ck", [512, S], BF16)
    g.cv = dsc("cv", [S, 512], BF16)
    g.lqT = dsc("lqT", [256, S], F32)
    g.lkT = dsc("lkT", [256, S], F32)
    g.lk = dsc("lk", [S, 256], F32)
    g.lv = dsc("lv", [S, 512], BF16)
    g.lr = dsc("lr", [S, 512], F32)
    g.la = dsc("la", [S, 256], F32)
    m = k.mark()
    W = k.sb("wino", [128, 8, 3088], BF16)
    m2 = k.mark()
    load_w_bf16(g, g.I["w_in_odd"], W, 8, 3088)
    k.release(m2)
    wg32 = k.sb("wg32", [128, 256])
    wg = k.sb("wg", [128, 256], BF16)
    k.dve("memset", wg32[:], 0.0, W=[wg32])
    k.dma(wg32[112:128, :], g.I["gla_w_gate"][:, :], W=[wg32])
    k.dve("tensor_copy", wg[:], wg32[:], R=[wg32], W=[wg])
    bgb = k.sb("bgb", [128, 256])
    k.dma(bgb[:], g.I["gla_b_gate"][0, :].partition_broadcast(128), W=[bgb])
    xin = k.rot("oxT", [128, 8, 512], BF16, 2)
    fbr = k.rot("ofb", [128, 512], BF16, 3)
    ffr = k.rot("off", [128, 512], F32, 3)
    tbr = k.rot("otb", [128, 512], BF16, 3)
    tfr = k.rot("otf", [128, 512], F32, 3)
    gdr = k.rot("ogd", [128, 512], BF16, 2)
    hTv = g.hT.rearrange("(c p) s -> p c s", p=128)
    xT_next = None
    for b in range(g.NB):
        s0 = b * 512
        if b == 0:
            xT = xin.next()
            k.dma(xT[:], hTv[:, :, 0:512], W=[xT])
        else:
            xT = xT_next
        if b + 1 < g.NB:
            xT_next = xin.next()
            k.dma(xT_next[:], hTv[:, :, s0 + 512:s0 + 1024], W=[xT_next])

        def fm(c0, ncol):
            ps = g.PS.next()
            for c in range(8):
                k.mm(ps[0:ncol, :], W[:, c, c0:c0 + ncol], xT[:, c, :], c == 0, c == 7, [W, xT], [ps])
            return ps

        def tm(t, c0, ncol):
            ps = g.PS.next()
            for c in range(8):
                k.mm(ps[:, 0:ncol], xT[:, c, t * 128:(t + 1) * 128], W[:, c, c0:c0 + ncol], c == 0, c == 7, [W, xT], [ps])
            return ps

        for cc in range(8):
            ps = fm(cc * 128, 128)
            fb = fbr.next()
            if cc % 2:
                k.op("act", "copy", fb[:], ps[:, :], R=[ps], W=[fb])
            else:
                k.dve("tensor_copy", fb[:], ps[:, :], R=[ps], W=[fb])
            dst = g.cq if cc < 4 else g.ck
            k.dma(dst[(cc % 4) * 128:(cc % 4 + 1) * 128, s0:s0 + 512], fb[:], R=[fb], q="pool")
        for cc in range(4):
            ps = fm(1536 + cc * 128, 128)
            ff = ffr.next()
            k.op("act", "copy", ff[:], ps[:, :], R=[ps], W=[ff])
            dst = g.lqT if cc < 2 else g.lkT
            k.dma(dst[(cc % 2) * 128:(cc % 2 + 1) * 128, s0:s0 + 512], ff[:], R=[ff], q="pool")
        ps = fm(2960, 128)
        gd = gdr.next()
        k.dve("tensor_copy", gd[:], ps[:, :], R=[ps], W=[gd])
        for t in range(4):
            t0 = s0 + t * 128
            ps = tm(t, 1024, 512)
            tb = tbr.next()
            k.dve("tensor_copy", tb[:], ps[:, :], R=[ps], W=[tb])
            k.dma(g.cv[t0:t0 + 128, :], tb[:], R=[tb], q="pool")
            ps = tm(t, 2048, 512)
            tb = tbr.next()
            k.op("act", "copy", tb[:], ps[:, :], R=[ps], W=[tb])
            k.dma(g.lv[t0:t0 + 128, :], tb[:], R=[tb], q="pool")
            ps = tm(t, 2560, 512)
            tf = tfr.next()
            k.act(tf[:], ps[:, :], AF.Silu, [ps], [tf])
            k.dma(g.lr[t0:t0 + 128, :], tf[:], R=[tf], q="pool")
            ps = tm(t, 1792, 256)
            tf = tfr.next()
            k.dve("tensor_copy", tf[:, 0:256], ps[:, 0:256], R=[ps], W=[tf])
            k.dma(g.lk[t0:t0 + 128, :], tf[:, 0:256], R=[tf], q="pool")
            ps = g.PS.next()
            k.mm(ps[:, 0:256], gd[:, t * 128:(t + 1) * 128], wg[:], True, True, [gd, wg], [ps])
            tf = tfr.next()
            k.dve("tensor_tensor", tf[:, 0:256], ps[:, 0:256], bgb[:], ADD, R=[ps, bgb], W=[tf])
            k.act(tf[:, 0:256], tf[:, 0:256], AF.Exp, [tf], [tf], scale=-1.0)
            k.act(tf[:, 0:256], tf[:, 0:256], AF.Ln, [tf], [tf], bias=1.0)
            k.dve("tensor_scalar_mul", tf[:, 0:256], tf[:, 0:256], -1.0 / 16.0, R=[tf], W=[tf])
            k.dma(g.la[t0:t0 + 128, :], tf[:, 0:256], R=[tf], q="pool")
    k.release(m)


def phase_diff(g):
    import math
    k, S, nc = g.k, g.S, g.nc
    m = k.mark()
    lam_init = 0.8 - 0.6 * math.exp(-0.3 * 1)
    lt = k.sb("clt", [128, 256])
    k.dma(lt[:], g.I["diff_lambda"].rearrange("a b -> (a b)").partition_broadcast(128), W=[lt])
    ls = k.sb("cls", [128, 8])
    pr = k.sb("cpr", [128, 128])
    k.dve("tensor_tensor", pr[:, 0:64], lt[:, 0:64], lt[:, 64:128], MUL, R=[lt], W=[pr])
    k.dve("tensor_tensor", pr[:, 64:128], lt[:, 128:192], lt[:, 192:256], MUL, R=[lt], W=[pr])
    k.dve("reduce_sum", ls[:, 0:1], pr[:, 0:64], AX.X, R=[pr], W=[ls])
    k.dve("reduce_sum", ls[:, 1:2], pr[:, 64:128], AX.X, R=[pr], W=[ls])
    k.act(ls[:, 0:2], ls[:, 0:2], AF.Exp, [ls], [ls])
    k.dve("tensor_tensor", ls[:, 2:3], ls[:, 1:2], ls[:, 0:1], SUB, R=[ls], W=[ls])
    k.dve("tensor_scalar_add", ls[:, 2:3], ls[:, 2:3], -lam_init, R=[ls], W=[ls])
    cs = k.sb("ccs", [128, 1])
    k.dma(cs[:], g.I["diff_norm_w"][:, :], W=[cs])
    k.dve("tensor_scalar_mul", cs[:], cs[:], 1.0 - lam_init, R=[cs], W=[cs])
    kT = k.sb("ckT", [128, S], BF16)
    V = k.sb("cV", [128, S // 128, 128], BF16)
    F = k.sb("cF", [128, 2432], BF16)
    qr = k.rot("cqT", [128, 2, 512], BF16, 2)
    for qb_ in qr.bufs:
        k.dve("memset", qb_[:], 0.0, W=[qb_])
    ptr = k.rot("cpt", [128, 512], BF16, 4)
    wr = k.rot("cw", [128, 512], F32, 4)
    sqr = k.rot("csq", [128, 512], BF16, 2)
    obr = k.rot("cob", [128, 512], BF16, 2)
    acc = g.PS.bufs[0:4]
    scb = Rot(g.PS.bufs[4:6])
    for h in range(4):
        k.dma(kT[:], g.ck[h * 128:(h + 1) * 128, :], W=[kT])
        k.dma(V[:], g.cv[:, h * 128:(h + 1) * 128].rearrange("(kb p) d -> p kb d", p=128), W=[V])
        k.dma(F[:], bass.AP(tensor=g.ftd_h, offset=h * 128 * 2560 + 127, ap=[[2559, 128], [1, 2432]]), W=[F])
        for Q in range(S // 512):
            qT = qr.next()
            for c in range(2):
                k.dma(qT[c * 64:(c + 1) * 64, c, :], g.cq[h * 128 + c * 64:h * 128 + (c + 1) * 64, Q * 512:(Q + 1) * 512], W=[qT])
            nkb = 4 * Q + 4
            items = [(kb, c) for kb in range(nkb) for c in range(2)]
            LA = 2
            pts = {}
            for i in range(len(items) + LA):
                if i < len(items):
                    kb, c = items[i]
                    dl = Q * 512 - kb * 128
                    psS = scb.next()
                    k.mm(psS[:, :], kT[:, kb * 128:(kb + 1) * 128], qT[:, c, :], True, True, [kT, qT], [psS])
                    pt = ptr.next()
                    k.act(pt[:], psS[:, :], AF.Exp, [psS], [pt], scale=0.125)
                    if dl <= 1536:
                        k.dve("tensor_tensor", pt[:], pt[:], F[:, dl + 384:dl + 384 + 512], MUL, R=[pt, F], W=[pt])
                    pts[i] = pt
                if i >= LA:
                    kb, c = items[i - LA]
                    pt = pts.pop(i - LA)
                    k.mm(acc[c][:, :], V[:, kb, :], pt[:], kb == 0, kb == nkb - 1, [V, pt], [acc[c]])
                    k.mm(acc[2 + c][:, :], g.onesb[:], pt[:], kb == 0, kb == nkb - 1, [g.onesb, pt], [acc[2 + c]])
            r0, r1, o0, o1 = wr.next(), wr.next(), wr.next(), wr.next()
            k.dve("reciprocal", r0[:], acc[2][:, :], R=[acc[2]], W=[r0])
            k.dve("reciprocal", r1[:], acc[3][:, :], R=[acc[3]], W=[r1])
            k.dve("tensor_tensor", o0[:], acc[0][:, :], r0[:], MUL, R=[acc[0], r0], W=[o0])
            k.dve("tensor_tensor", o1[:], acc[1][:, :], r1[:], MUL, R=[acc[1], r1], W=[o1])
            k.dve("scalar_tensor_tensor", o0[:], o1[:], ls[:, 2:3], o0[:], MUL, ADD, R=[o1, ls, o0], W=[o0])
            sq = sqr.next()
            k.pool("tensor_tensor", sq[:], o0[:], o0[:], MUL, R=[o0], W=[sq])
            psq = scb.next()
            k.mm(psq[:, :], g.onesb[:], sq[:], True, True, [g.onesb, sq], [psq])
            k.act(r0[:], psq[:, :], AF.Ln, [psq], [r0], scale=1.0 / 128.0, bias=RMS_EPS)
            k.act(r0[:], r0[:], AF.Exp, [r0], [r0], scale=-0.5)
            ob = obr.next()
            k.dve("scalar_tensor_tensor", ob[:], o0[:], cs[:, 0:1], r0[:], MUL, MUL, R=[o0, cs, r0], W=[ob])
            k.dma(g.oT[h * 128:(h + 1) * 128, Q * 512:(Q + 1) * 512], ob[:], R=[ob], q="pool")
    k.release(m)


def phase_gla(g):
    k, S, nc = g.k, g.S, g.nc
    m = k.mark()
    H, NP = 4, 2
    normw = k.sb("lnw", [128, 128])
    k.dma(normw[:], g.I["gla_norm_w"][0, :].partition_broadcast(128), W=[normw])
    St = [k.sb("lS%d" % p, [128, 2, 128], F32) for p in range(NP)]
    Sb = [k.sb("lSb%d" % p, [128, 128], BF16) for p in range(NP)]
    for p in range(NP):
        k.dve("memset", St[p][:], 0.0, W=[St[p]])
        k.dve("memset", Sb[p][:], 0.0, W=[Sb[p]])
    qTr = k.rot("lqT", [128, NP, 128], F32, 2)
    kTr = k.rot("lkT", [128, NP, 128], F32, 2)
    ktr = k.rot("lkt", [128, 256], F32, 2)
    lar = k.rot("lla", [128, 256], F32, 2)
    vtr = k.rot("lvt", [128, 512], BF16, 2)
    rsr = k.rot("lrs", [128, 512], F32, 2)
    oabr = k.rot("loab", [128, 512], BF16, 2)
    oTr = k.rot("loT", [128, H, 128], BF16, 2)

    def pp(name, shape, dt=F32, n=NP):
        return [k.sb("%s_%d" % (name, i), shape, dt) for i in range(n)]

    ebT = pp("lebT", [128, 128])
    enbT = pp("lenbT", [128, 128])
    kpT = pp("lkpT", [128, 128], BF16)
    enb = pp("lenb", [128, 128])
    kp = pp("lkp", [128, 128], BF16)
    ebl = pp("lebl", [128, 1])
    qz = pp("lqz", [128, 128], BF16, H)
    for h in range(H):
        k.dve("memset", qz[h][:], 0.0, W=[qz[h]])
    att = pp("latt", [128, 128], BF16, H)
    ob = pp("lob", [128, 128], F32, H)
    tmp = pp("ltmp", [128, 128], F32, H)
    st8 = pp("lst8", [128, 12], F32, H)
    lqv = g.lqT.rearrange("(pr p) s -> p pr s", p=128)
    lkv = g.lkT.rearrange("(pr p) s -> p pr s", p=128)
    oTv = g.oT[512:1024, :].rearrange("(h d) s -> d h s", d=128)
    PK = lambda p: slice(p * 128, (p + 1) * 128)
    HV = lambda h: slice(h * 128, (h + 1) * 128)
    for n in range(S // 128):
        t0 = n * 128
        qT, kT, kt, la, vt, rs = qTr.next(), kTr.next(), ktr.next(), lar.next(), vtr.next(), rsr.next()
        k.dma(qT[:], lqv[:, :, t0:t0 + 128], W=[qT])
        k.dma(kT[:], lkv[:, :, t0:t0 + 128], W=[kT])
        k.dma(kt[:], g.lk[t0:t0 + 128, :], W=[kt])
        k.dma(la[:], g.la[t0:t0 + 128, :], W=[la])
        k.dma(vt[:], g.lv[t0:t0 + 128, :], W=[vt])
        k.dma(rs[:], g.lr[t0:t0 + 128, :], W=[rs])
        oab = oabr.next()
        for p in range(NP):
            psB = g.PS.next()
            k.mm(psB[:, 0:128], la[:, PK(p)], g.ltri[:], True, True, [la, g.ltri], [psB])
            k.act(ebT[p][:], psB[:, 0:128], AF.Exp, [psB], [ebT[p]])
            k.act(enbT[p][:], psB[:, 0:128], AF.Exp, [psB], [enbT[p]], scale=-1.0)
            k.dve("tensor_copy", ebl[p][:], ebT[p][:, 127:128], R=[ebT[p]], W=[ebl[p]])
            psb = g.PS.next()
            k.mm(psb[:, 0:128], g.ltri[:], la[:, PK(p)], True, True, [la, g.ltri], [psb])
            k.act(enb[p][:], psb[:, 0:128], AF.Exp, [psb], [enb[p]], scale=-1.0)
            k.pool("tensor_tensor", kpT[p][:], kT[:, p, :], enbT[p][:], MUL, R=[kT, enbT[p]], W=[kpT[p]])
            k.pool("tensor_tensor", kp[p][:], kt[:, PK(p)], enb[p][:], MUL, R=[kt, enb[p]], W=[kp[p]])
            for j in range(2):
                h, hp = 2 * p + j, slice(j * 64, (j + 1) * 64)
                k.dve("scalar_tensor_tensor", qz[h][hp, :], qT[hp, p, :], 0.125, ebT[p][hp, :], MUL, MUL,
                      R=[qT, ebT[p]], W=[qz[h]])
        for h in range(H):
            p, j = h // 2, h % 2
            hp = slice(j * 64, (j + 1) * 64)
            psA = g.PS.next()
            k.mm(psA[:, 0:128], kpT[p][:], qz[h][:], True, True, [kpT[p], qz[h]], [psA])
            k.dve("tensor_tensor", att[h][:], psA[:, 0:128], g.ltri[:], MUL, R=[psA, g.ltri], W=[att[h]])
            psO = g.PS.next()
            k.mm(psO[:, 0:128], qz[h][:], Sb[p][:], True, False, [qz[h], Sb[p]], [psO])
            k.mm(psO[:, 0:128], att[h][:], vt[:, HV(h)], False, True, [att[h], vt], [psO])
            k.op("act", "copy", ob[h][:], psO[:, 0:128], R=[psO], W=[ob[h]])
            psS = g.PS.next()
            k.mm(psS[:, 0:128], kp[p][:], vt[:, HV(h)], True, True, [kp[p], vt], [psS])
            k.dve("tensor_tensor", St[p][hp, j, :], St[p][hp, j, :], psS[hp, 0:128], ADD, R=[St[p], psS, Sb[p]], W=[St[p]])
            k.dve("tensor_scalar_mul", St[p][hp, j, :], St[p][hp, j, :], ebl[p][hp, 0:1], R=[St[p], ebl[p]], W=[St[p]])
            k.op("act", "copy", Sb[p][hp, :], St[p][hp, j, :], R=[St[p]], W=[Sb[p]])
        for h in range(H):
            s8 = st8[h]
            k.dve("bn_stats", s8[:, 0:6], ob[h][:], R=[ob[h]], W=[s8])
            k.dve("bn_aggr", s8[:, 6:8], s8[:, 0:6], R=[s8], W=[s8])
            k.dve("scalar_tensor_tensor", s8[:, 8:9], s8[:, 6:7], s8[:, 6:7], s8[:, 7:8], MUL, ADD, R=[s8], W=[s8])
            k.act(s8[:, 9:10], s8[:, 8:9], AF.Ln, [s8], [s8], bias=RMS_EPS)
            k.act(s8[:, 9:10], s8[:, 9:10], AF.Exp, [s8], [s8], scale=-0.5)
            k.dve("scalar_tensor_tensor", tmp[h][:], ob[h][:], s8[:, 9:10], normw[:], MUL, MUL,
                  R=[ob[h], s8, normw], W=[tmp[h]])
            k.pool("tensor_tensor", oab[:, HV(h)], tmp[h][:], rs[:, HV(h)], MUL, R=[tmp[h], rs], W=[oab])
        pbk = g.PB.next()
        for h in range(H):
            k.transpose(pbk[:, HV(h)], oab[:, HV(h)], g.identb[:], [oab, g.identb], [pbk], inc=(h == H - 1))
        oT = oTr.next()
        k.dve("tensor_copy", oT[:], pbk[:, 0:512].rearrange("p (h t) -> p h t", h=H), R=[pbk], W=[oT])
        k.dma(oTv[:, :, t0:t0 + 128], oT[:], R=[oT], q="pool")
    k.release(m)


def _rel_bucket(dist):
    import math
    dist = np.asarray(dist)
    d = np.maximum(dist, 1).astype(np.float32)
    large = 16 + (np.log(d / np.float32(16)) / np.float32(math.log(128.0)) * np.float32(16)).astype(np.int32)
    large = np.minimum(large, 31)
    return np.where(dist < 16, dist, large)


def _consts():
    idx = np.arange(128)
    c = {}
    c["c_ident"] = np.eye(128, dtype=np.float32)
    c["c_ltri"] = (idx[:, None] <= idx[None, :]).astype(np.float32)
    c["c_lstrict"] = (idx[:, None] < idx[None, :]).astype(np.float32)
    c["c_neg"] = np.where(idx[:, None] > idx[None, :], -30000.0, 0.0).astype(np.float32)
    oh = np.zeros((33, 2560), np.float32)
    d = np.arange(2560) - 511
    bk = _rel_bucket(np.maximum(d, 0))
    oh[bk, np.arange(2560)] = 1.0
    oh[31, :] -= 1.0
    oh[:32, d < 0] = 0.0
    oh[32, d < 0] = -30000.0
    c["c_ohdiff"] = oh
    ohl = np.zeros((3, 33, 384), np.float32)
    for gi, dil in enumerate((1, 4, 16)):
        rel = np.arange(384) - 127
        ok = (rel >= 0) & (rel <= 128)
        bk = _rel_bucket(np.maximum(rel, 0) * dil)
        ohl[gi, bk, np.arange(384)] = 1.0
        ohl[gi, :32, ~ok] = 0.0
        ohl[gi, 32, ~ok] = -30000.0
    c["c_ohdil"] = ohl
    return c


_CACHE = {}


def kernel(**inputs):
    S = 8192
    B = 4
    if "nc" not in _CACHE:
        _CACHE["nc"] = build(S)[0]
    nc = _CACHE["nc"]
    consts = _consts()
    f = lambda a: np.ascontiguousarray(np.asarray(a, dtype=np.float32))
    shared = {
        "rel_bias": f(inputs["rel_bias"]),
        "w_in_even": f(inputs["w_in_even"][0]), "gdn_conv_w": f(inputs["gdn_conv_w"][0]),
        "gdn_a_log": f(inputs["gdn_a_log"]).reshape(1, 6), "gdn_dt_bias": f(inputs["gdn_dt_bias"]).reshape(1, 6),
        "gdn_norm_w": f(inputs["gdn_norm_w"]).reshape(1, 128), "w_out_even": f(inputs["w_out_even"][0]),
        "w_in_odd": f(inputs["w_in_odd"][0]), "diff_lambda": f(inputs["diff_lambda"][0]),
        "diff_norm_w": f(inputs["diff_norm_w"]).reshape(128, 1), "gla_w_gate": f(inputs["gla_w_gate"][0]),
        "gla_b_gate": f(inputs["gla_b_gate"]).reshape(1, 256), "gla_norm_w": f(inputs["gla_norm_w"]).reshape(1, 128),
        "w_out_odd": f(inputs["w_out_odd"][0]), "ffn_w_up": f(inputs["ffn_w_up"]),
        "ffn_conv_w": f(inputs["ffn_conv_w"]), "ffn_conv_b": f(inputs["ffn_conv_b"]).reshape(2, 1, 2 * DFF),
        "ffn_w_down": f(inputs["ffn_w_down"]), "ln_g": f(inputs["ln_g"]).reshape(4, D),
        "ln_b": f(inputs["ln_b"]).reshape(4, D),
    }
    shared.update(consts)
    x = f(inputs["x"])
    in_maps = []
    for c in range(8):
        mp = dict(shared)
        mp["x"] = x[c % B]
        in_maps.append(mp)
    res = run_bass_kernel_spmd(nc, in_maps, core_ids=list(range(8)))
    return np.stack([res.results[b]["y"] for b in range(B)], 0).astype(np.float32)
```

```python
import numpy as np
import concourse.bass as bass
import concourse.mybir as mybir
from concourse.bass_utils import run_bass_kernel_spmd

F32 = mybir.dt.float32
BF16 = mybir.dt.bfloat16
ALU = mybir.AluOpType
AF = mybir.ActivationFunctionType
AX = mybir.AxisListType

SEM_CAP = 30000


class Buf:
    __slots__ = ("name", "t", "w", "r", "excl")

    def __init__(self, name, t=None, excl=False):
        self.name = name
        self.t = t
        self.excl = excl
        self.w = None
        self.r = []

    def __getitem__(self, idx):
        return self.t[idx]


class Rot:
    def __init__(self, bufs):
        self.bufs = bufs
        self.i = 0

    def next(self):
        b = self.bufs[self.i]
        self.i = (self.i + 1) % len(self.bufs)
        return b


class Stream:
    __slots__ = ("sem", "cnt", "id")
    _next = 0

    def __init__(self, sem):
        self.sem = sem
        self.cnt = 0
        self.id = Stream._next
        Stream._next += 1


class KB:
    def __init__(self, nc, n_dma_sems=24):
        self.nc = nc
        self.eng = {"pe": nc.tensor, "act": nc.scalar, "dve": nc.vector,
                    "pool": nc.gpsimd, "sp": nc.sync}
        self._ctx = []
        self._sem_ctx = []
        self.stream = {}
        for e in self.eng:
            self.stream[e] = self._new_stream("s_" + e)
        self.seen = {e: {} for e in self.eng}
        self.dma_pools = {q: [self._new_stream("d%s%d" % (q, i)) for i in range(n)]
                          for q, n in (("sp", 12), ("act", 6), ("pool", 8))}
        self.dma_rr = {"sp": 0, "act": 0, "pool": 0}
        self.ninst = {e: 0 for e in self.eng}
        self.nwait = {e: 0 for e in self.eng}
        self.out_dma = []
        self.prog = {e: [] for e in self.eng}

    def _new_stream(self, name):
        cm = self.nc.semaphore(name + "_%d" % Stream._next)
        sem = cm.__enter__()
        self._sem_ctx.append(cm)
        return Stream(sem)

    def enter(self, cm):
        v = cm.__enter__()
        self._ctx.append(cm)
        return v

    def emit(self):
        with self.nc.Block() as block:
            def mk(e):
                def body(eng):
                    for f in self.prog[e]:
                        f(eng)
                return body
            block.tensor(mk("pe"))
            block.scalar(mk("act"))
            block.vector(mk("dve"))
            block.gpsimd(mk("pool"))
            block.sync(mk("sp"))

    def close(self):
        for cm in reversed(self._ctx):
            cm.__exit__(None, None, None)
        self._ctx = []
        for cm in reversed(self._sem_ctx):
            cm.__exit__(None, None, None)
        self._sem_ctx = []

    def mark(self):
        return len(self._ctx)

    def release(self, mark):
        self.barrier()
        while len(self._ctx) > mark:
            self._ctx.pop().__exit__(None, None, None)

    def rot(self, name, shape, dt, n):
        return Rot([self.sb("%s%d" % (name, i), shape, dt) for i in range(n)])

    def sb(self, name, shape, dt=F32):
        self._uid = getattr(self, "_uid", 0) + 1
        name = "%s_%d" % (name, self._uid)
        t = self.enter(self.nc.sbuf_tensor(name, list(shape), dt))
        return Buf(name, t)

    def ps(self, name, shape, dt=F32):
        t = self.enter(self.nc.psum_tensor(name, list(shape), dt))
        return Buf(name, t, excl=True)

    def _wait(self, e, dep):
        st, c = dep
        if self.seen[e].get(st.id, 0) >= c:
            return
        if st is self.stream[e] and (e == "pe" or c > st.cnt):
            return
        sem = st.sem
        self.prog[e].append(lambda eng, sem=sem, c=c: eng.wait_ge(sem, c))
        self.seen[e][st.id] = c
        self.nwait[e] += 1

    def _deps(self, e, reads, writes):
        for b in reads:
            if b.w is not None:
                self._wait(e, b.w)
        for b in writes:
            if b.w is not None:
                self._wait(e, b.w)
            for d in b.r:
                self._wait(e, d)

    def _record(self, tag, reads, writes):
        for b in reads:
            b.r.append(tag)
            if len(b.r) > 12:
                best = {}
                for st, c in b.r:
                    if st.id not in best or best[st.id][1] < c:
                        best[st.id] = (st, c)
                b.r = list(best.values())
        for b in writes:
            b.w = tag
            b.r = []

    def op(self, e, meth, *args, R=(), W=(), inc=True, **kw):
        if any(b.excl for b in R):
            W = list(W) + [b for b in R if b.excl and b not in W]
            R = [b for b in R if not b.excl]
        self._deps(e, R, W)
        st = self.stream[e]
        if st.cnt >= SEM_CAP and inc:
            st = self.stream[e] = self._new_stream("s_" + e)
        self.ninst[e] += 1
        if inc:
            st.cnt += 1
            sem = st.sem
            self.prog[e].append(lambda eng: getattr(eng, meth)(*args, **kw).then_inc(sem, 1))
            tag = (st, st.cnt)
        else:
            self.prog[e].append(lambda eng: getattr(eng, meth)(*args, **kw))
            tag = (st, st.cnt + 1)
        self._record(tag, R, W)

    def dma(self, out, in_, R=(), W=(), q="sp", is_output=False, **kw):
        self._deps(q, R, W)
        pool = self.dma_pools[q]
        idx = self.dma_rr[q]
        st = pool[idx]
        self.dma_rr[q] = (idx + 1) % len(pool)
        if st.cnt > 0:
            self._wait(q, (st, st.cnt))
        if st.cnt >= SEM_CAP:
            st = pool[idx] = self._new_stream("d" + q)
        st.cnt += 16
        sem = st.sem
        self.prog[q].append(lambda eng: eng.dma_start(out=out, in_=in_, **kw).then_inc(sem, 16))
        self.ninst[q] += 1
        tag = (st, st.cnt)
        self._record(tag, R, W)
        if is_output:
            self.out_dma.append(tag)

    def coll(self, kind, op, groups, ins, outs, R=(), W=()):
        q = "pool"
        self._deps(q, R, W)
        pool = self.dma_pools[q]
        idx = self.dma_rr[q]
        st = pool[idx]
        self.dma_rr[q] = (idx + 1) % len(pool)
        if st.cnt > 0:
            self._wait(q, (st, st.cnt))
        st.cnt += 16
        sem = st.sem
        self.prog[q].append(lambda eng: eng.collective_compute(kind, op, groups, ins, outs).then_inc(sem, 16))
        self.ninst[q] += 1
        self._record((st, st.cnt), R, W)

    def all_dma_streams(self):
        return [st for p in self.dma_pools.values() for st in p]

    def finish(self):
        for tag in self.out_dma:
            self._wait("sp", tag)
        for e in self.eng:
            st = self.stream[e]
            if st.cnt > 0:
                self._wait("sp", (st, st.cnt))
        for st in self.all_dma_streams():
            if st.cnt > 0:
                self._wait("sp", (st, st.cnt))

    def mm(self, out, lhsT, rhs, start, stop, R, W, inc=None):
        if inc is None:
            inc = stop
        self.op("pe", "matmul", out, lhsT, rhs, start=start, stop=stop, R=R, W=W, inc=inc)

    def transpose(self, out, in_, ident, R, W, inc=True):
        self.op("pe", "transpose", out, in_, ident, R=R, W=W, inc=inc)

    def act(self, out, in_, func, R, W, **kw):
        self.op("act", "activation", out, in_, func, R=R, W=W, **kw)

    def dve(self, meth, *args, R=(), W=(), **kw):
        self.op("dve", meth, *args, R=R, W=W, **kw)

    def pool(self, meth, *args, R=(), W=(), **kw):
        self.op("pool", meth, *args, R=R, W=W, **kw)

    def barrier(self):
        tags = []
        for e in self.eng:
            st = self.stream[e]
            if st.cnt > 0:
                tags.append((st, st.cnt))
        for st in self.all_dma_streams():
            if st.cnt > 0:
                tags.append((st, st.cnt))
        for e in self.eng:
            for t in tags:
                self._wait(e, t)


D = 1024
DFF = 2816
ALPHA = float(4 ** 0.25)
LN_EPS = 1e-5
RMS_EPS = 1e-6
MUL = ALU.mult
ADD = ALU.add
SUB = ALU.subtract
POW = ALU.pow


class G:
    pass


def build(S, dbg=False, phases=None):
    nc = bass.Bass("TRN2", target_bir_lowering=False)
    k = KB(nc)
    g = G()
    g.nc, g.k, g.S, g.NB = nc, k, S, S // 512
    skind = "ExternalOutput" if dbg else "Internal"

    def din(name, shape, dt=F32):
        return nc.dram_tensor(name, list(shape), dt, kind="ExternalInput").ap()

    def dsc(name, shape, dt=F32):
        return nc.dram_tensor(name, list(shape), dt, kind=skind).ap()

    g.dsc = dsc
    I = {}
    for name, shape in [
        ("x", [S, D]), ("rel_bias", [32, 4]), ("w_in_even", [D, 5388]), ("gdn_conv_w", [4, 2304]),
        ("gdn_a_log", [1, 6]), ("gdn_dt_bias", [1, 6]), ("gdn_norm_w", [1, 128]),
        ("w_out_even", [D, D]), ("w_in_odd", [D, 3088]), ("diff_lambda", [4, 64]),
        ("diff_norm_w", [128, 1]), ("gla_w_gate", [16, 256]), ("gla_b_gate", [1, 256]),
        ("gla_norm_w", [1, 128]), ("w_out_odd", [D, D]), ("ffn_w_up", [2, D, 2 * DFF]),
        ("ffn_conv_w", [2, 3, 2 * DFF]), ("ffn_conv_b", [2, 1, 2 * DFF]),
        ("ffn_w_down", [2, DFF, D]), ("ln_g", [4, D]), ("ln_b", [4, D]),
        ("c_ident", [128, 128]), ("c_ltri", [128, 128]), ("c_lstrict", [128, 128]),
        ("c_neg", [128, 128]), ("c_ohdiff", [33, 2560]), ("c_ohdil", [3, 33, 384]),
    ]:
        I[name] = din(name, shape)
    g.I = I
    g.y = nc.dram_tensor("y", [S, D], F32, kind="ExternalOutput").ap()
    g.hT = dsc("hT", [D, S], BF16)
    g.hA = dsc("hA", [S, D])
    g.hB = dsc("hB", [S, D])
    g.oT = dsc("oT", [D, S], BF16)
    g.ftd_h = nc.dram_tensor("ftd", [4, 128, 2560], BF16, kind=skind)
    g.ftl_h = nc.dram_tensor("ftl", [12, 128, 384], BF16, kind=skind)
    g.ftd, g.ftl = g.ftd_h.ap(), g.ftl_h.ap()

    g.PS = Rot([k.ps("psf%d" % i, [128, 512], F32) for i in range(6)])
    g.PB = Rot([k.ps("psb%d" % i, [128, 1024], BF16) for i in range(2)])
    g.ident = k.sb("ident", [128, 128])
    g.identb = k.sb("identb", [128, 128], BF16)
    g.ltri = k.sb("ltri", [128, 128])
    g.lstrict = k.sb("lstrict", [128, 128])
    g.neg = k.sb("neg", [128, 128])
    g.ones = k.sb("ones", [128, 128])
    g.nones = k.sb("nones", [128, 128])
    g.onesb = k.sb("onesb", [128, 128], BF16)
    k.dma(g.ident[:], I["c_ident"][:, :], W=[g.ident])
    k.dma(g.ltri[:], I["c_ltri"][:, :], W=[g.ltri])
    k.dma(g.lstrict[:], I["c_lstrict"][:, :], W=[g.lstrict])
    k.dma(g.neg[:], I["c_neg"][:, :], W=[g.neg])
    k.dve("tensor_copy", g.identb[:], g.ident[:], R=[g.ident], W=[g.identb])
    k.dve("memset", g.ones[:], 1.0, W=[g.ones])
    k.dve("memset", g.nones[:], -1.0, W=[g.nones])
    k.dve("memset", g.onesb[:], 1.0, W=[g.onesb])

    ph = phases
    def on(p):
        return ph is None or p in ph

    if on("tab"):
        phase_tables(g)
    if on("x0"):
        phase_x0(g)
    if on("ine"):
        phase_in_even(g)
    if on("gdn"):
        phase_gdn(g)
    if on("dil"):
        phase_dil(g)
    if on("oute"):
        phase_out(g, I["w_out_even"], I["x"], 0)
    if on("ffn0"):
        phase_ffn(g, 0, g.hB)
    if on("ino"):
        phase_in_odd(g)
    if on("diff"):
        phase_diff(g)
    if on("gla"):
        phase_gla(g)
    if on("outo"):
        phase_out(g, I["w_out_odd"], g.hB, 2)
    if on("ffn1"):
        phase_ffn(g, 1, g.y)
    k.finish()
    k.emit()
    k.close()
    return nc, k


def load_w_bf16(g, src2d, dst, nchunk, ncols, cs=None):
    k = g.k
    cs = cs or ncols
    stg = k.rot("wstg", [128, cs], F32, 2)
    i = 0
    for c in range(nchunk):
        for c0 in range(0, ncols, cs):
            w = min(cs, ncols - c0)
            st = stg.next()
            k.dma(st[:, 0:w], src2d[c * 128:(c + 1) * 128, c0:c0 + w], W=[st], q="act" if i % 2 else "sp")
            e = ("dve", "pool", "act")[i % 3]
            if e == "act":
                k.op("act", "copy", dst[:, c, c0:c0 + w], st[:, 0:w], R=[st], W=[dst])
            else:
                k.op(e, "tensor_copy", dst[:, c, c0:c0 + w], st[:, 0:w], R=[st], W=[dst])
            i += 1


def rows_to_cols(g, src_rows, nrow, ncol, dst):
    k = g.k
    m = k.mark()
    st = k.sb("r2c", [8, ncol])
    k.dma(st[0:nrow, :], src_rows, W=[st])
    for c in range(ncol // 128):
        ps = g.PS.next()
        k.transpose(ps[:, 0:nrow], st[0:nrow, c * 128:(c + 1) * 128], g.ident[0:nrow, 0:nrow], [st, g.ident], [ps])
        k.dve("tensor_copy", dst[:, c, 0:nrow], ps[:, 0:nrow], R=[ps], W=[dst])
    k.release(m)


def ln_block(g, pa_pb_fn, hin_src, lnrow, t0, hout_dst, xTn, tt, lnw, final_bufs):
    k = g.k
    gb, bb, hin_r, z_r, y_r, yb_r, sm_r = lnw
    hin = hin_r.next()
    k.dma(hin[:], hin_src[t0:t0 + 128, :], W=[hin])
    pa, pb = pa_pb_fn()
    z = z_r.next()
    k.dve("scalar_tensor_tensor", z[:, 0:512], hin[:, 0:512], ALPHA, pa[:, :], MUL, ADD, R=[hin, pa], W=[z])
    k.dve("scalar_tensor_tensor", z[:, 512:1024], hin[:, 512:1024], ALPHA, pb[:, :], MUL, ADD, R=[hin, pb], W=[z])
    sm = sm_r.next()
    k.dve("bn_stats", sm[:, 0:6], z[:, 0:512], R=[z], W=[sm])
    k.dve("bn_stats", sm[:, 6:12], z[:, 512:1024], R=[z], W=[sm])
    k.dve("bn_aggr", sm[:, 12:14], sm[:, 0:12], R=[sm], W=[sm])
    k.act(sm[:, 14:15], sm[:, 13:14], AF.Ln, [sm], [sm], bias=LN_EPS)
    k.act(sm[:, 14:15], sm[:, 14:15], AF.Exp, [sm], [sm], scale=-0.5)
    k.dve("scalar_tensor_tensor", sm[:, 15:16], sm[:, 12:13], -1.0, sm[:, 14:15], MUL, MUL, R=[sm], W=[sm])
    k.act(z[:], z[:], AF.Identity, [z, sm], [z], scale=sm[:, 14:15], bias=sm[:, 15:16])
    k.dve("tensor_tensor", z[:], z[:], gb[:], MUL, R=[z, gb], W=[z])
    yt = z
    k.pool("tensor_tensor", yt[:], z[:], bb[:], ADD, R=[z, bb], W=[yt])
    k.dma(hout_dst[t0:t0 + 128, :], yt[:], R=[yt], q="pool", is_output=final_bufs)
    if xTn is not None:
        yb = yb_r.next()
        k.op("act", "copy", yb[:], yt[:], R=[yt], W=[yb])
        pbk = g.PB.next()
        for c in range(8):
            k.transpose(pbk[:, c * 128:(c + 1) * 128], yb[:, c * 128:(c + 1) * 128], g.identb[:], [yb, g.identb], [pbk],
                        inc=(c == 7))
        k.dve("tensor_copy", xTn[:, :, tt * 128:(tt + 1) * 128], pbk[:, :].rearrange("p (c t) -> p c t", c=8),
              R=[pbk], W=[xTn])


def ln_stages(g, pa_pb_fn, hin_src, t0, hout_dst, xTn, tt, lnw, final_bufs):
    k = g.k
    gb, bb, hin_r, z_r, y_r, yb_r, sm_r = lnw
    st = {}

    def s0():
        hin = hin_r.next()
        k.dma(hin[:], hin_src[t0:t0 + 128, :], W=[hin])
        pa, pb = pa_pb_fn()
        z = st["z"] = z_r.next()
        sm = st["sm"] = sm_r.next()
        k.dve("scalar_tensor_tensor", z[:, 0:512], hin[:, 0:512], ALPHA, pa[:, :], MUL, ADD, R=[hin, pa], W=[z])
        k.dve("scalar_tensor_tensor", z[:, 512:1024], hin[:, 512:1024], ALPHA, pb[:, :], MUL, ADD, R=[hin, pb], W=[z])
        k.dve("bn_stats", sm[:, 0:6], z[:, 0:512], R=[z], W=[sm])
        k.dve("bn_stats", sm[:, 6:12], z[:, 512:1024], R=[z], W=[sm])
        k.dve("bn_aggr", sm[:, 12:14], sm[:, 0:12], R=[sm], W=[sm])

    def s1():
        sm = st["sm"]
        k.act(sm[:, 14:15], sm[:, 13:14], AF.Ln, [sm], [sm], bias=LN_EPS)
        k.act(sm[:, 14:15], sm[:, 14:15], AF.Exp, [sm], [sm], scale=-0.5)
        k.dve("scalar_tensor_tensor", sm[:, 15:16], sm[:, 12:13], -1.0, sm[:, 14:15], MUL, MUL, R=[sm], W=[sm])

    def s2():
        z, sm = st["z"], st["sm"]
        k.act(z[:], z[:], AF.Identity, [z, sm], [z], scale=sm[:, 14:15], bias=sm[:, 15:16])

    def s3():
        z = st["z"]
        k.dve("tensor_tensor", z[:], z[:], gb[:], MUL, R=[z, gb], W=[z])
        k.pool("tensor_tensor", z[:], z[:], bb[:], ADD, R=[z, bb], W=[z])
        k.dma(hout_dst[t0:t0 + 128, :], z[:], R=[z], q="pool", is_output=final_bufs)

    def s4():
        if xTn is None:
            return
        z = st["z"]
        yb = yb_r.next()
        k.op("act", "copy", yb[:], z[:], R=[z], W=[yb])
        pbk = g.PB.next()
        for c in range(8):
            k.transpose(pbk[:, c * 128:(c + 1) * 128], yb[:, c * 128:(c + 1) * 128], g.identb[:], [yb, g.identb], [pbk],
                        inc=(c == 7))
        k.dve("tensor_copy", xTn[:, :, tt * 128:(tt + 1) * 128], pbk[:, :].rearrange("p (c t) -> p c t", c=8),
              R=[pbk], W=[xTn])

    return [s0, s1, s2, s3, s4]


def ln_alloc(g, lnrow, depth=2):
    k = g.k
    gb = k.sb("lngb", [128, D])
    bb = k.sb("lnbb", [128, D])
    k.dma(gb[:], g.I["ln_g"][lnrow, :].partition_broadcast(128), W=[gb])
    k.dma(bb[:], g.I["ln_b"][lnrow, :].partition_broadcast(128), W=[bb])
    return (gb, bb, k.rot("lnhin", [128, D], F32, 2), k.rot("lnz", [128, D], F32, depth),
            None, k.rot("lnyb", [128, D], BF16, 1 if depth == 2 else 2), k.rot("lnsm", [128, 16], F32, depth))


def phase_x0(g):
    k, S = g.k, g.S
    m = k.mark()
    xin = k.rot("x0in", [128, D], F32, 2)
    xo = k.rot("x0o", [128, 8, 512], BF16, 2)
    for b in range(g.NB):
        xt = xo.next()
        for t in range(4):
            t0 = b * 512 + t * 128
            xi = xin.next()
            k.dma(xi[:], g.I["x"][t0:t0 + 128, :], W=[xi])
            for hf in range(2):
                ps = g.PS.next()
                for c in range(4):
                    cc = hf * 4 + c
                    k.transpose(ps[:, c * 128:(c + 1) * 128], xi[:, cc * 128:(cc + 1) * 128], g.ident[:],
                                [xi, g.ident], [ps], inc=(c == 3))
                k.op("act" if hf else "dve", "copy" if hf else "tensor_copy",
                     xt[:, hf * 4:(hf + 1) * 4, t * 128:(t + 1) * 128],
                     ps[:, :].rearrange("p (c t) -> p c t", c=4), R=[ps], W=[xt])
        k.dma(g.hT.rearrange("(c p) s -> p c s", p=128)[:, :, b * 512:(b + 1) * 512], xt[:], R=[xt], q="pool")
    k.release(m)


def phase_out(g, wout, hin_src, lnrow):
    k, S = g.k, g.S
    m = k.mark()
    W = k.sb("wout", [128, 8, D], BF16)
    m2 = k.mark()
    load_w_bf16(g, wout, W, 8, D)
    k.release(m2)
    lnw = ln_alloc(g, lnrow, depth=6)
    oin = k.rot("oin", [128, 8, 512], BF16, 2)
    xo = k.rot("xTn", [128, 8, 512], BF16, 3)
    oTv = g.oT.rearrange("(c p) s -> p c s", p=128)
    hTv = g.hT.rearrange("(c p) s -> p c s", p=128)
    NT = S // 128
    tiles = {}
    blk = {}
    NST = 5
    for i in range(NT + NST - 1):
        if i < NT:
            b, t = divmod(i, 4)
            if t == 0:
                ot = oin.next()
                k.dma(ot[:], oTv[:, :, b * 512:(b + 1) * 512], W=[ot])
                blk[b] = (ot, xo.next())
            ot, xTn = blk[b]

            def mk(t=t, ot=ot):
                pa, pb = g.PS.next(), g.PS.next()
                for hf, p in enumerate((pa, pb)):
                    for c in range(8):
                        k.mm(p[:, :], ot[:, c, t * 128:(t + 1) * 128], W[:, c, hf * 512:(hf + 1) * 512],
                             c == 0, c == 7, [ot, W], [p])
                return pa, pb
            tiles[i] = ln_stages(g, mk, hin_src, i * 128, g.hA, xTn, t, lnw, False)
        for s in range(NST):
            j = i - s
            if 0 <= j < NT:
                tiles[j][s]()
                if s == NST - 1:
                    del tiles[j]
                    bj, tj = divmod(j, 4)
                    if tj == 3:
                        k.dma(hTv[:, :, bj * 512:(bj + 1) * 512], blk[bj][1][:], R=[blk[bj][1]], q="pool")
                        del blk[bj]
    k.release(m)


def phase_ffn(g, layer, hout):
    k, S = g.k, g.S
    m = k.mark()
    Wu = k.sb("wup", [128, 8, 2 * DFF], BF16)
    Wd = k.sb("wdn", [128, 22, D], BF16)
    cw = k.sb("fcw", [128, 44, 4], F32)
    m2 = k.mark()
    load_w_bf16(g, g.I["ffn_w_up"][layer], Wu, 8, 2 * DFF, cs=2816)
    load_w_bf16(g, g.I["ffn_w_down"][layer], Wd, 22, D)
    k.release(m2)
    m2 = k.mark()
    st = k.sb("fcst", [8, 2 * DFF])
    k.dma(st[0:3, :], g.I["ffn_conv_w"][layer], W=[st])
    k.dma(st[3:4, :], g.I["ffn_conv_b"][layer], W=[st])
    for c in range(44):
        ps = g.PS.next()
        k.transpose(ps[:, 0:4], st[0:4, c * 128:(c + 1) * 128], g.ident[0:4, 0:4], [st, g.ident], [ps])
        k.dve("tensor_copy", cw[:, c, :], ps[:, 0:4], R=[ps], W=[cw])
    k.release(m2)
    final = hout is g.y
    lnw = ln_alloc(g, 2 * layer + 1)
    halo = k.sb("fhalo", [128, 44, 2], F32)
    k.dve("memset", halo[:], 0.0, W=[halo])
    FB = 256
    xin = k.rot("fxT", [128, 8, FB], BF16, 2)
    gT = k.sb("fgT", [128, 22, FB], BF16)
    ur = k.rot("fu", [128, FB + 2], F32, 7)
    yr = k.rot("fy", [128, FB], F32, 10)
    hTv = g.hT.rearrange("(c p) s -> p c s", p=128)
    xT_next = None
    for b in range(S // FB):
        if b == 0:
            xT = xin.next()
            k.dma(xT[:], hTv[:, :, 0:FB], W=[xT])
        else:
            xT = xT_next
        if b + 1 < S // FB:
            xT_next = xin.next()
            k.dma(xT_next[:], hTv[:, :, (b + 1) * FB:(b + 2) * FB], W=[xT_next])
        items = [(j, half) for j in range(22) for half in range(2)]
        us, yvs = {}, {}
        LAG = 2
        for i in range(len(items) + 6):
            if i < len(items):
                j, half = items[i]
                cc = j + 22 * half
                ps = g.PS.next()
                for c in range(8):
                    k.mm(ps[:, 0:FB], Wu[:, c, cc * 128:(cc + 1) * 128], xT[:, c, :], c == 0, c == 7, [Wu, xT], [ps])
                u = ur.next()
                k.pool("tensor_copy", u[:, 0:2], halo[:, cc, :], R=[halo], W=[u])
                k.op("act", "copy", u[:, 2:FB + 2], ps[:, 0:FB], R=[ps], W=[u])
                k.pool("tensor_copy", halo[:, cc, :], u[:, FB:FB + 2], R=[u], W=[halo])
                us[i] = u
            if i >= 2 and i - 2 < len(items):
                j, half = items[i - 2]
                cc = j + 22 * half
                u = us[i - 2]
                yv = yvs[(j, half)] = yr.next()
                if half == 0:
                    k.act(yv[:], u[:, 2:FB + 2], AF.Identity, [u, cw], [yv], scale=cw[:, cc, 2:3], bias=cw[:, cc, 3:4])
                else:
                    k.dve("tensor_scalar", yv[:], u[:, 2:FB + 2], cw[:, cc, 2:3], cw[:, cc, 3:4], MUL, ADD, R=[u, cw], W=[yv])
            if i >= 3 and i - 3 < len(items):
                j, half = items[i - 3]
                cc = j + 22 * half
                u, yv = us[i - 3], yvs[(j, half)]
                k.dve("scalar_tensor_tensor", yv[:], u[:, 1:FB + 1], cw[:, cc, 1:2], yv[:], MUL, ADD, R=[u, cw, yv], W=[yv])
            if i >= 4 and i - 4 < len(items):
                j, half = items[i - 4]
                cc = j + 22 * half
                u, yv = us.pop(i - 4), yvs[(j, half)]
                k.dve("scalar_tensor_tensor", yv[:], u[:, 0:FB], cw[:, cc, 0:1], yv[:], MUL, ADD, R=[u, cw, yv], W=[yv])
            if i >= 5 and i - 5 < len(items) and items[i - 5][1] == 1:
                j = items[i - 5][0]
                k.act(yvs[(j, 0)][:], yvs[(j, 0)][:], AF.Silu, [yvs[(j, 0)]], [yvs[(j, 0)]])
            if i >= 6 and i - 6 < len(items) and items[i - 6][1] == 1:
                j = items[i - 6][0]
                y0, y1 = yvs.pop((j, 0)), yvs.pop((j, 1))
                k.dve("tensor_tensor", gT[:, j, :], y0[:], y1[:], MUL, R=[y0, y1], W=[gT])
        xTn = xT if not final else None
        for t in range(FB // 128):
            def mk(t=t):
                pa, pb = g.PS.next(), g.PS.next()
                for hf, p in enumerate((pa, pb)):
                    for c in range(22):
                        k.mm(p[:, :], gT[:, c, t * 128:(t + 1) * 128], Wd[:, c, hf * 512:(hf + 1) * 512],
                             c == 0, c == 21, [gT, Wd], [p])
                return pa, pb
            ln_block(g, mk, g.hA, 2 * layer + 1, b * FB + t * 128, hout, xTn, t, lnw, final)
        if xTn is not None:
            k.dma(hTv[:, :, b * FB:(b + 1) * FB], xTn[:], R=[xTn], q="pool")
    k.release(m)


def phase_tables(g):
    k, nc = g.k, g.nc
    m = k.mark()
    rb = k.sb("rb", [33, 4])
    k.dve("memset", rb[:], 1.0, W=[rb])
    k.dma(rb[0:32, :], g.I["rel_bias"][:, :], W=[rb])
    ohd = k.sb("ohd", [33, 2560])
    k.dma(ohd[:], g.I["c_ohdiff"][:, :], W=[ohd])
    ohl = k.sb("ohl", [33, 3, 384])
    k.dma(ohl[:], g.I["c_ohdil"].rearrange("g b x -> b g x"), W=[ohl])
    lhs_r = k.rot("tlhs", [33, 128], F32, 2)
    tb_r = k.rot("ttb", [128, 2560], BF16, 2)
    tl_r = k.rot("ttl", [128, 384], BF16, 2)
    for h in range(4):
        lhs = lhs_r.next()
        k.dve("tensor_scalar_mul", lhs[:], g.ones[0:33, :], rb[:, h:h + 1], R=[g.ones, rb], W=[lhs])
        tb = tb_r.next()
        for x0 in range(0, 2560, 512):
            ps = g.PS.next()
            k.mm(ps[:, :], lhs[:], ohd[:, x0:x0 + 512], True, True, [lhs, ohd], [ps])
            k.act(tb[:, x0:x0 + 512], ps[:, :], AF.Exp, [ps], [tb])
        k.dma(g.ftd[h], tb[:], R=[tb], q="pool")
        for gi in range(3):
            tl = tl_r.next()
            ps = g.PS.next()
            k.mm(ps[:, 0:384], lhs[:], ohl[:, gi, :], True, True, [lhs, ohl], [ps])
            k.act(tl[:], ps[:, 0:384], AF.Exp, [ps], [tl])
            k.dma(g.ftl[gi * 4 + h], tl[:], R=[tl], q="pool")
    k.release(m)


def phase_in_even(g):
    k, S, nc = g.k, g.S, g.nc
    dsc = g.dsc
    g.gqT = dsc("gqT", [768, S], BF16)
    g.gkT = dsc("gkT", [768, S], BF16)
    g.gk = dsc("gk", [S, 768], BF16)
    g.gv = dsc("gv", [S, 768], BF16)
    g.gz = dsc("gz", [S, 768], F32)
    g.gg = dsc("gg", [S, 6], F32)
    g.gbeta = dsc("gbeta", [S, 6], F32)
    g.dq = dsc("dq", [768, S], BF16)
    g.dk = dsc("dk", [768, S], BF16)
    g.dvv = dsc("dvv", [S, 768], BF16)
    m = k.mark()
    W = k.sb("wine", [128, 8, 5388], BF16)
    m2 = k.mark()
    load_w_bf16(g, g.I["w_in_even"], W, 8, 5388, cs=2694)
    k.release(m2)
    cw = k.sb("gcw", [128, 18, 4], F32)
    rows_to_cols(g, g.I["gdn_conv_w"], 4, 2304, cw)
    dtb = k.sb("dtb", [128, 6])
    nea = k.sb("nea", [128, 6])
    k.dma(dtb[:], g.I["gdn_dt_bias"][0, :].partition_broadcast(128), W=[dtb])
    k.dma(nea[:], g.I["gdn_a_log"][0, :].partition_broadcast(128), W=[nea])
    k.act(nea[:], nea[:], AF.Exp, [nea], [nea])
    k.dve("tensor_scalar_mul", nea[:], nea[:], -1.0, R=[nea], W=[nea])
    halo = k.sb("ghalo", [128, 18, 3], F32)
    k.dve("memset", halo[:], 0.0, W=[halo])
    xin = k.rot("exT", [128, 8, 512], BF16, 2)
    ur = k.rot("eu", [128, 515], F32, 5)
    yr = k.rot("ey", [128, 512], F32, 4)
    rsr = k.rot("ers", [128, 512], F32, 2)
    sqr = k.rot("esq", [128, 512], BF16, 2)
    qnr = k.rot("eqn", [128, 512], BF16, 3)
    tkr = k.rot("etk", [128, 4, 128], BF16, 2)
    ztr = k.rot("ezt", [128, 768], F32, 2)
    dvr = k.rot("edv", [128, 768], BF16, 2)
    smr = k.rot("esm", [128, 12], F32, 2)
    ggr = k.rot("egg", [128, 4, 6], F32, 2)
    ber = k.rot("ebe", [128, 4, 6], F32, 2)
    ybig = k.sb("eybig", [128, 12, 512], F32)
    hTv = g.hT.rearrange("(c p) s -> p c s", p=128)

    def to_tok(qn, dst, hh):
        pbk = g.PB.next()
        for t in range(4):
            k.transpose(pbk[:, t * 128:(t + 1) * 128], qn[:, t * 128:(t + 1) * 128], g.identb[:],
                        [qn, g.identb], [pbk], inc=(t == 3))
        tk = tkr.next()
        k.dve("tensor_copy", tk[:], pbk[:, 0:512].rearrange("p (t d) -> p t d", t=4), R=[pbk], W=[tk])
        k.dma(dst[to_tok.s0:to_tok.s0 + 512, hh * 128:(hh + 1) * 128].rearrange("(t p) d -> p t d", p=128), tk[:],
              R=[tk], q="pool")

    xT_next = None
    for b in range(g.NB):
        s0 = b * 512
        to_tok.s0 = s0
        if b == 0:
            xT = xin.next()
            k.dma(xT[:], hTv[:, :, 0:512], W=[xT])
        else:
            xT = xT_next
        if b + 1 < g.NB:
            xT_next = xin.next()
            k.dma(xT_next[:], hTv[:, :, s0 + 512:s0 + 1024], W=[xT_next])
        us = {}
        LAG = 2
        for i in range(18 + LAG):
            if i < 18:
                cc = i
                ps = g.PS.next()
                for c in range(8):
                    k.mm(ps[:, :], W[:, c, cc * 128:(cc + 1) * 128], xT[:, c, :], c == 0, c == 7, [W, xT], [ps])
                u = ur.next()
                k.pool("tensor_copy", u[:, 0:3], halo[:, cc, :], R=[halo], W=[u])
                k.op("act", "copy", u[:, 3:515], ps[:, :], R=[ps], W=[u])
                k.pool("tensor_copy", halo[:, cc, :], u[:, 512:515], R=[u], W=[halo])
                us[i] = u
            if i >= LAG:
                cc = i - LAG
                u = us.pop(cc)
                y = yr.next()
                if cc % 2:
                    k.act(y[:], u[:, 3:515], AF.Identity, [u, cw], [y], scale=cw[:, cc, 3:4])
                else:
                    k.dve("tensor_scalar_mul", y[:], u[:, 3:515], cw[:, cc, 3:4], R=[u, cw], W=[y])
                k.dve("scalar_tensor_tensor", y[:], u[:, 2:514], cw[:, cc, 2:3], y[:], MUL, ADD, R=[u, cw, y], W=[y])
                k.dve("scalar_tensor_tensor", y[:], u[:, 1:513], cw[:, cc, 1:2], y[:], MUL, ADD, R=[u, cw, y], W=[y])
                k.dve("scalar_tensor_tensor", y[:], u[:, 0:512], cw[:, cc, 0:1], y[:], MUL, ADD, R=[u, cw, y], W=[y])
                if cc < 12:
                    k.act(ybig[:, cc, :], y[:], AF.Silu, [y], [ybig])
                else:
                    hh = cc - 12
                    qn = qnr.next()
                    k.act(qn[:], y[:], AF.Silu, [y], [qn])
                    to_tok(qn, g.gv, hh)
        for cc in range(12):
            sq = sqr.next()
            k.dve("tensor_tensor", sq[:], ybig[:, cc, :], ybig[:, cc, :], MUL, R=[ybig], W=[sq])
            ps2 = g.PS.next()
            k.mm(ps2[:, :], g.onesb[:], sq[:], True, True, [g.onesb, sq], [ps2])
            rs = rsr.next()
            k.act(rs[:], ps2[:, :], AF.Ln, [ps2], [rs], bias=RMS_EPS)
            k.act(rs[:], rs[:], AF.Exp, [rs], [rs], scale=-0.5)
            sc = float(128 ** -0.5) if cc < 6 else 1.0
            qn = qnr.next()
            k.dve("scalar_tensor_tensor", qn[:], ybig[:, cc, :], sc, rs[:], MUL, MUL, R=[ybig, rs], W=[qn])
            dst = g.gqT if cc < 6 else g.gkT
            hh = cc % 6
            k.dma(dst[hh * 128:(hh + 1) * 128, s0:s0 + 512], qn[:], R=[qn], q="pool")
            if cc >= 6:
                to_tok(qn, g.gk, hh)
        gg = ggr.next()
        be = ber.next()
        for t in range(4):
            t0 = s0 + t * 128
            zt = ztr.next()
            dv = dvr.next()
            for hf in range(2):
                ps = g.PS.next()
                for c in range(8):
                    k.mm(ps[:, 0:384], xT[:, c, t * 128:(t + 1) * 128], W[:, c, 2304 + hf * 384:2304 + (hf + 1) * 384],
                         c == 0, c == 7, [W, xT], [ps])
                k.act(zt[:, hf * 384:(hf + 1) * 384], ps[:, 0:384], AF.Silu, [ps], [zt])
                ps = g.PS.next()
                for c in range(8):
                    k.mm(ps[:, 0:384], xT[:, c, t * 128:(t + 1) * 128], W[:, c, 4620 + hf * 384:4620 + (hf + 1) * 384],
                         c == 0, c == 7, [W, xT], [ps])
                k.dve("tensor_copy", dv[:, hf * 384:(hf + 1) * 384], ps[:, 0:384], R=[ps], W=[dv])
            k.dma(g.gz[t0:t0 + 128, :], zt[:], R=[zt], q="pool")
            k.dma(g.dvv[t0:t0 + 128, :], dv[:], R=[dv], q="pool")
            ps = g.PS.next()
            for c in range(8):
                k.mm(ps[:, 0:12], xT[:, c, t * 128:(t + 1) * 128], W[:, c, 3072:3084], c == 0, c == 7, [W, xT], [ps])
            sm = smr.next()
            k.act(sm[:, 0:6], ps[:, 0:6], AF.Exp, [ps], [sm], scale=-1.0)
            k.dve("tensor_scalar_add", sm[:, 0:6], sm[:, 0:6], 1.0, R=[sm], W=[sm])
            k.dve("reciprocal", be[:, t, :], sm[:, 0:6], R=[sm], W=[be])
            k.dve("tensor_tensor", sm[:, 6:12], ps[:, 6:12], dtb[:], ADD, R=[ps, dtb], W=[sm])
            k.act(sm[:, 6:12], sm[:, 6:12], AF.Exp, [sm], [sm])
            k.act(sm[:, 6:12], sm[:, 6:12], AF.Ln, [sm], [sm], bias=1.0)
            k.dve("tensor_tensor", gg[:, t, :], sm[:, 6:12], nea[:], MUL, R=[sm, nea], W=[gg])
        k.dma(g.gg[s0:s0 + 512, :].rearrange("(t p) h -> p t h", p=128), gg[:], R=[gg], q="pool")
        k.dma(g.gbeta[s0:s0 + 512, :].rearrange("(t p) h -> p t h", p=128), be[:], R=[be], q="pool")
        for cc in range(12):
            c0 = 3084 + cc * 128
            ps = g.PS.next()
            for c in range(8):
                k.mm(ps[:, :], W[:, c, c0:c0 + 128], xT[:, c, :], c == 0, c == 7, [W, xT], [ps])
            qn = qnr.next()
            if cc % 2:
                k.op("act", "copy", qn[:], ps[:, :], R=[ps], W=[qn])
            else:
                k.dve("tensor_copy", qn[:], ps[:, :], R=[ps], W=[qn])
            dst = g.dq if cc < 6 else g.dk
            hh = cc % 6
            k.dma(dst[hh * 128:(hh + 1) * 128, s0:s0 + 512], qn[:], R=[qn], q="pool")
    k.release(m)


def phase_gdn(g):
    k, S, nc = g.k, g.S, g.nc
    m = k.mark()
    H = 6
    normw = k.sb("gnw", [128, 128])
    k.dma(normw[:], g.I["gdn_norm_w"][0, :].partition_broadcast(128), W=[normw])
    St = [k.sb("gS%d" % h, [128, 128], F32) for h in range(H)]
    Sb = [k.sb("gSb%d" % h, [128, 128], BF16) for h in range(H)]
    for h in range(H):
        k.dve("memset", St[h][:], 0.0, W=[St[h]])
        k.dve("memset", Sb[h][:], 0.0, W=[Sb[h]])
    qTr = k.rot("gqT", [128, H, 128], BF16, 2)
    kTr = k.rot("gkT", [128, H, 128], BF16, 2)
    ktr = k.rot("gkt", [128, 768], BF16, 2)
    vtr = k.rot("gvt", [128, 768], BF16, 2)
    zsr = k.rot("gzs", [128, 768], F32, 2)
    ggr = k.rot("ggg", [128, 6], F32, 2)
    ber = k.rot("gbe", [128, 6], F32, 2)
    smr = k.rot("gsm", [128, 32], F32, 2)
    oabr = k.rot("goab", [128, 768], BF16, 2)
    oTr = k.rot("goT", [128, H, 128], BF16, 2)

    def per_head(name, dt=F32, n=1):
        return [[k.sb("%s%d_%d" % (name, i, h), [128, 128], dt) for h in range(H)] for i in range(n)]

    P = per_head("gP", F32, 2)
    PT = per_head("gPT", F32, 2)
    Tt = per_head("gTt", F32, 2)
    Lg = per_head("gLg")[0]
    GT = per_head("gGT")[0]
    GTs = per_head("gGTs")[0]
    Ttb = per_head("gTtb", BF16)[0]
    Ut = per_head("gUt")[0]
    Ke = per_head("gKe", BF16)[0]
    WtT = per_head("gWtT", BF16)[0]
    tmp = per_head("gtmp")[0]
    dlb = per_head("gdlb", BF16)[0]
    asb = per_head("gasb")[0]
    QKG = per_head("gQKG", BF16)[0]
    ob = per_head("gob")[0]
    Kd = per_head("gKd", BF16)[0]
    st8 = [k.sb("gst8_%d" % h, [128, 12]) for h in range(H)]
    gqTv = g.gqT.rearrange("(h d) s -> d h s", d=128)
    gkTv = g.gkT.rearrange("(h d) s -> d h s", d=128)
    oTv = g.oT[0:768, :].rearrange("(h d) s -> d h s", d=128)
    HS = lambda h: slice(h * 128, (h + 1) * 128)
    for n in range(S // 128):
        t0 = n * 128
        qT, kT, kt, vt, zs, gg, be = qTr.next(), kTr.next(), ktr.next(), vtr.next(), zsr.next(), ggr.next(), ber.next()
        k.dma(qT[:], gqTv[:, :, t0:t0 + 128], W=[qT])
        k.dma(kT[:], gkTv[:, :, t0:t0 + 128], W=[kT])
        k.dma(kt[:], g.gk[t0:t0 + 128, :], W=[kt])
        k.dma(vt[:], g.gv[t0:t0 + 128, :], W=[vt])
        k.dma(zs[:], g.gz[t0:t0 + 128, :], W=[zs])
        k.dma(gg[:], g.gg[t0:t0 + 128, :], W=[gg])
        k.dma(be[:], g.gbeta[t0:t0 + 128, :], W=[be])
        sm = smr.next()
        ps = g.PS.next()
        k.mm(ps[:, 0:6], g.ltri[:], gg[:], True, True, [g.ltri, gg], [ps])
        k.mm(ps[:, 8:14], g.ones[:], gg[:], True, True, [g.ones, gg], [ps])
        k.act(sm[:, 0:6], ps[:, 0:6], AF.Exp, [ps], [sm])
        k.act(sm[:, 6:12], ps[:, 8:14], AF.Exp, [ps], [sm])
        k.dve("tensor_copy", sm[:, 18:24], ps[:, 0:6], R=[ps], W=[sm])
        k.dve("tensor_tensor", sm[:, 12:18], ps[:, 8:14], sm[:, 18:24], SUB, R=[ps, sm], W=[sm])
        k.act(sm[:, 12:18], sm[:, 12:18], AF.Exp, [sm], [sm])
        k.dve("tensor_scalar_mul", sm[:, 24:30], be[:], -1.0, R=[be], W=[sm])
        HR = range(H)
        psDs, psKs, psTs = {}, {}, {}
        for h in HR:
            k.pool("tensor_scalar_mul", Lg[h][:], g.ltri[:], gg[:, h:h + 1], R=[g.ltri, gg], W=[Lg[h]])
        for h in HR:
            psD = psDs[h] = g.PS.next()
            k.mm(psD[:, 0:128], g.ones[:], Lg[h][:], True, False, [g.ones, Lg[h]], [psD])
            k.mm(psD[:, 0:128], Lg[h][:], g.nones[:], False, True, [g.nones, Lg[h]], [psD])
            if h >= 2:
                hh = h - 2
                k.dve("tensor_tensor", GT[hh][:], psDs[hh][:, 0:128], g.neg[:], ADD, R=[psDs[hh], g.neg], W=[GT[hh]])
        for hh in (H - 2, H - 1):
            k.dve("tensor_tensor", GT[hh][:], psDs[hh][:, 0:128], g.neg[:], ADD, R=[psDs[hh], g.neg], W=[GT[hh]])
        for h in HR:
            k.act(GT[h][:], GT[h][:], AF.Exp, [GT[h]], [GT[h]])
        for h in HR:
            k.pool("tensor_tensor", GTs[h][:], GT[h][:], g.lstrict[:], MUL, R=[GT[h], g.lstrict], W=[GTs[h]])
        for h in HR:
            psK = g.PS.next()
            k.mm(psK[:, 0:128], kT[:, h, :], kT[:, h, :], True, True, [kT], [psK])
            k.dve("scalar_tensor_tensor", PT[0][h][:], psK[:, 0:128], sm[:, 24 + h:25 + h], GTs[h][:], MUL, MUL,
                  R=[psK, sm, GTs[h]], W=[PT[0][h]])
        for h in HR:
            k.pool("tensor_tensor", Tt[0][h][:], PT[0][h][:], g.ident[:], ADD, R=[PT[0][h], g.ident], W=[Tt[0][h]])
        for h in HR:
            psT = g.PS.next()
            k.transpose(psT[:, 0:128], PT[0][h][:], g.ident[:], [PT[0][h], g.ident], [psT])
            k.op("act", "copy", P[0][h][:], psT[:, 0:128], R=[psT], W=[P[0][h]])
        for s in range(1, 8):
            a, b = (s - 1) % 2, s % 2
            for h in HR:
                if s <= 6:
                    psP = g.PS.next()
                    k.mm(psP[:, 0:128], PT[a][h][:], P[a][h][:], True, True, [PT[a][h], P[a][h]], [psP])
                    k.op("act", "copy", P[b][h][:], psP[:, 0:128], R=[psP], W=[P[b][h]])
                if s < 6:
                    psPT = g.PS.next()
                    k.mm(psPT[:, 0:128], P[a][h][:], PT[a][h][:], True, True, [PT[a][h], P[a][h]], [psPT])
                    k.op("act", "copy", PT[b][h][:], psPT[:, 0:128], R=[psPT], W=[PT[b][h]])
                if s >= 2:
                    psTt = g.PS.next()
                    k.mm(psTt[:, 0:128], P[a][h][:], Tt[b][h][:], True, True, [P[a][h], Tt[b][h]], [psTt])
                    k.dve("tensor_tensor", Tt[a][h][:], Tt[b][h][:], psTt[:, 0:128], ADD, R=[Tt[b][h], psTt], W=[Tt[a][h]])
        TF = Tt[0]
        oab = oabr.next()
        for h in range(H):
            k.op("act", "copy", Ttb[h][:], TF[h][:], R=[TF[h]], W=[Ttb[h]])
            k.dve("tensor_scalar_mul", Ke[h][:], kt[:, HS(h)], sm[:, h:h + 1], R=[kt, sm], W=[Ke[h]])
            k.pool("tensor_scalar_mul", Kd[h][:], kt[:, HS(h)], sm[:, 12 + h:13 + h], R=[kt, sm], W=[Kd[h]])
        for h in range(H):
            psU = g.PS.next()
            k.mm(psU[:, 0:128], Ttb[h][:], vt[:, HS(h)], True, True, [Ttb[h], vt], [psU])
            k.op("act", "copy", Ut[h][:], psU[:, 0:128], R=[psU], W=[Ut[h]])
            psW = g.PS.next()
            k.mm(psW[:, 0:128], Ke[h][:], Ttb[h][:], True, True, [Ke[h], Ttb[h]], [psW])
            k.dve("tensor_copy", WtT[h][:], psW[:, 0:128], R=[psW], W=[WtT[h]])
        for h in range(H):
            psWS = g.PS.next()
            k.mm(psWS[:, 0:128], WtT[h][:], Sb[h][:], True, True, [WtT[h], Sb[h]], [psWS])
            k.dve("tensor_tensor", tmp[h][:], Ut[h][:], psWS[:, 0:128], SUB, R=[Ut[h], psWS], W=[tmp[h]])
            k.act(dlb[h][:], tmp[h][:], AF.Identity, [tmp[h], be], [dlb[h]], scale=be[:, h:h + 1])
            psQS = g.PS.next()
            k.mm(psQS[:, 0:128], qT[:, h, :], Sb[h][:], True, True, [qT, Sb[h]], [psQS])
            k.act(asb[h][:], psQS[:, 0:128], AF.Identity, [psQS, sm], [asb[h]], scale=sm[:, h:h + 1])
            psQK = g.PS.next()
            k.mm(psQK[:, 0:128], kT[:, h, :], qT[:, h, :], True, True, [kT, qT], [psQK])
            k.dve("tensor_tensor", QKG[h][:], psQK[:, 0:128], GT[h][:], MUL, R=[psQK, GT[h]], W=[QKG[h]])
        for h in range(H):
            psIN = g.PS.next()
            k.mm(psIN[:, 0:128], QKG[h][:], dlb[h][:], True, True, [QKG[h], dlb[h]], [psIN])
            k.dve("tensor_tensor", ob[h][:], psIN[:, 0:128], asb[h][:], ADD, R=[psIN, asb[h]], W=[ob[h]])
            psSU = g.PS.next()
            k.mm(psSU[:, 0:128], Kd[h][:], dlb[h][:], True, True, [Kd[h], dlb[h]], [psSU])
            k.dve("scalar_tensor_tensor", St[h][:], St[h][:], sm[:, 6 + h:7 + h], psSU[:, 0:128], MUL, ADD,
                  R=[St[h], sm, psSU, Sb[h]], W=[St[h]])
            k.op("act", "copy", Sb[h][:], St[h][:], R=[St[h]], W=[Sb[h]])
        for h in range(H):
            s8 = st8[h]
            k.dve("bn_stats", s8[:, 0:6], ob[h][:], R=[ob[h]], W=[s8])
            k.dve("bn_aggr", s8[:, 6:8], s8[:, 0:6], R=[s8], W=[s8])
            k.dve("scalar_tensor_tensor", s8[:, 8:9], s8[:, 6:7], s8[:, 6:7], s8[:, 7:8], MUL, ADD, R=[s8], W=[s8])
            k.act(s8[:, 9:10], s8[:, 8:9], AF.Ln, [s8], [s8], bias=RMS_EPS)
            k.act(s8[:, 9:10], s8[:, 9:10], AF.Exp, [s8], [s8], scale=-0.5)
            k.dve("scalar_tensor_tensor", tmp[h][:], ob[h][:], s8[:, 9:10], normw[:], MUL, MUL,
                  R=[ob[h], s8, normw], W=[tmp[h]])
            k.pool("tensor_tensor", oab[:, HS(h)], tmp[h][:], zs[:, HS(h)], MUL, R=[tmp[h], zs], W=[oab])
        pbk = g.PB.next()
        for h in range(H):
            k.transpose(pbk[:, HS(h)], oab[:, HS(h)], g.identb[:], [oab, g.identb], [pbk], inc=(h == H - 1))
        oT = oTr.next()
        k.dve("tensor_copy", oT[:], pbk[:, 0:768].rearrange("p (h t) -> p h t", h=H), R=[pbk], W=[oT])
        k.dma(oTv[:, :, t0:t0 + 128], oT[:], R=[oT], q="pool")
    k.release(m)


def phase_dil(g):
    k, S, nc = g.k, g.S, g.nc
    m = k.mark()
    U = min(2048, S)
    DIL = (1, 4, 16)
    ft = k.sb("dft", [128, 12, 256], BF16)
    for i in range(12):
        src = bass.AP(tensor=g.ftl_h, offset=i * 128 * 384 + 127, ap=[[383, 128], [1, 256]])
        k.dma(ft[:, i, :], src, W=[ft])
    Oacc = k.sb("dOacc", [128, 2, U], F32)
    Lacc = k.sb("dLacc", [128, 2, U], F32)
    ob = k.sb("dob", [128, 2, U], BF16)
    kbuf = [k.rot("dkT%d" % gi, [128, 2, 128 * d], BF16, 2) for gi, d in enumerate(DIL)]
    vbuf = [k.rot("dvt%d" % gi, [128, d, 256], BF16, 2) for gi, d in enumerate(DIL)]
    qbuf = [k.rot("dqT%d" % gi, [128, 4, 128 * d], BF16, 1) for gi, d in enumerate(DIL)]
    for qb_ in qbuf:
        k.dve("memset", qb_.bufs[0][:], 0.0, W=[qb_.bufs[0]])
    ptr = k.rot("dpt", [128, 512], BF16, 4)
    scb = Rot(g.PS.bufs[4:6])
    accb = Rot(g.PS.bufs[0:4])
    oTv = g.oT[768:1024, :].rearrange("(pr p) s -> p pr s", p=128)
    for u in range(S // U):
        u0 = u * U
        for gi, d in enumerate(DIL):
            SBT = 128 * d
            dqg = g.dq[gi * 256:(gi + 1) * 256, :]
            dkv = g.dk[gi * 256:(gi + 1) * 256, :].rearrange("(pr p) s -> p pr s", p=128)
            kprev = vprev = None
            if u0 > 0:
                kprev, vprev = kbuf[gi].next(), vbuf[gi].next()
                p0 = u0 - SBT
                k.dma(kprev[:], dkv[:, :, p0:p0 + SBT], W=[kprev])
                k.dma(vprev[:], g.dvv[p0:p0 + SBT, gi * 256:(gi + 1) * 256].rearrange("(m r) c -> m r c", r=d), W=[vprev])
            for sbi in range(U // SBT):
                s0 = u0 + sbi * SBT
                kc, vc, qc = kbuf[gi].next(), vbuf[gi].next(), qbuf[gi].next()
                k.dma(kc[:], dkv[:, :, s0:s0 + SBT], W=[kc])
                for h in range(4):
                    k.dma(qc[(h % 2) * 64:(h % 2 + 1) * 64, h, :], dqg[h * 64:(h + 1) * 64, s0:s0 + SBT], W=[qc])
                k.dma(vc[:], g.dvv[s0:s0 + SBT, gi * 256:(gi + 1) * 256].rearrange("(m r) c -> m r c", r=d), W=[vc])
                blocks = ([(kprev, vprev, 128)] if kprev is not None else []) + [(kc, vc, 0)]
                nb = len(blocks)
                pts = {}
                for i in range(d + 1):
                    if i < d:
                        r = i
                        cur = []
                        for bi, (kk, vv, off) in enumerate(blocks):
                            psS = scb.next()
                            for h in range(4):
                                k.mm(psS[:, h * 128:(h + 1) * 128], kk[:, h // 2, r:r + 127 * d + 1:d],
                                     qc[:, h, r:r + 127 * d + 1:d], True, True, [kk, qc], [psS], inc=(h == 3))
                            pt = ptr.next()
                            k.act(pt[:], psS[:, :], AF.Exp, [psS], [pt], scale=0.125)
                            ptv = pt[:].rearrange("p (h q) -> p h q", h=4)
                            k.dve("tensor_tensor", ptv, ptv, ft[:, gi * 4:(gi + 1) * 4, off:off + 128], MUL, R=[pt, ft], W=[pt])
                            cur.append(pt)
                        pts[i] = cur
                    if i >= 1:
                        r = i - 1
                        cur = pts.pop(r)
                        psO, psL = accb.next(), accb.next()
                        for h in range(4):
                            pr = h // 2
                            for bi, (kk, vv, off) in enumerate(blocks):
                                last_i = (h == 3 and bi == nb - 1)
                                k.mm(psO[:, h * 128:(h + 1) * 128], vv[:, r, pr * 128:(pr + 1) * 128],
                                     cur[bi][:, h * 128:(h + 1) * 128], bi == 0, bi == nb - 1, [vv, cur[bi]], [psO], inc=last_i)
                        for h in range(4):
                            for bi, (kk, vv, off) in enumerate(blocks):
                                last_i = (h == 3 and bi == nb - 1)
                                k.mm(psL[:, h * 128:(h + 1) * 128], g.onesb[:], cur[bi][:, h * 128:(h + 1) * 128],
                                     bi == 0, bi == nb - 1, [g.onesb, cur[bi]], [psL], inc=last_i)
                        lo = sbi * SBT + r
                        for par in range(2):
                            hp = slice(par * 64, (par + 1) * 64)
                            osl = Oacc[hp, :, lo:lo + 127 * d + 1:d]
                            lsl = Lacc[hp, :, lo:lo + 127 * d + 1:d]
                            pso = psO[hp, :].rearrange("p (h q) -> p h q", h=4)[:, par::2, :]
                            psl = psL[hp, :].rearrange("p (h q) -> p h q", h=4)[:, par::2, :]
                            if gi == 0:
                                k.op("act", "copy", osl, pso, R=[psO], W=[Oacc])
                                k.dve("tensor_copy", lsl, psl, R=[psL], W=[Lacc])
                            else:
                                k.dve("tensor_tensor", osl, osl, pso, ADD, R=[psO, Oacc], W=[Oacc])
                                k.dve("tensor_tensor", lsl, lsl, psl, ADD, R=[psL, Lacc], W=[Lacc])
                kprev, vprev = kc, vc
        for pr in range(2):
            k.dve("reciprocal", Lacc[:, pr, :], Lacc[:, pr, :], R=[Lacc], W=[Lacc])
            k.dve("tensor_tensor", ob[:, pr, :], Oacc[:, pr, :], Lacc[:, pr, :], MUL, R=[Oacc, Lacc], W=[ob])
        k.dma(oTv[:, :, u0:u0 + U], ob[:], R=[ob], q="pool")
    k.release(m)


def phase_in_odd(g):
    k, S, nc = g.k, g.S, g.nc
    dsc = g.dsc
    g.cq = dsc("cq", [512, S], BF16)
    g.ck = dsc("ck", [512, S], BF16)
    g.cv = dsc("cv", [S, 512], BF16)
    g.lqT = dsc("lqT", [256, S], F32)
    g.lkT = dsc("lkT", [256, S], F32)
    g.lk = dsc("lk", [S, 256], F32)
    g.lv = dsc("lv", [S, 512], BF16)
    g.lr = dsc("lr", [S, 512], F32)
    g.la = dsc("la", [S, 256], F32)
    m = k.mark()
    W = k.sb("wino", [128, 8, 3088], BF16)
    m2 = k.mark()
    load_w_bf16(g, g.I["w_in_odd"], W, 8, 3088)
    k.release(m2)
    wg32 = k.sb("wg32", [128, 256])
    wg = k.sb("wg", [128, 256], BF16)
    k.dve("memset", wg32[:], 0.0, W=[wg32])
    k.dma(wg32[112:128, :], g.I["gla_w_gate"][:, :], W=[wg32])
    k.dve("tensor_copy", wg[:], wg32[:], R=[wg32], W=[wg])
    bgb = k.sb("bgb", [128, 256])
    k.dma(bgb[:], g.I["gla_b_gate"][0, :].partition_broadcast(128), W=[bgb])
    xin = k.rot("oxT", [128, 8, 512], BF16, 2)
    fbr = k.rot("ofb", [128, 512], BF16, 3)
    ffr = k.rot("off", [128, 512], F32, 3)
    tbr = k.rot("otb", [128, 512], BF16, 3)
    tfr = k.rot("otf", [128, 512], F32, 3)
    gdr = k.rot("ogd", [128, 512], BF16, 2)
    hTv = g.hT.rearrange("(c p) s -> p c s", p=128)
    xT_next = None
    for b in range(g.NB):
        s0 = b * 512
        if b == 0:
            xT = xin.next()
            k.dma(xT[:], hTv[:, :, 0:512], W=[xT])
        else:
            xT = xT_next
        if b + 1 < g.NB:
            xT_next = xin.next()
            k.dma(xT_next[:], hTv[:, :, s0 + 512:s0 + 1024], W=[xT_next])

        def fm(c0, ncol):
            ps = g.PS.next()
            for c in range(8):
                k.mm(ps[0:ncol, :], W[:, c, c0:c0 + ncol], xT[:, c, :], c == 0, c == 7, [W, xT], [ps])
            return ps

        def tm(t, c0, ncol):
            ps = g.PS.next()
            for c in range(8):
                k.mm(ps[:, 0:ncol], xT[:, c, t * 128:(t + 1) * 128], W[:, c, c0:c0 + ncol], c == 0, c == 7, [W, xT], [ps])
            return ps

        for cc in range(8):
            ps = fm(cc * 128, 128)
            fb = fbr.next()
            if cc % 2:
                k.op("act", "copy", fb[:], ps[:, :], R=[ps], W=[fb])
            else:
                k.dve("tensor_copy", fb[:], ps[:, :], R=[ps], W=[fb])
            dst = g.cq if cc < 4 else g.ck
            k.dma(dst[(cc % 4) * 128:(cc % 4 + 1) * 128, s0:s0 + 512], fb[:], R=[fb], q="pool")
        for cc in range(4):
            ps = fm(1536 + cc * 128, 128)
            ff = ffr.next()
            k.op("act", "copy", ff[:], ps[:, :], R=[ps], W=[ff])
            dst = g.lqT if cc < 2 else g.lkT
            k.dma(dst[(cc % 2) * 128:(cc % 2 + 1) * 128, s0:s0 + 512], ff[:], R=[ff], q="pool")
        ps = fm(2960, 128)
        gd = gdr.next()
        k.dve("tensor_copy", gd[:], ps[:, :], R=[ps], W=[gd])
        for t in range(4):
            t0 = s0 + t * 128
            ps = tm(t, 1024, 512)
            tb = tbr.next()
            k.dve("tensor_copy", tb[:], ps[:, :], R=[ps], W=[tb])
            k.dma(g.cv[t0:t0 + 128, :], tb[:], R=[tb], q="pool")
            ps = tm(t, 2048, 512)
            tb = tbr.next()
            k.op("act", "copy", tb[:], ps[:, :], R=[ps], W=[tb])
            k.dma(g.lv[t0:t0 + 128, :], tb[:], R=[tb], q="pool")
            ps = tm(t, 2560, 512)
            tf = tfr.next()
            k.act(tf[:], ps[:, :], AF.Silu, [ps], [tf])
            k.dma(g.lr[t0:t0 + 128, :], tf[:], R=[tf], q="pool")
            ps = tm(t, 1792, 256)
            tf = tfr.next()
            k.dve("tensor_copy", tf[:, 0:256], ps[:, 0:256], R=[ps], W=[tf])
            k.dma(g.lk[t0:t0 + 128, :], tf[:, 0:256], R=[tf], q="pool")
            ps = g.PS.next()
            k.mm(ps[:, 0:256], gd[:, t * 128:(t + 1) * 128], wg[:], True, True, [gd, wg], [ps])
            tf = tfr.next()
            k.dve("tensor_tensor", tf[:, 0:256], ps[:, 0:256], bgb[:], ADD, R=[ps, bgb], W=[tf])
            k.act(tf[:, 0:256], tf[:, 0:256], AF.Exp, [tf], [tf], scale=-1.0)
            k.act(tf[:, 0:256], tf[:, 0:256], AF.Ln, [tf], [tf], bias=1.0)
            k.dve("tensor_scalar_mul", tf[:, 0:256], tf[:, 0:256], -1.0 / 16.0, R=[tf], W=[tf])
            k.dma(g.la[t0:t0 + 128, :], tf[:, 0:256], R=[tf], q="pool")
    k.release(m)


def phase_diff(g):
    import math
    k, S, nc = g.k, g.S, g.nc
    m = k.mark()
    lam_init = 0.8 - 0.6 * math.exp(-0.3 * 1)
    lt = k.sb("clt", [128, 256])
    k.dma(lt[:], g.I["diff_lambda"].rearrange("a b -> (a b)").partition_broadcast(128), W=[lt])
    ls = k.sb("cls", [128, 8])
    pr = k.sb("cpr", [128, 128])
    k.dve("tensor_tensor", pr[:, 0:64], lt[:, 0:64], lt[:, 64:128], MUL, R=[lt], W=[pr])
    k.dve("tensor_tensor", pr[:, 64:128], lt[:, 128:192], lt[:, 192:256], MUL, R=[lt], W=[pr])
    k.dve("reduce_sum", ls[:, 0:1], pr[:, 0:64], AX.X, R=[pr], W=[ls])
    k.dve("reduce_sum", ls[:, 1:2], pr[:, 64:128], AX.X, R=[pr], W=[ls])
    k.act(ls[:, 0:2], ls[:, 0:2], AF.Exp, [ls], [ls])
    k.dve("tensor_tensor", ls[:, 2:3], ls[:, 1:2], ls[:, 0:1], SUB, R=[ls], W=[ls])
    k.dve("tensor_scalar_add", ls[:, 2:3], ls[:, 2:3], -lam_init, R=[ls], W=[ls])
    cs = k.sb("ccs", [128, 1])
    k.dma(cs[:], g.I["diff_norm_w"][:, :], W=[cs])
    k.dve("tensor_scalar_mul", cs[:], cs[:], 1.0 - lam_init, R=[cs], W=[cs])
    kT = k.sb("ckT", [128, S], BF16)
    V = k.sb("cV", [128, S // 128, 128], BF16)
    F = k.sb("cF", [128, 2432], BF16)
    qr = k.rot("cqT", [128, 2, 512], BF16, 2)
    for qb_ in qr.bufs:
        k.dve("memset", qb_[:], 0.0, W=[qb_])
    ptr = k.rot("cpt", [128, 512], BF16, 4)
    wr = k.rot("cw", [128, 512], F32, 4)
    sqr = k.rot("csq", [128, 512], BF16, 2)
    obr = k.rot("cob", [128, 512], BF16, 2)
    acc = g.PS.bufs[0:4]
    scb = Rot(g.PS.bufs[4:6])
    for h in range(4):
        k.dma(kT[:], g.ck[h * 128:(h + 1) * 128, :], W=[kT])
        k.dma(V[:], g.cv[:, h * 128:(h + 1) * 128].rearrange("(kb p) d -> p kb d", p=128), W=[V])
        k.dma(F[:], bass.AP(tensor=g.ftd_h, offset=h * 128 * 2560 + 127, ap=[[2559, 128], [1, 2432]]), W=[F])
        for Q in range(S // 512):
            qT = qr.next()
            for c in range(2):
                k.dma(qT[c * 64:(c + 1) * 64, c, :], g.cq[h * 128 + c * 64:h * 128 + (c + 1) * 64, Q * 512:(Q + 1) * 512], W=[qT])
            nkb = 4 * Q + 4
            items = [(kb, c) for kb in range(nkb) for c in range(2)]
            LA = 2
            pts = {}
            for i in range(len(items) + LA):
                if i < len(items):
                    kb, c = items[i]
                    dl = Q * 512 - kb * 128
                    psS = scb.next()
                    k.mm(psS[:, :], kT[:, kb * 128:(kb + 1) * 128], qT[:, c, :], True, True, [kT, qT], [psS])
                    pt = ptr.next()
                    k.act(pt[:], psS[:, :], AF.Exp, [psS], [pt], scale=0.125)
                    if dl <= 1536:
                        k.dve("tensor_tensor", pt[:], pt[:], F[:, dl + 384:dl + 384 + 512], MUL, R=[pt, F], W=[pt])
                    pts[i] = pt
                if i >= LA:
                    kb, c = items[i - LA]
                    pt = pts.pop(i - LA)
                    k.mm(acc[c][:, :], V[:, kb, :], pt[:], kb == 0, kb == nkb - 1, [V, pt], [acc[c]])
                    k.mm(acc[2 + c][:, :], g.onesb[:], pt[:], kb == 0, kb == nkb - 1, [g.onesb, pt], [acc[2 + c]])
            r0, r1, o0, o1 = wr.next(), wr.next(), wr.next(), wr.next()
            k.dve("reciprocal", r0[:], acc[2][:, :], R=[acc[2]], W=[r0])
            k.dve("reciprocal", r1[:], acc[3][:, :], R=[acc[3]], W=[r1])
            k.dve("tensor_tensor", o0[:], acc[0][:, :], r0[:], MUL, R=[acc[0], r0], W=[o0])
            k.dve("tensor_tensor", o1[:], acc[1][:, :], r1[:], MUL, R=[acc[1], r1], W=[o1])
            k.dve("scalar_tensor_tensor", o0[:], o1[:], ls[:, 2:3], o0[:], MUL, ADD, R=[o1, ls, o0], W=[o0])
            sq = sqr.next()
            k.pool("tensor_tensor", sq[:], o0[:], o0[:], MUL, R=[o0], W=[sq])
            psq = scb.next()
            k.mm(psq[:, :], g.onesb[:], sq[:], True, True, [g.onesb, sq], [psq])
            k.act(r0[:], psq[:, :], AF.Ln, [psq], [r0], scale=1.0 / 128.0, bias=RMS_EPS)
            k.act(r0[:], r0[:], AF.Exp, [r0], [r0], scale=-0.5)
            ob = obr.next()
            k.dve("scalar_tensor_tensor", ob[:], o0[:], cs[:, 0:1], r0[:], MUL, MUL, R=[o0, cs, r0], W=[ob])
            k.dma(g.oT[h * 128:(h + 1) * 128, Q * 512:(Q + 1) * 512], ob[:], R=[ob], q="pool")
    k.release(m)


def phase_gla(g):
    k, S, nc = g.k, g.S, g.nc
    m = k.mark()
    H, NP = 4, 2
    normw = k.sb("lnw", [128, 128])
    k.dma(normw[:], g.I["gla_norm_w"][0, :].partition_broadcast(128), W=[normw])
    St = [k.sb("lS%d" % p, [128, 2, 128], F32) for p in range(NP)]
    Sb = [k.sb("lSb%d" % p, [128, 128], BF16) for p in range(NP)]
    for p in range(NP):
        k.dve("memset", St[p][:], 0.0, W=[St[p]])
        k.dve("memset", Sb[p][:], 0.0, W=[Sb[p]])
    qTr = k.rot("lqT", [128, NP, 128], F32, 2)
    kTr = k.rot("lkT", [128, NP, 128], F32, 2)
    ktr = k.rot("lkt", [128, 256], F32, 2)
    lar = k.rot("lla", [128, 256], F32, 2)
    vtr = k.rot("lvt", [128, 512], BF16, 2)
    rsr = k.rot("lrs", [128, 512], F32, 2)
    oabr = k.rot("loab", [128, 512], BF16, 2)
    oTr = k.rot("loT", [128, H, 128], BF16, 2)

    def pp(name, shape, dt=F32, n=NP):
        return [k.sb("%s_%d" % (name, i), shape, dt) for i in range(n)]

    ebT = pp("lebT", [128, 128])
    enbT = pp("lenbT", [128, 128])
    kpT = pp("lkpT", [128, 128], BF16)
    enb = pp("lenb", [128, 128])
    kp = pp("lkp", [128, 128], BF16)
    ebl = pp("lebl", [128, 1])
    qz = pp("lqz", [128, 128], BF16, H)
    for h in range(H):
        k.dve("memset", qz[h][:], 0.0, W=[qz[h]])
    att = pp("latt", [128, 128], BF16, H)
    ob = pp("lob", [128, 128], F32, H)
    tmp = pp("ltmp", [128, 128], F32, H)
    st8 = pp("lst8", [128, 12], F32, H)
    lqv = g.lqT.rearrange("(pr p) s -> p pr s", p=128)
    lkv = g.lkT.rearrange("(pr p) s -> p pr s", p=128)
    oTv = g.oT[512:1024, :].rearrange("(h d) s -> d h s", d=128)
    PK = lambda p: slice(p * 128, (p + 1) * 128)
    HV = lambda h: slice(h * 128, (h + 1) * 128)
    for n in range(S // 128):
        t0 = n * 128
        qT, kT, kt, la, vt, rs = qTr.next(), kTr.next(), ktr.next(), lar.next(), vtr.next(), rsr.next()
        k.dma(qT[:], lqv[:, :, t0:t0 + 128], W=[qT])
        k.dma(kT[:], lkv[:, :, t0:t0 + 128], W=[kT])
        k.dma(kt[:], g.lk[t0:t0 + 128, :], W=[kt])
        k.dma(la[:], g.la[t0:t0 + 128, :], W=[la])
        k.dma(vt[:], g.lv[t0:t0 + 128, :], W=[vt])
        k.dma(rs[:], g.lr[t0:t0 + 128, :], W=[rs])
        oab = oabr.next()
        for p in range(NP):
            psB = g.PS.next()
            k.mm(psB[:, 0:128], la[:, PK(p)], g.ltri[:], True, True, [la, g.ltri], [psB])
            k.act(ebT[p][:], psB[:, 0:128], AF.Exp, [psB], [ebT[p]])
            k.act(enbT[p][:], psB[:, 0:128], AF.Exp, [psB], [enbT[p]], scale=-1.0)
            k.dve("tensor_copy", ebl[p][:], ebT[p][:, 127:128], R=[ebT[p]], W=[ebl[p]])
            psb = g.PS.next()
            k.mm(psb[:, 0:128], g.ltri[:], la[:, PK(p)], True, True, [la, g.ltri], [psb])
            k.act(enb[p][:], psb[:, 0:128], AF.Exp, [psb], [enb[p]], scale=-1.0)
            k.pool("tensor_tensor", kpT[p][:], kT[:, p, :], enbT[p][:], MUL, R=[kT, enbT[p]], W=[kpT[p]])
            k.pool("tensor_tensor", kp[p][:], kt[:, PK(p)], enb[p][:], MUL, R=[kt, enb[p]], W=[kp[p]])
            for j in range(2):
                h, hp = 2 * p + j, slice(j * 64, (j + 1) * 64)
                k.dve("scalar_tensor_tensor", qz[h][hp, :], qT[hp, p, :], 0.125, ebT[p][hp, :], MUL, MUL,
                      R=[qT, ebT[p]], W=[qz[h]])
        for h in range(H):
            p, j = h // 2, h % 2
            hp = slice(j * 64, (j + 1) * 64)
            psA = g.PS.next()
            k.mm(psA[:, 0:128], kpT[p][:], qz[h][:], True, True, [kpT[p], qz[h]], [psA])
            k.dve("tensor_tensor", att[h][:], psA[:, 0:128], g.ltri[:], MUL, R=[psA, g.ltri], W=[att[h]])
            psO = g.PS.next()
            k.mm(psO[:, 0:128], qz[h][:], Sb[p][:], True, False, [qz[h], Sb[p]], [psO])
            k.mm(psO[:, 0:128], att[h][:], vt[:, HV(h)], False, True, [att[h], vt], [psO])
            k.op("act", "copy", ob[h][:], psO[:, 0:128], R=[psO], W=[ob[h]])
            psS = g.PS.next()
            k.mm(psS[:, 0:128], kp[p][:], vt[:, HV(h)], True, True, [kp[p], vt], [psS])
            k.dve("tensor_tensor", St[p][hp, j, :], St[p][hp, j, :], psS[hp, 0:128], ADD, R=[St[p], psS, Sb[p]], W=[St[p]])
            k.dve("tensor_scalar_mul", St[p][hp, j, :], St[p][hp, j, :], ebl[p][hp, 0:1], R=[St[p], ebl[p]], W=[St[p]])
            k.op("act", "copy", Sb[p][hp, :], St[p][hp, j, :], R=[St[p]], W=[Sb[p]])
        for h in range(H):
            s8 = st8[h]
            k.dve("bn_stats", s8[:, 0:6], ob[h][:], R=[ob[h]], W=[s8])
            k.dve("bn_aggr", s8[:, 6:8], s8[:, 0:6], R=[s8], W=[s8])
            k.dve("scalar_tensor_tensor", s8[:, 8:9], s8[:, 6:7], s8[:, 6:7], s8[:, 7:8], MUL, ADD, R=[s8], W=[s8])
            k.act(s8[:, 9:10], s8[:, 8:9], AF.Ln, [s8], [s8], bias=RMS_EPS)
            k.act(s8[:, 9:10], s8[:, 9:10], AF.Exp, [s8], [s8], scale=-0.5)
            k.dve("scalar_tensor_tensor", tmp[h][:], ob[h][:], s8[:, 9:10], normw[:], MUL, MUL,
                  R=[ob[h], s8, normw], W=[tmp[h]])
            k.pool("tensor_tensor", oab[:, HV(h)], tmp[h][:], rs[:, HV(h)], MUL, R=[tmp[h], rs], W=[oab])
        pbk = g.PB.next()
        for h in range(H):
            k.transpose(pbk[:, HV(h)], oab[:, HV(h)], g.identb[:], [oab, g.identb], [pbk], inc=(h == H - 1))
        oT = oTr.next()
        k.dve("tensor_copy", oT[:], pbk[:, 0:512].rearrange("p (h t) -> p h t", h=H), R=[pbk], W=[oT])
        k.dma(oTv[:, :, t0:t0 + 128], oT[:], R=[oT], q="pool")
    k.release(m)


def _rel_bucket(dist):
    import math
    dist = np.asarray(dist)
    d = np.maximum(dist, 1).astype(np.float32)
    large = 16 + (np.log(d / np.float32(16)) / np.float32(math.log(128.0)) * np.float32(16)).astype(np.int32)
    large = np.minimum(large, 31)
    return np.where(dist < 16, dist, large)


def _consts():
    idx = np.arange(128)
    c = {}
    c["c_ident"] = np.eye(128, dtype=np.float32)
    c["c_ltri"] = (idx[:, None] <= idx[None, :]).astype(np.float32)
    c["c_lstrict"] = (idx[:, None] < idx[None, :]).astype(np.float32)
    c["c_neg"] = np.where(idx[:, None] > idx[None, :], -30000.0, 0.0).astype(np.float32)
    oh = np.zeros((33, 2560), np.float32)
    d = np.arange(2560) - 511
    bk = _rel_bucket(np.maximum(d, 0))
    oh[bk, np.arange(2560)] = 1.0
    oh[31, :] -= 1.0
    oh[:32, d < 0] = 0.0
    oh[32, d < 0] = -30000.0
    c["c_ohdiff"] = oh
    ohl = np.zeros((3, 33, 384), np.float32)
    for gi, dil in enumerate((1, 4, 16)):
        rel = np.arange(384) - 127
        ok = (rel >= 0) & (rel <= 128)
        bk = _rel_bucket(np.maximum(rel, 0) * dil)
        ohl[gi, bk, np.arange(384)] = 1.0
        ohl[gi, :32, ~ok] = 0.0
        ohl[gi, 32, ~ok] = -30000.0
    c["c_ohdil"] = ohl
    return c


_CACHE = {}


def kernel(**inputs):
    S = 8192
    B = 4
    if "nc" not in _CACHE:
        _CACHE["nc"] = build(S)[0]
    nc = _CACHE["nc"]
    consts = _consts()
    f = lambda a: np.ascontiguousarray(np.asarray(a, dtype=np.float32))
    shared = {
        "rel_bias": f(inputs["rel_bias"]),
        "w_in_even": f(inputs["w_in_even"][0]), "gdn_conv_w": f(inputs["gdn_conv_w"][0]),
        "gdn_a_log": f(inputs["gdn_a_log"]).reshape(1, 6), "gdn_dt_bias": f(inputs["gdn_dt_bias"]).reshape(1, 6),
        "gdn_norm_w": f(inputs["gdn_norm_w"]).reshape(1, 128), "w_out_even": f(inputs["w_out_even"][0]),
        "w_in_odd": f(inputs["w_in_odd"][0]), "diff_lambda": f(inputs["diff_lambda"][0]),
        "diff_norm_w": f(inputs["diff_norm_w"]).reshape(128, 1), "gla_w_gate": f(inputs["gla_w_gate"][0]),
        "gla_b_gate": f(inputs["gla_b_gate"]).reshape(1, 256), "gla_norm_w": f(inputs["gla_norm_w"]).reshape(1, 128),
        "w_out_odd": f(inputs["w_out_odd"][0]), "ffn_w_up": f(inputs["ffn_w_up"]),
        "ffn_conv_w": f(inputs["ffn_conv_w"]), "ffn_conv_b": f(inputs["ffn_conv_b"]).reshape(2, 1, 2 * DFF),
        "ffn_w_down": f(inputs["ffn_w_down"]), "ln_g": f(inputs["ln_g"]).reshape(4, D),
        "ln_b": f(inputs["ln_b"]).reshape(4, D),
    }
    shared.update(consts)
    x = f(inputs["x"])
    in_maps = []
    for c in range(8):
        mp = dict(shared)
        mp["x"] = x[c % B]
        in_maps.append(mp)
    res = run_bass_kernel_spmd(nc, in_maps, core_ids=list(range(8)))
    return np.stack([res.results[b]["y"] for b in range(B)], 0).astype(np.float32)
```
